# Optimizing a Trainium2 kernel written in Bass

```python
import jax, jax.numpy as jnp
from jax import lax
import numpy as np

D_MODEL = 1024
BATCH = 4
SEQ = 8192
DEPTH = 2

LRU_WIDTH = D_MODEL
LRU_BLOCKS = 8
LRU_BLOCK_W = LRU_WIDTH // LRU_BLOCKS
CONV_W = 4
LRU_C = 8.0
MLA_HEADS = 8
QK_NOPE = 128
QK_ROPE = 64
QK_HEAD = QK_NOPE + QK_ROPE
V_HEAD = D_MODEL // MLA_HEADS
Q_RANK = 256
KV_RANK = 128
ROPE_THETA = 10000.0
Q_BLOCK = 128
D_FF = -(-8 * D_MODEL // (3 * 256)) * 256
EPS = 1e-6
D_IN = LRU_WIDTH + Q_RANK + KV_RANK + QK_ROPE + 2 * D_MODEL
IN_SPLIT_POINTS = (LRU_WIDTH,
                   LRU_WIDTH + Q_RANK,
                   LRU_WIDTH + Q_RANK + KV_RANK,
                   LRU_WIDTH + Q_RANK + KV_RANK + QK_ROPE,
                   LRU_WIDTH + Q_RANK + KV_RANK + QK_ROPE + D_MODEL)

kernel_name = 'hybrid_rglru_mla_adaln_block'


def rms_norm(x, gain=None):
    xf = x.astype(jnp.float32)
    y = xf * lax.rsqrt(jnp.mean(xf * xf, axis=-1, keepdims=True) + EPS)
    if gain is not None:
        y = y * gain.astype(jnp.float32)
    return y.astype(x.dtype)


def rope_tables(positions):
    inv_freq = ROPE_THETA ** (-jnp.arange(0, QK_ROPE, 2, dtype=jnp.float32) / QK_ROPE)
    ang = positions.astype(jnp.float32)[..., None] * inv_freq
    return jnp.cos(ang), jnp.sin(ang)


def apply_rope(x, cos, sin):
    half = QK_ROPE // 2
    xf = x.astype(jnp.float32)
    x1, x2 = xf[..., :half], xf[..., half:]
    return jnp.concatenate([x1 * cos - x2 * sin, x2 * cos + x1 * sin], axis=-1).astype(x.dtype)


def causal_depthwise_conv(x, w, b):
    y = lax.conv_general_dilated(x, w[:, None, :].astype(x.dtype), window_strides=(1,),
                                 padding=((CONV_W - 1, 0),),
                                 dimension_numbers=('NWC', 'WIO', 'NWC'),
                                 feature_group_count=x.shape[-1])
    return y + b


def rg_lru(x, positions, w_a, b_a, w_x, b_x, a_param):
    B, S, _ = x.shape
    xf = x.astype(jnp.float32)
    xb = xf.reshape(B, S, LRU_BLOCKS, LRU_BLOCK_W)
    r = jax.nn.sigmoid(jnp.einsum('bsni,nij->bsnj', xb, w_a.astype(jnp.float32))
                       + b_a.astype(jnp.float32)).reshape(B, S, LRU_WIDTH)
    i = jax.nn.sigmoid(jnp.einsum('bsni,nij->bsnj', xb, w_x.astype(jnp.float32))
                       + b_x.astype(jnp.float32)).reshape(B, S, LRU_WIDTH)
    log_a = -LRU_C * r * jax.nn.softplus(-a_param.astype(jnp.float32))
    reset = (positions == 0)[..., None]
    a = jnp.where(reset, 0.0, jnp.exp(log_a))
    mult = jnp.where(reset, 1.0, jnp.sqrt(-jnp.expm1(2.0 * log_a)))
    b_in = xf * i * mult

    def combine(left, right):
        return (left[0] * right[0], right[0] * left[1] + right[1])

    _, h = lax.associative_scan(combine, (a, b_in), axis=1)
    return h.astype(x.dtype)


def causal_block_attention(q, k, v):
    B, S, H, Dk = q.shape
    nb = S // Q_BLOCK
    scale = QK_HEAD ** -0.5
    qb = q.reshape(B, nb, Q_BLOCK, H, Dk).transpose(1, 0, 2, 3, 4)
    k_idx = jnp.arange(S)

    def one_block(args):
        q_blk, blk = args
        s = jnp.einsum('bqhd,bkhd->bhqk', q_blk, k, preferred_element_type=jnp.float32) * scale
        q_idx = blk * Q_BLOCK + jnp.arange(Q_BLOCK)
        s = jnp.where(k_idx[None, :] <= q_idx[:, None], s, -jnp.inf)
        p = jax.nn.softmax(s, axis=-1)
        o = jnp.einsum('bhqk,bkhd->bqhd', p.astype(v.dtype), v, preferred_element_type=jnp.float32)
        return o.astype(v.dtype)

    o = lax.map(one_block, (qb, jnp.arange(nb)))
    return o.transpose(1, 0, 2, 3, 4).reshape(B, S, H, v.shape[-1])


def mla(q_down, kv_down, k_rope, cos, sin, q_norm_g, kv_norm_g, w_uq, w_ukv):
    B, S, _ = q_down.shape
    c_q = rms_norm(q_down, q_norm_g)
    q = jnp.einsum('bsr,rhd->bshd', c_q, w_uq)
    q = jnp.concatenate([q[..., :QK_NOPE],
                         apply_rope(q[..., QK_NOPE:], cos[:, :, None, :], sin[:, :, None, :])], axis=-1)
    c_kv = rms_norm(kv_down, kv_norm_g)
    kv = jnp.einsum('bsr,rhd->bshd', c_kv, w_ukv)
    k_pe = apply_rope(k_rope, cos, sin)
    k = jnp.concatenate([kv[..., :QK_NOPE],
                         jnp.broadcast_to(k_pe[:, :, None, :], (B, S, MLA_HEADS, QK_ROPE))], axis=-1)
    v = kv[..., QK_NOPE:]
    o = causal_block_attention(q, k, v)
    return o.reshape(B, S, MLA_HEADS * V_HEAD)


def hybrid_layer(x, c, positions, cos, sin, w_ada, b_ada, w_in, conv_w, conv_b,
                 lru_wa, lru_ba, lru_wx, lru_bx, lru_a_param, q_norm_g, kv_norm_g,
                 w_uq, w_ukv, w_out, w_ffn_in, w_ffn_out):
    mod = (c @ w_ada + b_ada)[:, None, :]
    sh_m, sc_m, g_m, sh_f, sc_f, g_f = jnp.split(mod, 6, axis=-1)
    h = rms_norm(x) * (1 + sc_m) + sh_m
    proj = h @ w_in
    x_lru, q_down, kv_down, k_rope, gate_a, gate_b = jnp.split(proj, IN_SPLIT_POINTS, axis=-1)
    y_a = rg_lru(causal_depthwise_conv(x_lru, conv_w, conv_b), positions,
                 lru_wa, lru_ba, lru_wx, lru_bx, lru_a_param)
    y_b = mla(q_down, kv_down, k_rope, cos, sin, q_norm_g, kv_norm_g, w_uq, w_ukv)
    y = jax.nn.sigmoid(gate_a) * y_a + jax.nn.sigmoid(gate_b) * y_b
    x = x + g_m * (y @ w_out)
    h = rms_norm(x) * (1 + sc_f) + sh_f
    gate, up = jnp.split(h @ w_ffn_in, 2, axis=-1)
    x = x + g_f * ((jax.nn.silu(gate) * up) @ w_ffn_out)
    return x


def setup_inputs(seed: int = 0) -> dict:
    key = jax.random.key(seed)
    ks = jax.random.split(key, 24)
    f32 = jnp.float32
    x = jax.random.normal(ks[0], (BATCH, SEQ, D_MODEL), f32)
    c = jax.random.normal(ks[1], (BATCH, D_MODEL), f32)
    positions = jnp.broadcast_to(jnp.arange(SEQ, dtype=jnp.int32)[None, :], (BATCH, SEQ))
    w_ada = jax.random.normal(ks[2], (DEPTH, D_MODEL, 6 * D_MODEL), f32) * (0.3 * D_MODEL ** -0.5)
    b_ada = jax.random.normal(ks[3], (DEPTH, 6 * D_MODEL), f32) * 0.02
    w_in = jax.random.normal(ks[4], (DEPTH, D_MODEL, D_IN), f32) * D_MODEL ** -0.5
    conv_w = jax.random.normal(ks[5], (DEPTH, CONV_W, LRU_WIDTH), f32) * CONV_W ** -0.5
    conv_b = jax.random.normal(ks[6], (DEPTH, LRU_WIDTH), f32) * 0.01
    lru_wa = jax.random.normal(ks[7], (DEPTH, LRU_BLOCKS, LRU_BLOCK_W, LRU_BLOCK_W), f32) * LRU_BLOCK_W ** -0.5
    lru_ba = jax.random.normal(ks[8], (DEPTH, LRU_BLOCKS, LRU_BLOCK_W), f32) * 0.01
    lru_wx = jax.random.normal(ks[9], (DEPTH, LRU_BLOCKS, LRU_BLOCK_W, LRU_BLOCK_W), f32) * LRU_BLOCK_W ** -0.5
    lru_bx = jax.random.normal(ks[10], (DEPTH, LRU_BLOCKS, LRU_BLOCK_W), f32) * 0.01
    rad = jax.random.uniform(ks[11], (DEPTH, LRU_WIDTH), f32, minval=0.9, maxval=0.999)
    a0 = rad ** (1.0 / LRU_C)
    lru_a_param = jnp.log(a0) - jnp.log1p(-a0)
    q_norm_g = 1.0 + 0.01 * jax.random.normal(ks[12], (DEPTH, Q_RANK), f32)
    kv_norm_g = 1.0 + 0.01 * jax.random.normal(ks[13], (DEPTH, KV_RANK), f32)
    w_uq = jax.random.normal(ks[14], (DEPTH, Q_RANK, MLA_HEADS, QK_HEAD), f32) * Q_RANK ** -0.5
    w_ukv = jax.random.normal(ks[15], (DEPTH, KV_RANK, MLA_HEADS, QK_NOPE + V_HEAD), f32) * KV_RANK ** -0.5
    w_out = jax.random.normal(ks[16], (DEPTH, D_MODEL, D_MODEL), f32) * D_MODEL ** -0.5
    w_ffn_in = jax.random.normal(ks[17], (DEPTH, D_MODEL, 2 * D_FF), f32) * D_MODEL ** -0.5
    w_ffn_out = jax.random.normal(ks[18], (DEPTH, D_FF, D_MODEL), f32) * D_FF ** -0.5
    final_norm_g = 1.0 + 0.01 * jax.random.normal(ks[19], (D_MODEL,), f32)
    return {'x': x, 'c': c, 'positions': positions, 'w_ada': w_ada, 'b_ada': b_ada,
            'w_in': w_in, 'conv_w': conv_w, 'conv_b': conv_b, 'lru_wa': lru_wa,
            'lru_ba': lru_ba, 'lru_wx': lru_wx, 'lru_bx': lru_bx, 'lru_a_param': lru_a_param,
            'q_norm_g': q_norm_g, 'kv_norm_g': kv_norm_g, 'w_uq': w_uq, 'w_ukv': w_ukv,
            'w_out': w_out, 'w_ffn_in': w_ffn_in, 'w_ffn_out': w_ffn_out,
            'final_norm_g': final_norm_g}


def reference(x, c, positions, w_ada, b_ada, w_in, conv_w, conv_b, lru_wa, lru_ba, lru_wx,
              lru_bx, lru_a_param, q_norm_g, kv_norm_g, w_uq, w_ukv, w_out, w_ffn_in,
              w_ffn_out, final_norm_g):
    cos, sin = rope_tables(positions)
    for l in range(DEPTH):
        x = hybrid_layer(x, c, positions, cos, sin, w_ada[l], b_ada[l], w_in[l], conv_w[l],
                         conv_b[l], lru_wa[l], lru_ba[l], lru_wx[l], lru_bx[l], lru_a_param[l],
                         q_norm_g[l], kv_norm_g[l], w_uq[l], w_ukv[l], w_out[l],
                         w_ffn_in[l], w_ffn_out[l])
    return rms_norm(x, final_norm_g)
```

```python
import numpy as np
from contextlib import ExitStack
import concourse.bass as bass
import concourse.mybir as mybir
from concourse.bass_utils import run_bass_kernel_spmd

F32 = mybir.dt.float32
BF16 = mybir.dt.bfloat16
I32 = mybir.dt.int32
AF = mybir.ActivationFunctionType
ALU = mybir.AluOpType

D = 1024
NCH = 8
SEQ = 8192
BATCH = 4
TB = 512
NB = 8
NT = NB * TB
HEADS = 8
DFF = 2816
NFF = 22
DIN = 3520
DAUG = 3584
EPS = 1e-6
QSCALE = 192 ** -0.5
GROUPS = [[0, 1], [2, 3], [4, 5], [6, 7]]
PPL = 116
PP_FG = 232
PP_FLAG = 240
PP_INVF = 242
PP_C = 244
NPP = 252

MODE = "fused"


class H:
    __slots__ = ("name", "w", "r", "sem", "cnt")

    def __init__(self, name):
        self.name = name
        self.w = []
        self.r = []
        self.sem = None
        self.cnt = 0


class _Eng:
    def __init__(self, name, sem):
        self.name = name
        self.sem = sem
        self.cnt = 0
        self.known = {}
        self.prog = []


class Sched:
    ENGS = ("pe", "act", "dve", "pool", "sp")

    def __init__(self, nc, stack, n_dma_sems=80):
        self.nc = nc
        self.E = {}
        for n in self.ENGS:
            sem = stack.enter_context(nc.semaphore("s_" + n))
            self.E[n] = _Eng(n, sem)
        self.free_sems = []
        for i in range(n_dma_sems):
            self.free_sems.append([stack.enter_context(nc.semaphore("d%d" % i)), 0])
        self.live = []
        self.store_tickets = {}
        self.n_ops = 0
        self.n_waits = 0

    def _deps(self, E, reads, writes, awrites=()):
        deps = []
        for h in reads:
            deps.extend(h.w)
        for h in writes:
            deps.extend(h.w)
            for t in h.r:
                if t[0] is E.sem:
                    continue
                deps.append(t)
        for h in awrites:
            for t in h.r:
                deps.append(t)
        return deps

    def _emit_waits(self, E, deps):
        best = {}
        for (sem, val) in deps:
            k = id(sem)
            if sem is E.sem and (E.name == "pe" or val > E.cnt):
                continue
            if E.known.get(k, 0) >= val:
                continue
            if k not in best or best[k][1] < val:
                best[k] = (sem, val)
        for k, (sem, val) in best.items():
            E.known[k] = val
            E.prog.append(("wait", sem, val))
            self.n_waits += 1

    def _update(self, ticket, reads, writes, awrites=()):
        for h in reads:
            h.r = [t for t in h.r if t[0] is not ticket[0]]
            h.r.append(ticket)
        for h in writes:
            h.w = [ticket]
            h.r = []
        for h in awrites:
            h.w = [t for t in h.w if t[0] is not ticket[0]]
            h.w.append(ticket)
            h.r = []

    def op(self, ename, fn, reads=(), writes=(), inc=True):
        E = self.E[ename]
        self._emit_waits(E, self._deps(E, reads, writes))
        if inc:
            E.cnt += 1
            ticket = (E.sem, E.cnt)
        else:
            ticket = (E.sem, E.cnt + 1)
        E.prog.append(("op", fn, inc))
        self._update(ticket, reads, writes)
        self.n_ops += 1
        return ticket

    def _hsem(self, h):
        if h.sem is None:
            if not self.free_sems:
                raise RuntimeError("out of DMA semaphores")
            ent = self.free_sems.pop()
            h.sem = ent[0]
            h.cnt = ent[1]
            self.live.append(h)
        return h.sem

    def dma(self, q, out, in_, reads=(), writes=(), awrites=(), owner=None, is_store=False, **kw):
        E = self.E[q]
        self._emit_waits(E, self._deps(E, reads, writes, awrites))
        sem = self._hsem(owner)
        owner.cnt += 16
        ticket = (sem, owner.cnt)
        E.prog.append(("dma", out, in_, sem, kw))
        self._update(ticket, reads, writes, awrites)
        if is_store:
            self.store_tickets[id(sem)] = ticket
        self.n_ops += 1
        return ticket

    def collective(self, kind, ins, outs, reads, writes, owner):
        E = self.E["pool"]
        self._emit_waits(E, self._deps(E, reads, writes))
        sem = self._hsem(owner)
        owner.cnt += 1
        ticket = (sem, owner.cnt)
        E.prog.append(("cc", kind, ins, outs, sem))
        self._update(ticket, reads, writes)
        return ticket

    def barrier(self, scratch_ap):
        P = self.E["pool"]
        deps = []
        for n in self.ENGS:
            E = self.E[n]
            if E.cnt > 0:
                deps.append((E.sem, E.cnt))
        for h in self.live:
            deps.append((h.sem, h.cnt))
        self._emit_waits(P, deps)
        P.cnt += 1
        P.prog.append(("op", lambda e: e.memset(scratch_ap, 0.0), True))
        t = (P.sem, P.cnt)
        for n in self.ENGS:
            if n != "pool":
                self._emit_waits(self.E[n], [t])
        for h in self.live:
            self.free_sems.append([h.sem, h.cnt])
            h.sem = None
        self.live = []

    def final_wait(self):
        self._emit_waits(self.E["pool"], list(self.store_tickets.values()))

    def emit(self):
        nc = self.nc
        S = self

        def run(eng_obj, E):
            for item in E.prog:
                k = item[0]
                if k == "wait":
                    eng_obj.wait_ge(item[1], item[2])
                elif k == "op":
                    ins = item[1](eng_obj)
                    if item[2]:
                        ins.then_inc(E.sem, 1)
                elif k == "dma":
                    eng_obj.dma_start(out=item[1], in_=item[2], **item[4]).then_inc(item[3], 16)
                elif k == "cc":
                    eng_obj.collective_compute(item[1], ALU.bypass, replica_groups=GROUPS,
                                               ins=item[2], outs=item[3]).then_inc(item[4], 1)

        with nc.Block() as block:
            @block.tensor
            def _(e):
                run(e, S.E["pe"])

            @block.scalar
            def _(e):
                run(e, S.E["act"])

            @block.vector
            def _(e):
                run(e, S.E["dve"])

            @block.gpsimd
            def _(e):
                run(e, S.E["pool"])

            @block.sync
            def _(e):
                run(e, S.E["sp"])


class Tile:
    __slots__ = ("t", "h")

    def __init__(self, t, name):
        self.t = t
        self.h = H(name)


DT_SIZE = {F32: 4, BF16: 2, I32: 4}


class Builder:
    def __init__(self, phases, fused):
        self.phases = phases
        self.fused = fused
        self.nc = bass.Bass("TRN2", target_bir_lowering=False)
        self.ext_in = {}
        self.ext_out = {}
        self.dram = {}
        self.DH = {}
        self.uid = 0

    def D(self, name, shape, dtype, writer):
        if name in self.dram:
            return self.dram[name]
        if writer == "host" or writer not in self.phases:
            t = self.nc.dram_tensor(name, list(shape), dtype, kind="ExternalInput")
            self.ext_in[name] = (tuple(shape), dtype)
        elif self.fused and name != "out":
            t = self.nc.dram_tensor(name, list(shape), dtype)
        else:
            t = self.nc.dram_tensor(name, list(shape), dtype, kind="ExternalOutput")
            self.ext_out[name] = (tuple(shape), dtype)
        self.dram[name] = t
        return t

    def dh(self, name, key=None):
        k = (name, key)
        if k not in self.DH:
            self.DH[k] = H("D_%s_%s" % (name, key))
        return self.DH[k]

    def sb_reset(self):
        self.sb_ptr = self.sb_base

    def sb(self, name, shape, dtype):
        per = 1
        for s in shape[1:]:
            per *= s
        nbytes = (per * DT_SIZE[dtype] + 63) // 64 * 64
        if self.sb_ptr + nbytes > self.sb_top:
            raise RuntimeError("SBUF overflow allocating %s (%d + %d > %d)" % (name, self.sb_ptr, nbytes, self.sb_top))
        self.uid += 1
        t = self.nc.alloc_sbuf_tensor_at("%s_%d" % (name, self.uid), list(shape), dtype, offset=self.sb_ptr)
        self.sb_ptr += nbytes
        return Tile(t, name)

    def ring(self, name, n, shape, dtype):
        return _Ring([self.sb("%s%d" % (name, i), shape, dtype) for i in range(n)])

    def build(self):
        nc = self.nc
        with ExitStack() as st:
            self.S = S = Sched(nc, st)
            self.sb_base = (nc.sbuf_base + 63) // 64 * 64
            self.sb_top = nc.sbuf_top
            self.sb_reset()
            self.ps = st.enter_context(nc.psum_tensor("ps", [128, 8, 512], F32))
            self.psh = [H("ps%d" % i) for i in range(8)]
            self.ps_rr = 0
            self.bar = self.sb("bar", [128, 8], F32)
            self.ones = self.sb("ones", [128, 128], BF16)
            self.zeros = self.sb("zeros", [128, 512], F32)
            self.pp = self.sb("pp", [128, NPP], F32)
            self.persist_ptr = None
            S.op("pool", lambda e: e.memset(self.ones.t[:], 1.0), writes=[self.ones.h])
            S.op("pool", lambda e: e.memset(self.zeros.t[:], 0.0), writes=[self.zeros.h])
            ppd = self.D("pp", [128, NPP], F32, "host")
            S.dma("sp", self.pp.t[:], ppd[:, :], writes=[self.pp.h], owner=self.pp.h)
            self.persist_ptr = self.sb_ptr
            for ph in self.phases:
                kind = ph[0]
                if not (self.fused and kind in ("X1", "A")):
                    self.sb_ptr = self.persist_ptr
                l = ph[1] if len(ph) > 1 else None
                getattr(self, "phase_" + kind)(*([l] if l is not None else []))
                S.barrier(self.bar.t[:, 0:1])
            S.final_wait()
            print("ops", S.n_ops, "waits", S.n_waits, {n: len(S.E[n].prog) for n in S.ENGS}, flush=True)
            S.emit()
        return nc

    def ps_next(self, banks=(0, 1, 2, 3, 4, 5, 6, 7)):
        b = banks[self.ps_rr % len(banks)]
        self.ps_rr += 1
        return b

    def mm_group(self, out_ap, bank, pairs, reads, **kw):
        S = self.S
        n = len(pairs)
        for i, (l, r) in enumerate(pairs):
            S.op("pe", lambda e, l=l, r=r, i=i: e.matmul(out_ap, l, r, start=(i == 0), stop=(i == n - 1), **kw),
                 reads=reads, writes=[self.psh[bank]], inc=(i == n - 1))

    def ppc(self, col, n=1, parts=128):
        return self.pp.t[0:parts, col:col + n]

    def load_modv(self, l):
        modd = self.D("modv", [128, 96], F32, ("M",))
        mv = self.sb("modv", [128, 96], F32)
        self.S.dma("sp", mv.t[:], modd[:, :], reads=[self.dh("modv")], writes=[mv.h], owner=mv.h)
        return mv

    def norm_block(self, xt, W, sc1_ap, sh_ap, hout, rs, sq_ring, tmp_ring, mvh=None):
        S = self.S
        bank = self.ps_next()
        sqs = []
        for c in range(NCH):
            sq = sq_ring.next()
            S.op("act", lambda e, sq=sq, c=c: e.activation(sq.t[:, 0:W], xt.t[:, c, 0:W], AF.Square),
                 reads=[xt.h], writes=[sq.h])
            S.op("pe", lambda e, sq=sq, c=c: e.matmul(self.ps[:, bank, 0:W], self.ones.t[:, :], sq.t[:, 0:W],
                                                       start=(c == 0), stop=(c == NCH - 1)),
                 reads=[sq.h, self.ones.h], writes=[self.psh[bank]], inc=True)
        S.op("act", lambda e: e.activation(rs.t[:, 0:W], self.ps[:, bank, 0:W], AF.Sqrt, bias=EPS, scale=1.0 / D),
             reads=[self.psh[bank]], writes=[rs.h])
        S.op("dve", lambda e: e.reciprocal(rs.t[:, 0:W], rs.t[:, 0:W]), reads=[rs.h], writes=[rs.h])
        for c in range(NCH):
            tmp = tmp_ring.next()
            S.op("dve", lambda e, tmp=tmp, c=c: e.tensor_tensor(tmp.t[:, 0:W], xt.t[:, c, 0:W], rs.t[:, 0:W], ALU.mult),
                 reads=[xt.h, rs.h], writes=[tmp.h])
            S.op("pool", lambda e, tmp=tmp, c=c: e.tensor_scalar(hout.t[:, c, 0:W], tmp.t[:, 0:W],
                                                                 sc1_ap[:, c:c + 1], sh_ap[:, c:c + 1], ALU.mult, ALU.add),
                 reads=[tmp.h, mvh], writes=[hout.h])

    def load_w(self, dst_ap, src_ap, tile):
        self.S.dma("pool", dst_ap, src_ap, writes=[tile.h], owner=tile.h)

    def phase_M(self):
        S = self.S
        w_ada = self.D("w_ada", [2, D, 6 * D], F32, "host")
        modd = self.D("modv", [128, 96], F32, ("M",))
        wr = self.ring("wada", 2, [128, 8, 1536], BF16)
        cbf = self.sb("cbf", [128, 8], BF16)
        mv = self.sb("mv", [128, 96], F32)
        S.op("dve", lambda e: e.tensor_copy(cbf.t[:], self.ppc(PP_C, 8)), reads=[self.pp.h], writes=[cbf.h])
        bank = 0
        for l in range(2):
            for g in range(4):
                wt = wr.next()
                src = w_ada[l].rearrange("(k p) n -> p k n", p=128)
                for k in range(8):
                    self.S.dma("pool", wt.t[:, k, :], src[:, k, g * 1536:(g + 1) * 1536], writes=[wt.h], owner=wt.h)
                for j in range(12):
                    J = g * 12 + j
                    self.mm_group(self.ps[:, bank, J:J + 1], bank,
                                  [(wt.t[:, k, j * 128:(j + 1) * 128], cbf.t[:, k:k + 1]) for k in range(8)],
                                  reads=[wt.h, cbf.h])
            S.op("dve", lambda e, l=l: e.tensor_tensor(mv.t[:, l * 48:(l + 1) * 48], self.ps[:, bank, 0:48],
                                                       self.ppc(l * PPL, 48), ALU.add),
                 reads=[self.psh[bank], self.pp.h], writes=[mv.h])
            for off in (8, 32):
                S.op("dve", lambda e, l=l, off=off: e.tensor_scalar(mv.t[:, l * 48 + off:l * 48 + off + 8],
                                                                    mv.t[:, l * 48 + off:l * 48 + off + 8],
                                                                    1.0, None, ALU.add),
                     reads=[mv.h], writes=[mv.h])
        S.dma("sp", modd[:, :], mv.t[:], reads=[mv.h], writes=[self.dh("modv")], owner=mv.h, is_store=True)

    def phase_R(self):
        S = self.S
        posd = self.D("pos", [1, NT], I32, "host")
        ropd = self.D("rope", [4, 64, NT], F32, ("R",))
        posi = self.ring("posi", 2, [64, TB], I32)
        posf = self.ring("posf", 2, [64, TB], F32)
        ang = self.ring("ang", 2, [64, TB], F32)
        tq = self.ring("tq", 2, [64, TB], F32)
        ti = self.ring("ti", 2, [64, TB], I32)
        out = self.ring("rout", 4, [64, 2, TB], F32)
        invf = self.ppc(PP_INVF, 1, 64)
        invf2 = self.ppc(PP_INVF + 1, 1, 64)
        for i in range(NB):
            pi = posi.next()
            pf = posf.next()
            S.dma("sp", pi.t[:], posd[0:1, i * TB:(i + 1) * TB].partition_broadcast(64), writes=[pi.h], owner=pi.h)
            S.op("dve", lambda e, pi=pi, pf=pf: e.tensor_copy(pf.t[:], pi.t[:]), reads=[pi.h], writes=[pf.h])
            for which, (aoff, toff) in enumerate(((0.0, 0.0), (np.pi / 2, 0.25))):
                a = ang.next()
                t = tq.next()
                tii = ti.next()
                o = out.next()
                S.op("dve", lambda e, a=a, pf=pf, aoff=aoff: e.tensor_scalar(a.t[:], pf.t[:], invf, aoff, ALU.mult, ALU.add),
                     reads=[pf.h, self.pp.h], writes=[a.h])
                S.op("dve", lambda e, t=t, pf=pf, toff=toff: e.tensor_scalar(t.t[:], pf.t[:], invf2, toff, ALU.mult, ALU.add),
                     reads=[pf.h, self.pp.h], writes=[t.h])
                S.op("dve", lambda e, t=t, tii=tii: e.tensor_copy(tii.t[:], t.t[:]), reads=[t.h], writes=[tii.h])
                S.op("dve", lambda e, t=t, tii=tii: e.tensor_copy(t.t[:], tii.t[:]), reads=[tii.h], writes=[t.h])
                S.op("dve", lambda e, t=t, a=a: e.scalar_tensor_tensor(a.t[:], t.t[:], -2 * np.pi, a.t[:], ALU.mult, ALU.add),
                     reads=[t.h, a.h], writes=[a.h])
                S.op("act", lambda e, o=o, a=a: e.activation(o.t[:, 0, :], a.t[:], AF.Sin), reads=[a.h], writes=[o.h])
                S.op("dve", lambda e, o=o: e.tensor_scalar(o.t[:, 1, :], o.t[:, 0, :], QSCALE, None, ALU.mult),
                     reads=[o.h], writes=[o.h])
                tidx = 1 if which == 0 else 0
                S.dma("sp", ropd[tidx, :, i * TB:(i + 1) * TB], o.t[:, 0, :], reads=[o.h],
                      awrites=[self.dh("rope")], owner=o.h, is_store=True)
                S.dma("sp", ropd[tidx + 2, :, i * TB:(i + 1) * TB], o.t[:, 1, :], reads=[o.h],
                      awrites=[self.dh("rope")], owner=o.h, is_store=True)

    def get_w_in(self, l, lru_only):
        key = ("w_in", l)
        if getattr(self, "_w_in_key", None) == key:
            return self._w_in
        w_in = self.D("w_in", [2, D, DIN], F32, "host")
        ncols = D if lru_only else DAUG
        wt = self.sb("w_in_sb", [128, 8, ncols], BF16)
        src = w_in[l].rearrange("(k p) n -> p k n", p=128)
        for k in range(8):
            if lru_only:
                self.load_w(wt.t[:, k, 0:D], src[:, k, 0:D], wt)
            else:
                self.load_w(wt.t[:, k, 0:1472], src[:, k, 0:1472], wt)
                self.load_w(wt.t[:, k, 1472:1504], src[:, k, 1440:1472], wt)
                self.load_w(wt.t[:, k, 1504:1536], src[:, k, 1408:1440], wt)
                self.load_w(wt.t[:, k, 1536:DAUG], src[:, k, 1472:DIN], wt)
        if not lru_only:
            self.S.op("dve", lambda e: e.tensor_scalar(wt.t[:, :, 1472:1504], wt.t[:, :, 1472:1504], -1.0, None, ALU.mult),
                      reads=[wt.h], writes=[wt.h])
        self._w_in_key = key
        self._w_in = wt
        return wt

    def phase_P(self, l):
        S = self.S
        xname, xw = ("xT", "host") if l == 0 else ("x2_0", ("C2", 0))
        xd = self.D(xname, [NCH, 128, NT], F32, xw)
        shd = self.D("send_halo_%d" % l, [128, 192], F32, ("P", l))
        wt = self.get_w_in(l, lru_only=not self.fused)
        mv = self.load_modv(l)
        xh = self.sb("xh", [128, 8, 24], F32)
        hh = self.sb("hh", [128, 8, 24], BF16)
        rs = self.sb("rsP", [128, 24], F32)
        sq_ring = self.ring("sqP", 2, [128, 24], BF16)
        tmp_ring = self.ring("tmpP", 2, [128, 24], F32)
        xlh = self.sb("xlh", [128, 8, 24], F32)
        for c in range(NCH):
            src = xd[c].rearrange("p (i t) -> p i t", t=TB)[:, :, TB - 3:TB]
            rd = [self.dh(xname, (i,)) for i in range(NB)]
            S.dma("sp", xh.t[:, c, :].rearrange("p (i t) -> p i t", t=3), src, reads=rd, writes=[xh.h], owner=xh.h)
        self.norm_block(xh, 24, mv.t[:, l * 48 + 8:l * 48 + 16], mv.t[:, l * 48 + 0:l * 48 + 8], hh, rs, sq_ring, tmp_ring, mvh=mv.h)
        for m in range(NCH):
            bank = self.ps_next()
            self.mm_group(self.ps[:, bank, 0:24], bank,
                          [(wt.t[:, k, m * 128:(m + 1) * 128], hh.t[:, k, :]) for k in range(8)],
                          reads=[wt.h, hh.h])
            S.op("act", lambda e, m=m, bank=bank: e.activation(xlh.t[:, m, :], self.ps[:, bank, 0:24], AF.Copy),
                 reads=[self.psh[bank]], writes=[xlh.h])
        S.dma("sp", shd[:, :], xlh.t[:].rearrange("p m t -> p (m t)"), reads=[xlh.h], writes=[self.dh("send_halo_%d" % l)],
              owner=xlh.h, is_store=True)

    def phase_X1(self, l):
        shd = self.D("send_halo_%d" % l, [128, 192], F32, ("P", l))
        gd = self.D("G_halo_%d" % l, [256, 192], F32, ("X1", l))
        o = H("cc1")
        self.S.collective("AllGather", [shd.ap().opt()], [gd.ap().opt()],
                          reads=[self.dh("send_halo_%d" % l)], writes=[self.dh("G_halo_%d" % l)], owner=o)

    def phase_X2(self, l):
        for nm, sshape, gshape, dt in (("kv", [192, NT], [384, NT], BF16), ("sum", [128, 128], [256, 128], F32)):
            sd = self.D("send_%s_%d" % (nm, l), sshape, dt, ("A", l))
            gd = self.D("G_%s_%d" % (nm, l), gshape, dt, ("X2", l))
            o = H("cc2" + nm)
            self.S.collective("AllGather", [sd.ap().opt()], [gd.ap().opt()],
                              reads=[self.dh("send_%s_%d" % (nm, l))], writes=[self.dh("G_%s_%d" % (nm, l))], owner=o)

    def phase_A(self, l):
        S = self.S
        xname, xw = ("xT", "host") if l == 0 else ("x2_0", ("C2", 0))
        xd = self.D(xname, [NCH, 128, NT], F32, xw)
        posd = self.D("pos", [1, NT], I32, "host")
        ropd = self.D("rope", [4, 64, NT], F32, ("R",))
        ghd = self.D("G_halo_%d" % l, [256, 192], F32, ("X1", l))
        gayd = self.D("ga_y_%d" % l, [NCH, 128, NT], BF16, ("A", l))
        gaAd = self.D("ga_A_%d" % l, [NCH, 128, NT], BF16, ("A", l))
        tgbd = self.D("tgb_%d" % l, [NCH, 128, NT], BF16, ("A", l))
        cqd = self.D("cq_%d" % l, [2, 128, NT], BF16, ("A", l))
        skvd = self.D("send_kv_%d" % l, [192, NT], BF16, ("A", l))
        ssumd = self.D("send_sum_%d" % l, [128, 128], F32, ("A", l))
        lwa = self.D("lru_wa", [2, 8, 128, 128], F32, "host")
        lwx = self.D("lru_wx", [2, 8, 128, 128], F32, "host")
        wt = self.get_w_in(l, lru_only=False)
        base = l * PPL
        wa = self.sb("wa", [128, 8, 128], BF16)
        wx = self.sb("wx", [128, 8, 128], BF16)
        self.load_w(wa.t[:], lwa[l].rearrange("n i j -> i n j"), wa)
        self.load_w(wx.t[:], lwx[l].rearrange("n i j -> i n j"), wx)
        mv = self.load_modv(l)
        hb = self.sb("hb", [128, 16], F32)
        c05 = self.sb("c05", [128, 8], F32)
        S.op("dve", lambda e: e.tensor_scalar(hb.t[:], self.ppc(base + 88, 16), 0.5, None, ALU.mult),
             reads=[self.pp.h], writes=[hb.h])
        S.op("act", lambda e: e.activation(c05.t[:], self.ppc(base + 104, 8), AF.Exp, scale=-1.0),
             reads=[self.pp.h], writes=[c05.h])
        S.op("act", lambda e: e.activation(c05.t[:], c05.t[:], AF.Ln, bias=1.0, scale=1.0), reads=[c05.h], writes=[c05.h])
        S.op("dve", lambda e: e.tensor_scalar(c05.t[:], c05.t[:], -4.0, None, ALU.mult), reads=[c05.h], writes=[c05.h])
        gh = self.sb("gh", [128, 2, 8, 8, 3], F32)
        halo = self.sb("halo", [128, 8, 8, 3], F32)
        for s in range(2):
            S.dma("sp", gh.t[:, s].rearrange("p m i t -> p (m i t)"), ghd[s * 128:(s + 1) * 128, :],
                  reads=[self.dh("G_halo_%d" % l)], writes=[gh.h], owner=gh.h)
        f0 = self.ppc(PP_FLAG, 1)
        f1 = self.ppc(PP_FLAG + 1, 1)
        S.op("dve", lambda e: e.memset(halo.t[:], 0.0), writes=[halo.h])
        S.op("dve", lambda e: e.tensor_scalar(halo.t[:, :, 1:8, :], gh.t[:, 1, :, 0:7, :], f0, None, ALU.mult),
             reads=[gh.h, self.pp.h], writes=[halo.h])
        S.op("dve", lambda e: e.scalar_tensor_tensor(halo.t[:], gh.t[:, 0], f1, halo.t[:], ALU.mult, ALU.add),
             reads=[gh.h, halo.h, self.pp.h], writes=[halo.h])
        summ = self.sb("summ", [128, 2, 8, 8], F32)
        xt = self.sb("xtA", [128, 8, TB], F32)
        hT = self.sb("hT", [128, 8, TB], BF16)
        rs = self.sb("rsA", [128, TB], F32)
        sq_ring = self.ring("sqA", 2, [128, TB], BF16)
        tmp_ring = self.ring("tmpA", 2, [128, TB], F32)
        posi = self.sb("posiA", [128, TB], I32)
        posf = self.sb("posfA", [128, TB], F32)
        mb2 = self.sb("mb2", [128, TB], F32)
        rope = self.sb("ropeA", [64, 2, TB], F32)
        ui = self.sb("ui", [128, 8, TB], F32)
        aT = self.sb("aT", [128, 8, TB], F32)
        tga = self.sb("tga", [128, 8, TB], BF16)
        xl_ring = self.ring("xl", 1, [128, TB + 3], F32)
        u_ring = self.ring("u", 2, [128, TB], F32)
        ubf_ring = self.ring("ubf", 1, [128, TB], BF16)
        tr_ring = self.ring("tr", 1, [128, TB], F32)
        tiv_ring = self.ring("tiv", 1, [128, TB], F32)
        m4_ring = self.ring("m4", 1, [128, TB], F32)
        b_ring = self.ring("bb", 1, [128, TB], F32)
        h0_ring = self.ring("h0", 1, [128, TB], F32)
        A_ring = self.ring("AA", 1, [128, TB], F32)
        ob_ring = self.ring("ob", 4, [128, TB], BF16)
        qd = self.sb("qd", [128, 2, TB], F32)
        kvd = self.sb("kvd", [128, TB], F32)
        rq = rs
        rkv = self.sb("rkv", [128, TB], F32)
        kp_ring = tmp_ring
        print("phase A sbuf used", self.sb_ptr - self.sb_base, "of", self.sb_top - self.sb_base, flush=True)
        cw = lambda k, c: self.ppc(base + 48 + k * 8 + c, 1)
        cb = lambda c: self.ppc(base + 80 + c, 1)
        for i in range(NB):
            sl = slice(i * TB, (i + 1) * TB)
            S.dma("sp", xt.t[:], xd.ap().rearrange("c p t -> p c t")[:, :, sl], reads=[self.dh(xname, (i,))],
                  writes=[xt.h], owner=xt.h)
            S.dma("sp", posi.t[:], posd[0:1, sl].partition_broadcast(128), writes=[posi.h], owner=posi.h)
            S.dma("sp", rope.t[:], ropd.ap().rearrange("f p t -> p f t")[:, 0:2, sl], reads=[self.dh("rope")],
                  writes=[rope.h], owner=rope.h)
            S.op("dve", lambda e: e.tensor_copy(posf.t[:], posi.t[:]), reads=[posi.h], writes=[posf.h])
            S.op("dve", lambda e: e.tensor_scalar(mb2.t[:], posf.t[:], 0.0, 2e6, ALU.is_equal, ALU.mult),
                 reads=[posf.h], writes=[mb2.h])
            self.norm_block(xt, TB, mv.t[:, l * 48 + 8:l * 48 + 16], mv.t[:, l * 48 + 0:l * 48 + 8], hT, rs, sq_ring, tmp_ring, mvh=mv.h)

            def proj(m0, msz, bank, poff=0):
                self.mm_group(self.ps[poff:poff + msz, bank, :], bank,
                              [(wt.t[:, k, m0:m0 + msz], hT.t[:, k, :]) for k in range(8)], reads=[wt.h, hT.h])

            banks = (0, 1, 2, 3, 4, 5)
            for c in range(NCH):
                xl = xl_ring.next()
                u = u_ring.next()
                ubf = ubf_ring.next()
                tr = tr_ring.next()
                tiv = tiv_ring.next()
                b0 = self.ps_next(banks)
                proj(c * 128, 128, b0)
                S.op("act", lambda e, xl=xl, b0=b0: e.activation(xl.t[:, 3:TB + 3], self.ps[:, b0, :], AF.Copy),
                     reads=[self.psh[b0]], writes=[xl.h])
                S.op("pool", lambda e, xl=xl, c=c, i=i: e.tensor_copy(xl.t[:, 0:3], halo.t[:, c, i, :]),
                     reads=[halo.h, xl.h], writes=[xl.h])
                S.op("pool", lambda e, xl=xl, u=u, c=c: e.tensor_scalar(u.t[:], xl.t[:, 0:TB], cw(0, c), cb(c), ALU.mult, ALU.add),
                     reads=[xl.h, self.pp.h], writes=[u.h])
                for k in range(1, 4):
                    S.op("dve", lambda e, xl=xl, u=u, c=c, k=k: e.scalar_tensor_tensor(u.t[:], xl.t[:, k:k + TB], cw(k, c), u.t[:],
                                                                                        ALU.mult, ALU.add),
                         reads=[xl.h, u.h, self.pp.h], writes=[u.h])
                S.op("act", lambda e, u=u, ubf=ubf: e.activation(ubf.t[:], u.t[:], AF.Copy), reads=[u.h], writes=[ubf.h])
                b1 = self.ps_next(banks)
                b2 = self.ps_next(banks)
                self.mm_group(self.ps[:, b1, :], b1, [(wa.t[:, c, :], ubf.t[:])], reads=[wa.h, ubf.h])
                self.mm_group(self.ps[:, b2, :], b2, [(wx.t[:, c, :], ubf.t[:])], reads=[wx.h, ubf.h])
                S.op("act", lambda e, tr=tr, b1=b1, c=c: e.activation(tr.t[:], self.ps[:, b1, :], AF.Tanh, bias=hb.t[:, c:c + 1], scale=0.5),
                     reads=[self.psh[b1], hb.h], writes=[tr.h])
                S.op("act", lambda e, tiv=tiv, b2=b2, c=c: e.activation(tiv.t[:], self.ps[:, b2, :], AF.Tanh, bias=hb.t[:, 8 + c:9 + c], scale=0.5),
                     reads=[self.psh[b2], hb.h], writes=[tiv.h])
                S.op("pool", lambda e, tr=tr: e.tensor_tensor(tr.t[:], tr.t[:], mb2.t[:], ALU.add),
                     reads=[tr.h, mb2.h], writes=[tr.h])
                S.op("act", lambda e, tr=tr, c=c: e.activation(aT.t[:, c, :], tr.t[:], AF.Exp, bias=c05.t[:, c:c + 1], scale=c05.t[:, c:c + 1]),
                     reads=[tr.h, c05.h], writes=[aT.h])
                S.op("dve", lambda e, tiv=tiv, u=u, c=c: e.scalar_tensor_tensor(ui.t[:, c, :], tiv.t[:], 1.0, u.t[:], ALU.add, ALU.mult),
                     reads=[tiv.h, u.h], writes=[ui.h])
                b3 = self.ps_next(banks)
                proj(1536 + c * 128, 128, b3)
                S.op("act", lambda e, b3=b3, c=c: e.activation(tga.t[:, c, :], self.ps[:, b3, :], AF.Tanh, scale=0.5),
                     reads=[self.psh[b3]], writes=[tga.h])
                b4 = self.ps_next(banks)
                proj(2560 + c * 128, 128, b4)
                ob = ob_ring.next()
                S.op("act", lambda e, b4=b4, ob=ob: e.activation(ob.t[:], self.ps[:, b4, :], AF.Tanh, scale=0.5),
                     reads=[self.psh[b4]], writes=[ob.h])
                S.dma("sp", tgbd[c, :, sl], ob.t[:], reads=[ob.h], writes=[self.dh("tgb_%d" % l, (c, i))], owner=ob.h, is_store=True)
            for k2 in range(2):
                b = self.ps_next(banks)
                proj(1024 + k2 * 128, 128, b)
                S.op("act", lambda e, b=b, k2=k2: e.activation(qd.t[:, k2, :], self.ps[:, b, :], AF.Copy),
                     reads=[self.psh[b]], writes=[qd.h])
            b = self.ps_next(banks)
            proj(1280, 128, b)
            S.op("act", lambda e, b=b: e.activation(kvd.t[:], self.ps[:, b, :], AF.Copy), reads=[self.psh[b]], writes=[kvd.h])
            bA = self.ps_next(banks)
            bB = self.ps_next(banks)
            proj(1408, 64, bA)
            proj(1472, 64, bB)
            kp1 = kp_ring.next()
            kp2 = kp_ring.next()
            ob = ob_ring.next()
            S.op("dve", lambda e, kp1=kp1, bA=bA: e.tensor_tensor(kp1.t[0:64, :], self.ps[0:64, bA, :], rope.t[:, 0, :], ALU.mult),
                 reads=[self.psh[bA], rope.h], writes=[kp1.h])
            S.op("dve", lambda e, kp2=kp2, bB=bB: e.tensor_tensor(kp2.t[0:64, :], self.ps[0:64, bB, :], rope.t[:, 1, :], ALU.mult),
                 reads=[self.psh[bB], rope.h], writes=[kp2.h])
            S.op("dve", lambda e, kp1=kp1, kp2=kp2, ob=ob: e.tensor_tensor(ob.t[0:64, :], kp1.t[0:64, :], kp2.t[0:64, :], ALU.add),
                 reads=[kp1.h, kp2.h], writes=[ob.h])
            S.dma("sp", skvd[128:192, sl], ob.t[0:64, :], reads=[ob.h], awrites=[self.dh("send_kv_%d" % l)], owner=ob.h, is_store=True)
            bq = self.ps_next(banks)
            for k2 in range(2):
                sq = sq_ring.next()
                S.op("act", lambda e, sq=sq, k2=k2: e.activation(sq.t[:], qd.t[:, k2, :], AF.Square), reads=[qd.h], writes=[sq.h])
                S.op("pe", lambda e, sq=sq, k2=k2, bq=bq: e.matmul(self.ps[:, bq, :], self.ones.t[:, :], sq.t[:], start=(k2 == 0), stop=(k2 == 1)),
                     reads=[sq.h, self.ones.h], writes=[self.psh[bq]])
            bk = self.ps_next(banks)
            sq = sq_ring.next()
            S.op("act", lambda e, sq=sq: e.activation(sq.t[:], kvd.t[:], AF.Square), reads=[kvd.h], writes=[sq.h])
            S.op("pe", lambda e, sq=sq, bk=bk: e.matmul(self.ps[:, bk, :], self.ones.t[:, :], sq.t[:], start=True, stop=True),
                 reads=[sq.h, self.ones.h], writes=[self.psh[bk]])
            S.op("act", lambda e, bq=bq: e.activation(rq.t[:], self.ps[:, bq, :], AF.Sqrt, bias=EPS, scale=1.0 / 256), reads=[self.psh[bq]], writes=[rq.h])
            S.op("act", lambda e, bk=bk: e.activation(rkv.t[:], self.ps[:, bk, :], AF.Sqrt, bias=EPS, scale=1.0 / 128), reads=[self.psh[bk]], writes=[rkv.h])
            m4s = []
            S.op("dve", lambda e: e.reciprocal(rq.t[:], rq.t[:]), reads=[rq.h], writes=[rq.h])
            S.op("dve", lambda e: e.reciprocal(rkv.t[:], rkv.t[:]), reads=[rkv.h], writes=[rkv.h])
            for k2 in range(2):
                tmp = tmp_ring.next()
                ob = ob_ring.next()
                S.op("dve", lambda e, tmp=tmp, k2=k2: e.tensor_tensor(tmp.t[:], qd.t[:, k2, :], rq.t[:], ALU.mult), reads=[qd.h, rq.h], writes=[tmp.h])
                S.op("dve", lambda e, tmp=tmp, ob=ob, k2=k2: e.tensor_scalar(ob.t[:], tmp.t[:], self.ppc(base + 112 + k2, 1), None, ALU.mult),
                     reads=[tmp.h, self.pp.h], writes=[ob.h])
                S.dma("sp", cqd[k2, :, sl], ob.t[:], reads=[ob.h], writes=[self.dh("cq_%d" % l, (k2, i))], owner=ob.h, is_store=True)
            tmp = tmp_ring.next()
            ob = ob_ring.next()
            S.op("dve", lambda e, tmp=tmp: e.tensor_tensor(tmp.t[:], kvd.t[:], rkv.t[:], ALU.mult), reads=[kvd.h, rkv.h], writes=[tmp.h])
            S.op("dve", lambda e, tmp=tmp, ob=ob: e.tensor_scalar(ob.t[:], tmp.t[:], self.ppc(base + 114, 1), None, ALU.mult),
                 reads=[tmp.h, self.pp.h], writes=[ob.h])
            S.dma("sp", skvd[0:128, sl], ob.t[:], reads=[ob.h], awrites=[self.dh("send_kv_%d" % l)], owner=ob.h, is_store=True)
            for c in range(NCH):
                m4 = m4_ring.next()
                bt = b_ring.next()
                h0 = h0_ring.next()
                AA = A_ring.next()
                S.op("pool", lambda e, m4=m4, c=c: e.tensor_tensor(m4.t[:], aT.t[:, c, :], aT.t[:, c, :], ALU.mult), reads=[aT.h], writes=[m4.h])
                S.op("act", lambda e, m4=m4: e.activation(m4.t[:], m4.t[:], AF.Sqrt, bias=1.0 / 16, scale=-1.0 / 16), reads=[m4.h], writes=[m4.h])
                S.op("dve", lambda e, bt=bt, m4=m4, c=c: e.tensor_tensor(bt.t[:], ui.t[:, c, :], m4.t[:], ALU.mult), reads=[ui.h, m4.h], writes=[bt.h])
                S.op("dve", lambda e, bt=bt, h0=h0, c=c: e.tensor_tensor_scan(h0.t[:], aT.t[:, c, :], bt.t[:], 0.0, ALU.mult, ALU.add),
                     reads=[aT.h, bt.h], writes=[h0.h])
                S.op("dve", lambda e, AA=AA, c=c: e.tensor_tensor_scan(AA.t[:], aT.t[:, c, :], self.zeros.t[:], 1.0, ALU.mult, ALU.add),
                     reads=[aT.h, self.zeros.h], writes=[AA.h])
                S.op("pool", lambda e, h0=h0, c=c, i=i: e.tensor_copy(summ.t[:, 0, c, i:i + 1], h0.t[:, TB - 1:TB]), reads=[h0.h], writes=[summ.h])
                S.op("pool", lambda e, AA=AA, c=c, i=i: e.tensor_copy(summ.t[:, 1, c, i:i + 1], AA.t[:, TB - 1:TB]), reads=[AA.h, summ.h], writes=[summ.h])
                ob1 = ob_ring.next()
                ob2 = ob_ring.next()
                S.op("dve", lambda e, ob1=ob1, h0=h0, c=c: e.scalar_tensor_tensor(ob1.t[:], tga.t[:, c, :], 1.0, h0.t[:], ALU.add, ALU.mult),
                     reads=[tga.h, h0.h], writes=[ob1.h])
                S.op("dve", lambda e, ob2=ob2, AA=AA, c=c: e.scalar_tensor_tensor(ob2.t[:], tga.t[:, c, :], 1.0, AA.t[:], ALU.add, ALU.mult),
                     reads=[tga.h, AA.h], writes=[ob2.h])
                S.dma("sp", gayd[c, :, sl], ob1.t[:], reads=[ob1.h], writes=[self.dh("ga_y_%d" % l, (c, i))], owner=ob1.h, is_store=True)
                S.dma("sp", gaAd[c, :, sl], ob2.t[:], reads=[ob2.h], writes=[self.dh("ga_A_%d" % l, (c, i))], owner=ob2.h, is_store=True)
        S.dma("sp", ssumd[:, :], summ.t[:].rearrange("p a c i -> p (a c i)"), reads=[summ.h], writes=[self.dh("send_sum_%d" % l)],
              owner=summ.h, is_store=True)
        self._w_in_key = None

    def phase_B(self, l):
        S = self.S
        gkvd = self.D("G_kv_%d" % l, [384, NT], BF16, ("X2", l))
        cqd = self.D("cq_%d" % l, [2, 128, NT], BF16, ("A", l))
        tgbd = self.D("tgb_%d" % l, [NCH, 128, NT], BF16, ("A", l))
        gbyd = self.D("gb_y_%d" % l, [NCH, 128, NT], BF16, ("B", l))
        ropd = self.D("rope", [4, 64, NT], F32, ("R",))
        maskd = self.D("masks", [128, 8 * TB], F32, "host")
        wuqd = self.D("w_uq", [2, 256, HEADS, 192], F32, "host")
        wukvd = self.D("w_ukv", [2, 128, HEADS, 256], F32, "host")
        ckvT = self.sb("ckvT", [128, 2 * NT], BF16)
        kpeT = self.sb("kpeT", [64, 2 * NT], BF16)
        cqT = self.sb("cqT", [128, 2, NT], BF16)
        wuq = self.sb("wuq", [128, 2, HEADS, 256], BF16)
        wukv = self.sb("wukv", [128, HEADS, 256], BF16)
        maskf = self.sb("maskf", [128, TB], F32)
        masks = self.sb("masks", [128, 8, TB], BF16)
        ident = self.sb("ident", [128, 128], F32)
        KnT = self.sb("KnT", [128, 2 * NT], BF16)
        Vaug = self.sb("Vaug", [128, 64, 132], BF16)
        qnT = self.sb("qnT", [128, NT], BF16)
        qpeT = self.sb("qpeT", [64, NT], BF16)
        rope_ring = self.ring("ropeB", 2, [64, 2, TB], F32)
        pT_ring = self.ring("pT", 3, [128, TB], BF16)
        t1_ring = self.ring("t1", 2, [64, TB], F32)
        t2_ring = self.ring("t2", 2, [64, TB], F32)
        rc_ring = self.ring("rc", 4, [128, 1], F32)
        o_ring = self.ring("osb", 4, [128, 128], F32)
        tgb_ring = self.ring("tgbB", 2, [128, TB], BF16)
        gby_ring = self.ring("gby", 2, [128, TB], BF16)
        print("phase B sbuf used", self.sb_ptr - self.sb_base, "of", self.sb_top - self.sb_base, flush=True)
        gkv = self.dh("G_kv_%d" % l)
        for s in range(2):
            S.dma("sp", ckvT.t[:, s * NT:(s + 1) * NT], gkvd[s * 192:s * 192 + 128, :], reads=[gkv], writes=[ckvT.h], owner=ckvT.h)
            S.dma("sp", kpeT.t[:, s * NT:(s + 1) * NT], gkvd[s * 192 + 128:s * 192 + 192, :], reads=[gkv], writes=[kpeT.h], owner=kpeT.h)
        for k2 in range(2):
            S.dma("sp", cqT.t[:, k2, :], cqd[k2, :, :], reads=[self.dh("cq_%d" % l, (k2, i)) for i in range(NB)], writes=[cqT.h], owner=cqT.h)
        src = wuqd[l].rearrange("(k p) h d -> p k h d", p=128)
        for k2 in range(2):
            self.load_w(wuq.t[:, k2, :, 0:192], src[:, k2, :, :], wuq)
            self.load_w(wuq.t[:, k2, :, 192:224], src[:, k2, :, 160:192], wuq)
            self.load_w(wuq.t[:, k2, :, 224:256], src[:, k2, :, 128:160], wuq)
        S.op("dve", lambda e: e.tensor_scalar(wuq.t[:, :, :, 192:224], wuq.t[:, :, :, 192:224], -1.0, None, ALU.mult),
             reads=[wuq.h], writes=[wuq.h])
        self.load_w(wukv.t[:], wukvd[l], wukv)
        for j in range(8):
            S.dma("sp", maskf.t[:], maskd[:, j * TB:(j + 1) * TB], writes=[maskf.h], owner=maskf.h)
            S.op("dve", lambda e, j=j: e.tensor_copy(masks.t[:, j, :], maskf.t[:]), reads=[maskf.h], writes=[masks.h])
        S.op("pool", lambda e: e.memset(ident.t[:], 0.0), writes=[ident.h])
        S.op("pool", lambda e: e.affine_select(out=ident.t[:], in_=ident.t[:], pattern=[[-1, 128]], compare_op=ALU.not_equal,
                                               fill=1.0, base=0, channel_multiplier=1), reads=[ident.h], writes=[ident.h])
        S.op("pool", lambda e: e.memset(Vaug.t[:, :, 128:132], 1.0), writes=[Vaug.h])
        sbanks = (0, 1, 2)
        for h in range(HEADS):
            for t in range(16):
                b = self.ps_next(sbanks)
                self.mm_group(self.ps[:, b, :], b, [(wukv.t[:, h, 0:128], ckvT.t[:, t * TB:(t + 1) * TB])], reads=[wukv.h, ckvT.h])
                eng = "act" if t % 2 == 0 else "dve"
                if eng == "act":
                    S.op("act", lambda e, b=b, t=t: e.activation(KnT.t[:, t * TB:(t + 1) * TB], self.ps[:, b, :], AF.Copy),
                         reads=[self.psh[b]], writes=[KnT.h])
                else:
                    S.op("dve", lambda e, b=b, t=t: e.tensor_copy(KnT.t[:, t * TB:(t + 1) * TB], self.ps[:, b, :]),
                         reads=[self.psh[b]], writes=[KnT.h])
            for t4 in range(16):
                b = self.ps_next(sbanks)
                for u4 in range(4):
                    t = t4 * 4 + u4
                    S.op("pe", lambda e, b=b, t=t, u4=u4, h=h: e.matmul(self.ps[:, b, u4 * 128:(u4 + 1) * 128], ckvT.t[:, t * 128:(t + 1) * 128],
                                                                   wukv.t[:, h, 128:256], start=True, stop=True),
                         reads=[wukv.h, ckvT.h], writes=[self.psh[b]], inc=(u4 == 3))
                if t4 % 2 == 0:
                    S.op("dve", lambda e, b=b, t4=t4: e.tensor_copy(Vaug.t[:, t4 * 4:t4 * 4 + 4, 0:128],
                                                                     self.ps[:, b, :].rearrange("p (u d) -> p u d", d=128)),
                         reads=[self.psh[b]], writes=[Vaug.h])
                else:
                    S.op("act", lambda e, b=b, t4=t4: e.activation(Vaug.t[:, t4 * 4:t4 * 4 + 4, 0:128],
                                                                    self.ps[:, b, :].rearrange("p (u d) -> p u d", d=128), AF.Copy),
                         reads=[self.psh[b]], writes=[Vaug.h])
            for i in range(NB):
                sl = slice(i * TB, (i + 1) * TB)
                rp = rope_ring.next()
                S.dma("sp", rp.t[:], ropd.ap().rearrange("f p t -> p f t")[:, 2:4, sl], reads=[self.dh("rope")], writes=[rp.h], owner=rp.h)
                b = self.ps_next(sbanks)
                self.mm_group(self.ps[:, b, :], b, [(wuq.t[:, k2, h, 0:128], cqT.t[:, k2, sl]) for k2 in range(2)], reads=[wuq.h, cqT.h])
                S.op("act", lambda e, b=b, sl=sl: e.activation(qnT.t[:, sl], self.ps[:, b, :], AF.Identity, scale=QSCALE),
                     reads=[self.psh[b]], writes=[qnT.h])
                bA = self.ps_next(sbanks)
                self.mm_group(self.ps[0:64, bA, :], bA, [(wuq.t[:, k2, h, 128:192], cqT.t[:, k2, sl]) for k2 in range(2)], reads=[wuq.h, cqT.h])
                bB = self.ps_next(sbanks)
                self.mm_group(self.ps[0:64, bB, :], bB, [(wuq.t[:, k2, h, 192:256], cqT.t[:, k2, sl]) for k2 in range(2)], reads=[wuq.h, cqT.h])
                t1 = t1_ring.next()
                t2 = t2_ring.next()
                S.op("dve", lambda e, t1=t1, bA=bA, rp=rp: e.tensor_tensor(t1.t[:], self.ps[0:64, bA, :], rp.t[:, 0, :], ALU.mult),
                     reads=[self.psh[bA], rp.h], writes=[t1.h])
                S.op("dve", lambda e, t2=t2, bB=bB, rp=rp: e.tensor_tensor(t2.t[:], self.ps[0:64, bB, :], rp.t[:, 1, :], ALU.mult),
                     reads=[self.psh[bB], rp.h], writes=[t2.h])
                S.op("dve", lambda e, t1=t1, t2=t2, sl=sl: e.tensor_tensor(qpeT.t[:, sl], t1.t[:], t2.t[:], ALU.add),
                     reads=[t1.h, t2.h], writes=[qpeT.h])
            for i in range(NB):
                sl = slice(i * TB, (i + 1) * TB)
                chunks = []
                for jp in range(i + 1):
                    for sp_ in range(2):
                        for cc in range(4):
                            chunks.append((sp_ * 32 + jp * 4 + cc, (sp_ * 4 + cc) if jp == i else None))
                nck = len(chunks)
                for ci, (kc, mk) in enumerate(chunks):
                    b = self.ps_next(sbanks)
                    ksl = slice(kc * 128, (kc + 1) * 128)
                    self.mm_group(self.ps[:, b, :], b, [(KnT.t[:, ksl], qnT.t[:, sl]), (kpeT.t[:, ksl], qpeT.t[:, sl])],
                                  reads=[KnT.h, qnT.h, kpeT.h, qpeT.h])
                    pT = pT_ring.next()
                    S.op("act", lambda e, pT=pT, b=b: e.activation(pT.t[:], self.ps[:, b, :], AF.Exp), reads=[self.psh[b]], writes=[pT.h])
                    if mk is not None:
                        S.op("pool", lambda e, pT=pT, mk=mk: e.tensor_tensor(pT.t[:], pT.t[:], masks.t[:, mk, :], ALU.mult),
                             reads=[pT.h, masks.h], writes=[pT.h])
                    for qs in range(4):
                        S.op("pe", lambda e, pT=pT, qs=qs, kc=kc, ci=ci, nck=nck: e.matmul(self.ps[:, 3 + qs, 0:129], pT.t[:, qs * 128:(qs + 1) * 128],
                                                                                           Vaug.t[:, kc, 0:129], start=(ci == 0), stop=(ci == nck - 1)),
                             reads=[pT.h, Vaug.h], writes=[self.psh[3 + qs]], inc=(qs == 3))
                tg = tgb_ring.next()
                S.dma("sp", tg.t[:], tgbd[h, :, sl], reads=[self.dh("tgb_%d" % l, (h, i))], writes=[tg.h], owner=tg.h)
                for qs in range(4):
                    rc = rc_ring.next()
                    osb = o_ring.next()
                    S.op("dve", lambda e, rc=rc, qs=qs: e.reciprocal(rc.t[:], self.ps[:, 3 + qs, 128:129]), reads=[self.psh[3 + qs]], writes=[rc.h])
                    S.op("dve", lambda e, rc=rc: e.tensor_scalar(rc.t[:], rc.t[:], 0.5, None, ALU.mult), reads=[rc.h], writes=[rc.h])
                    S.op("dve", lambda e, rc=rc, osb=osb, qs=qs: e.tensor_scalar(osb.t[:], self.ps[:, 3 + qs, 0:128], rc.t[:, 0:1], None, ALU.mult),
                         reads=[self.psh[3 + qs], rc.h], writes=[osb.h])
                    S.op("pe", lambda e, osb=osb, qs=qs: e.transpose(self.ps[:, 7, qs * 128:(qs + 1) * 128], osb.t[:], ident.t[:]),
                         reads=[osb.h, ident.h], writes=[self.psh[7]])
                gb = gby_ring.next()
                S.op("dve", lambda e, gb=gb, tg=tg: e.scalar_tensor_tensor(gb.t[:], tg.t[:], 1.0, self.ps[:, 7, :], ALU.add, ALU.mult),
                     reads=[tg.h, self.psh[7]], writes=[gb.h])
                S.dma("sp", gbyd[h, :, sl], gb.t[:], reads=[gb.h], writes=[self.dh("gb_y_%d" % l, (h, i))], owner=gb.h, is_store=True)

    def phase_C1(self, l):
        S = self.S
        xname, xw = ("xT", "host") if l == 0 else ("x2_0", ("C2", 0))
        xd = self.D(xname, [NCH, 128, NT], F32, xw)
        gayd = self.D("ga_y_%d" % l, [NCH, 128, NT], BF16, ("A", l))
        gaAd = self.D("ga_A_%d" % l, [NCH, 128, NT], BF16, ("A", l))
        gbyd = self.D("gb_y_%d" % l, [NCH, 128, NT], BF16, ("B", l))
        gsd = self.D("G_sum_%d" % l, [256, 128], F32, ("X2", l))
        x1d = self.D("x1_%d" % l, [NCH, 128, NT], F32, ("C1", l))
        h2d = self.D("h2_%d" % l, [NCH, 128, NT], BF16, ("C1", l))
        woutd = self.D("w_out", [2, D, D], F32, "host")
        wo = self.sb("wo", [128, 8, D], BF16)
        src = woutd[l].rearrange("(k p) n -> p k n", p=128)
        for k in range(8):
            self.load_w(wo.t[:, k, :], src[:, k, :], wo)
        mv = self.load_modv(l)
        gs = self.sb("gs", [128, 2, 2, 8, 8], F32)
        for s in range(2):
            S.dma("sp", gs.t[:, s].rearrange("p a c i -> p (a c i)"), gsd[s * 128:(s + 1) * 128, :], reads=[self.dh("G_sum_%d" % l)],
                  writes=[gs.h], owner=gs.h)
        inits = self.sb("inits", [128, 17, 8], F32)
        tmpc = self.sb("tmpc", [128, 8], F32)
        S.op("dve", lambda e: e.memset(inits.t[:], 0.0), writes=[inits.h])
        for g in range(15):
            s, j = g % 2, g // 2
            S.op("dve", lambda e, s=s, j=j, g=g: e.tensor_tensor(tmpc.t[:], gs.t[:, s, 1, :, j], inits.t[:, g, :], ALU.mult),
                 reads=[gs.h, inits.h], writes=[tmpc.h])
            S.op("dve", lambda e, s=s, j=j, g=g: e.tensor_tensor(inits.t[:, g + 1, :], tmpc.t[:], gs.t[:, s, 0, :, j], ALU.add),
                 reads=[gs.h, tmpc.h, inits.h], writes=[inits.h])
        io = self.sb("io", [128, 8, 8], F32)
        f0 = self.ppc(PP_FLAG, 1)
        f1 = self.ppc(PP_FLAG + 1, 1)
        iv = inits.t[:, 0:16, :].rearrange("p (i two) c -> p i two c", two=2)
        S.op("dve", lambda e: e.tensor_scalar(io.t[:], iv[:, :, 0, :], f0, None, ALU.mult), reads=[inits.h, self.pp.h], writes=[io.h])
        S.op("dve", lambda e: e.scalar_tensor_tensor(io.t[:], iv[:, :, 1, :], f1, io.t[:], ALU.mult, ALU.add),
             reads=[inits.h, io.h, self.pp.h], writes=[io.h])
        xt = self.sb("xtC", [128, 8, TB], F32)
        gay = self.sb("gay", [128, 8, TB], BF16)
        gaA = self.sb("gaA", [128, 8, TB], BF16)
        gby = self.sb("gbyC", [128, 8, TB], BF16)
        yT = self.sb("yT", [128, 8, TB], BF16)
        h2 = self.sb("h2", [128, 8, TB], BF16)
        rs = self.sb("rsC", [128, TB], F32)
        sq_ring = self.ring("sqC", 2, [128, TB], BF16)
        tmp_ring = self.ring("tmpC", 2, [128, TB], F32)
        print("phase C1 sbuf used", self.sb_ptr - self.sb_base, "of", self.sb_top - self.sb_base, flush=True)
        for i in range(NB):
            sl = slice(i * TB, (i + 1) * TB)
            S.dma("sp", xt.t[:], xd.ap().rearrange("c p t -> p c t")[:, :, sl], reads=[self.dh(xname, (i,))], writes=[xt.h], owner=xt.h)
            S.dma("sp", gay.t[:], gayd.ap().rearrange("c p t -> p c t")[:, :, sl], reads=[self.dh("ga_y_%d" % l, (c, i)) for c in range(8)],
                  writes=[gay.h], owner=gay.h)
            S.dma("sp", gaA.t[:], gaAd.ap().rearrange("c p t -> p c t")[:, :, sl], reads=[self.dh("ga_A_%d" % l, (c, i)) for c in range(8)],
                  writes=[gaA.h], owner=gaA.h)
            S.dma("sp", gby.t[:], gbyd.ap().rearrange("c p t -> p c t")[:, :, sl], reads=[self.dh("gb_y_%d" % l, (c, i)) for c in range(8)],
                  writes=[gby.h], owner=gby.h)
            for c in range(NCH):
                tmp = tmp_ring.next()
                S.op("dve", lambda e, tmp=tmp, c=c, i=i: e.scalar_tensor_tensor(tmp.t[:], gaA.t[:, c, :], io.t[:, i, c:c + 1], gay.t[:, c, :], ALU.mult, ALU.add),
                     reads=[gaA.h, gay.h, io.h], writes=[tmp.h])
                S.op("pool", lambda e, tmp=tmp, c=c: e.tensor_tensor(yT.t[:, c, :], tmp.t[:], gby.t[:, c, :], ALU.add),
                     reads=[tmp.h, gby.h], writes=[yT.h])
            for m in range(NCH):
                b = self.ps_next()
                self.mm_group(self.ps[:, b, :], b, [(wo.t[:, k, m * 128:(m + 1) * 128], yT.t[:, k, :]) for k in range(8)], reads=[wo.h, yT.h])
                S.op("dve", lambda e, b=b, m=m: e.scalar_tensor_tensor(xt.t[:, m, :], self.ps[:, b, :], mv.t[:, l * 48 + 16 + m:l * 48 + 17 + m],
                                                                       xt.t[:, m, :], ALU.mult, ALU.add),
                     reads=[self.psh[b], xt.h, mv.h], writes=[xt.h])
            S.dma("sp", x1d.ap().rearrange("c p t -> p c t")[:, :, sl], xt.t[:], reads=[xt.h], writes=[self.dh("x1_%d" % l, (i,))],
                  owner=xt.h, is_store=True)
            self.norm_block(xt, TB, mv.t[:, l * 48 + 32:l * 48 + 40], mv.t[:, l * 48 + 24:l * 48 + 32], h2, rs, sq_ring, tmp_ring, mvh=mv.h)
            S.dma("sp", h2d.ap().rearrange("c p t -> p c t")[:, :, sl], h2.t[:], reads=[h2.h], writes=[self.dh("h2_%d" % l, (i,))],
                  owner=h2.h, is_store=True)

    def phase_C2(self, l):
        S = self.S
        last = (l == 1)
        x1d = self.D("x1_%d" % l, [NCH, 128, NT], F32, ("C1", l))
        h2d = self.D("h2_%d" % l, [NCH, 128, NT], BF16, ("C1", l))
        oname = "out" if last else "x2_0"
        x2d = self.D(oname, [NCH, 128, NT], F32, ("C2", l))
        wfid = self.D("w_ffn_in", [2, D, 2 * DFF], F32, "host")
        wfod = self.D("w_ffn_out", [2, DFF, D], F32, "host")
        wfi = self.sb("wfi", [128, 8, 2 * DFF], BF16)
        wfo = self.sb("wfo", [128, NFF, D], BF16)
        src = wfid[l].rearrange("(k p) n -> p k n", p=128)
        for k in range(8):
            for half in range(2):
                self.load_w(wfi.t[:, k, half * DFF:(half + 1) * DFF], src[:, k, half * DFF:(half + 1) * DFF], wfi)
        src = wfod[l].rearrange("(k p) n -> p k n", p=128)
        for k in range(NFF):
            self.load_w(wfo.t[:, k, :], src[:, k, :], wfo)
        mv = self.load_modv(l)
        xt = self.sb("xtF", [128, 8, TB], F32)
        h2 = self.sb("h2F", [128, 8, TB], BF16)
        act = self.sb("actT", [128, NFF, TB], BF16)
        sg_ring = self.ring("sg", 2, [128, TB], F32)
        if last:
            rs = self.sb("rsF", [128, TB], F32)
            sq_ring = self.ring("sqF", 2, [128, TB], BF16)
            tmp_ring = self.ring("tmpF", 2, [128, TB], F32)
        print("phase C2 sbuf used", self.sb_ptr - self.sb_base, "of", self.sb_top - self.sb_base, flush=True)
        for i in range(NB):
            sl = slice(i * TB, (i + 1) * TB)
            S.dma("sp", xt.t[:], x1d.ap().rearrange("c p t -> p c t")[:, :, sl], reads=[self.dh("x1_%d" % l, (i,))], writes=[xt.h], owner=xt.h)
            S.dma("sp", h2.t[:], h2d.ap().rearrange("c p t -> p c t")[:, :, sl], reads=[self.dh("h2_%d" % l, (i,))], writes=[h2.h], owner=h2.h)
            for f in range(NFF):
                bA = self.ps_next()
                self.mm_group(self.ps[:, bA, :], bA, [(wfi.t[:, k, f * 128:(f + 1) * 128], h2.t[:, k, :]) for k in range(8)], reads=[wfi.h, h2.h])
                bB = self.ps_next()
                self.mm_group(self.ps[:, bB, :], bB, [(wfi.t[:, k, DFF + f * 128:DFF + (f + 1) * 128], h2.t[:, k, :]) for k in range(8)],
                              reads=[wfi.h, h2.h])
                sg = sg_ring.next()
                S.op("act", lambda e, sg=sg, bA=bA: e.activation(sg.t[:], self.ps[:, bA, :], AF.Silu), reads=[self.psh[bA]], writes=[sg.h])
                S.op("dve", lambda e, sg=sg, bB=bB, f=f: e.tensor_tensor(act.t[:, f, :], sg.t[:], self.ps[:, bB, :], ALU.mult),
                     reads=[sg.h, self.psh[bB]], writes=[act.h])
            for m in range(NCH):
                b = self.ps_next()
                self.mm_group(self.ps[:, b, :], b, [(wfo.t[:, k, m * 128:(m + 1) * 128], act.t[:, k, :]) for k in range(NFF)], reads=[wfo.h, act.h])
                S.op("dve", lambda e, b=b, m=m: e.scalar_tensor_tensor(xt.t[:, m, :], self.ps[:, b, :], mv.t[:, l * 48 + 40 + m:l * 48 + 41 + m],
                                                                       xt.t[:, m, :], ALU.mult, ALU.add),
                     reads=[self.psh[b], xt.h, mv.h], writes=[xt.h])
            if last:
                bank = self.ps_next()
                for c in range(NCH):
                    sq = sq_ring.next()
                    S.op("act", lambda e, sq=sq, c=c: e.activation(sq.t[:], xt.t[:, c, :], AF.Square), reads=[xt.h], writes=[sq.h])
                    S.op("pe", lambda e, sq=sq, c=c, bank=bank: e.matmul(self.ps[:, bank, :], self.ones.t[:, :], sq.t[:], start=(c == 0), stop=(c == NCH - 1)),
                         reads=[sq.h, self.ones.h], writes=[self.psh[bank]])
                S.op("act", lambda e, bank=bank: e.activation(rs.t[:], self.ps[:, bank, :], AF.Sqrt, bias=EPS, scale=1.0 / D),
                     reads=[self.psh[bank]], writes=[rs.h])
                S.op("dve", lambda e: e.reciprocal(rs.t[:], rs.t[:]), reads=[rs.h], writes=[rs.h])
                for c in range(NCH):
                    S.op("dve", lambda e, c=c: e.scalar_tensor_tensor(xt.t[:, c, :], xt.t[:, c, :], self.ppc(PP_FG + c, 1), rs.t[:], ALU.mult, ALU.mult),
                         reads=[xt.h, rs.h, self.pp.h], writes=[xt.h])
            S.dma("sp", x2d.ap().rearrange("c p t -> p c t")[:, :, sl], xt.t[:], reads=[xt.h], writes=[self.dh(oname, (i,))],
                  owner=xt.h, is_store=True)


class _Ring:
    def __init__(self, tiles):
        self.tiles = tiles
        self.i = 0

    def next(self):
        t = self.tiles[self.i % len(self.tiles)]
        self.i += 1
        return t


def _host_inputs(inp):
    x = np.asarray(inp["x"], np.float32)
    cores = []
    shared = {
        "w_ada": np.ascontiguousarray(inp["w_ada"], np.float32),
        "w_in": np.ascontiguousarray(inp["w_in"], np.float32),
        "lru_wa": np.ascontiguousarray(inp["lru_wa"], np.float32),
        "lru_wx": np.ascontiguousarray(inp["lru_wx"], np.float32),
        "w_uq": np.ascontiguousarray(inp["w_uq"], np.float32),
        "w_ukv": np.ascontiguousarray(inp["w_ukv"], np.float32),
        "w_out": np.ascontiguousarray(inp["w_out"], np.float32),
        "w_ffn_in": np.ascontiguousarray(inp["w_ffn_in"], np.float32),
        "w_ffn_out": np.ascontiguousarray(inp["w_ffn_out"], np.float32),
    }
    half = 32
    invf = (10000.0 ** (-np.arange(0, 64, 2, dtype=np.float32) / 64)).astype(np.float32)
    invf64 = np.concatenate([invf, invf]).astype(np.float32)
    p = np.arange(128)[:, None]
    f = np.arange(TB)[None, :]
    diag = [((128 * j + p) <= f).astype(np.float32) for j in range(4)]
    onesm = np.ones((128, TB), np.float32)
    zerom = np.zeros((128, TB), np.float32)
    for core in range(8):
        b, r = core // 2, core % 2
        tok = np.concatenate([np.arange((2 * i + r) * TB, (2 * i + r + 1) * TB) for i in range(NB)])
        xs = x[b][tok]
        xT = np.ascontiguousarray(xs.T.reshape(NCH, 128, NT))
        pos = np.ascontiguousarray(np.asarray(inp["positions"])[b][tok].astype(np.int32)[None, :])
        pp = np.zeros((128, NPP), np.float32)
        for l in range(2):
            base = l * PPL
            pp[:, base:base + 48] = np.asarray(inp["b_ada"])[l].reshape(48, 128).T
            pp[:, base + 48:base + 80] = np.asarray(inp["conv_w"])[l].reshape(4, 8, 128).transpose(2, 0, 1).reshape(128, 32)
            pp[:, base + 80:base + 88] = np.asarray(inp["conv_b"])[l].reshape(8, 128).T
            pp[:, base + 88:base + 96] = np.asarray(inp["lru_ba"])[l].T
            pp[:, base + 96:base + 104] = np.asarray(inp["lru_bx"])[l].T
            pp[:, base + 104:base + 112] = np.asarray(inp["lru_a_param"])[l].reshape(8, 128).T
            pp[:, base + 112:base + 114] = np.asarray(inp["q_norm_g"])[l].reshape(2, 128).T
            pp[:, base + 114] = np.asarray(inp["kv_norm_g"])[l]
        pp[:, PP_FG:PP_FG + 8] = np.asarray(inp["final_norm_g"]).reshape(8, 128).T
        pp[:, PP_FLAG] = 1.0 - r
        pp[:, PP_FLAG + 1] = float(r)
        pp[0:64, PP_INVF] = invf64
        pp[0:64, PP_INVF + 1] = (invf64.astype(np.float64) / (2 * np.pi)).astype(np.float32)
        pp[:, PP_C:PP_C + 8] = np.asarray(inp["c"])[b].reshape(8, 128).T
        if r == 0:
            mk = diag + [zerom] * 4
        else:
            mk = [onesm] * 4 + diag
        masks = np.ascontiguousarray(np.concatenate(mk, axis=1))
        d = dict(shared)
        d.update({"xT": xT, "pos": pos, "pp": pp, "masks": masks})
        cores.append(d)
    return cores


def _assemble(outs):
    out = np.zeros((BATCH, SEQ, D), np.float32)
    for core in range(8):
        b, r = core // 2, core % 2
        oT = np.asarray(outs[core]).reshape(D, NT)
        for i in range(NB):
            g = 2 * i + r
            out[b, g * TB:(g + 1) * TB, :] = oT[:, i * TB:(i + 1) * TB].T
    return out


def _all_phases():
    ph = [("M",), ("R",)]
    for l in range(2):
        ph += [("P", l), ("X1", l), ("A", l), ("X2", l), ("B", l), ("C1", l), ("C2", l)]
    return ph


def kernel(**inputs):
    cores = _host_inputs(inputs)
    if MODE == "fused":
        bld = Builder(_all_phases(), fused=True)
        nc = bld.build()
        in_maps = [{k: c[k] for k in bld.ext_in} for c in cores]
        res = run_bass_kernel_spmd(nc, in_maps, core_ids=list(range(8)))
        return _assemble([res.results[i]["out"] for i in range(8)])
    store = [dict(c) for c in cores]
    for ph in _all_phases():
        if ph[0] == "X1":
            l = ph[1]
            for pair in range(4):
                g = np.concatenate([store[2 * pair]["send_halo_%d" % l], store[2 * pair + 1]["send_halo_%d" % l]], axis=0)
                store[2 * pair]["G_halo_%d" % l] = g
                store[2 * pair + 1]["G_halo_%d" % l] = g
            continue
        if ph[0] == "X2":
            l = ph[1]
            for pair in range(4):
                for nm in ("kv", "sum"):
                    g = np.concatenate([store[2 * pair]["send_%s_%d" % (nm, l)], store[2 * pair + 1]["send_%s_%d" % (nm, l)]], axis=0)
                    store[2 * pair]["G_%s_%d" % (nm, l)] = g
                    store[2 * pair + 1]["G_%s_%d" % (nm, l)] = g
            continue
        bld = Builder([ph], fused=False)
        nc = bld.build()
        in_maps = [{k: s[k] for k in bld.ext_in} for s in store]
        res = run_bass_kernel_spmd(nc, in_maps, core_ids=list(range(8)))
        for i in range(8):
            for k in bld.ext_out:
                store[i][k] = res.results[i][k]
    return _assemble([store[i]["out"] for i in range(8)])
```

```python
import numpy as np
from contextlib import ExitStack
import concourse.bass as bass
import concourse.mybir as mybir
from concourse.bass_utils import run_bass_kernel_spmd

F32 = mybir.dt.float32
BF16 = mybir.dt.bfloat16
I32 = mybir.dt.int32
AF = mybir.ActivationFunctionType
ALU = mybir.AluOpType

D = 1024
NCH = 8
SEQ = 8192
BATCH = 4
TB = 512
NB = 8
NT = NB * TB
HEADS = 8
DFF = 2816
NFF = 22
DIN = 3520
DAUG = 3584
EPS = 1e-6
QSCALE = 192 ** -0.5
GROUPS = [[0, 1], [2, 3], [4, 5], [6, 7]]
PPL = 116
PP_FG = 232
PP_FLAG = 240
PP_INVF = 242
PP_C = 244
NPP = 252

MODE = "fused"


class H:
    __slots__ = ("name", "w", "r", "sem", "cnt")

    def __init__(self, name):
        self.name = name
        self.w = []
        self.r = []
        self.sem = None
        self.cnt = 0


class _Eng:
    def __init__(self, name, sem):
        self.name = name
        self.sem = sem
        self.cnt = 0
        self.known = {}
        self.prog = []


class Sched:
    ENGS = ("pe", "act", "dve", "pool", "sp")

    def __init__(self, nc, stack, n_dma_sems=80):
        self.nc = nc
        self.E = {}
        for n in self.ENGS:
            sem = stack.enter_context(nc.semaphore("s_" + n))
            self.E[n] = _Eng(n, sem)
        self.free_sems = []
        for i in range(n_dma_sems):
            self.free_sems.append([stack.enter_context(nc.semaphore("d%d" % i)), 0])
        self.live = []
        self.store_tickets = {}
        self.n_ops = 0
        self.n_waits = 0

    def _deps(self, E, reads, writes, awrites=()):
        deps = []
        for h in reads:
            deps.extend(h.w)
        for h in writes:
            deps.extend(h.w)
            for t in h.r:
                if t[0] is E.sem:
                    continue
                deps.append(t)
        for h in awrites:
            for t in h.r:
                deps.append(t)
        return deps

    def _emit_waits(self, E, deps):
        best = {}
        for (sem, val) in deps:
            k = id(sem)
            if sem is E.sem and (E.name == "pe" or val > E.cnt):
                continue
            if E.known.get(k, 0) >= val:
                continue
            if k not in best or best[k][1] < val:
                best[k] = (sem, val)
        for k, (sem, val) in best.items():
            E.known[k] = val
            E.prog.append(("wait", sem, val))
            self.n_waits += 1

    def _update(self, ticket, reads, writes, awrites=()):
        for h in reads:
            h.r = [t for t in h.r if t[0] is not ticket[0]]
            h.r.append(ticket)
        for h in writes:
            h.w = [ticket]
            h.r = []
        for h in awrites:
            h.w = [t for t in h.w if t[0] is not ticket[0]]
            h.w.append(ticket)
            h.r = []

    def op(self, ename, fn, reads=(), writes=(), inc=True):
        E = self.E[ename]
        self._emit_waits(E, self._deps(E, reads, writes))
        if inc:
            E.cnt += 1
            ticket = (E.sem, E.cnt)
        else:
            ticket = (E.sem, E.cnt + 1)
        E.prog.append(("op", fn, inc))
        self._update(ticket, reads, writes)
        self.n_ops += 1
        return ticket

    def _hsem(self, h):
        if h.sem is None:
            if not self.free_sems:
                raise RuntimeError("out of DMA semaphores")
            ent = self.free_sems.pop()
            h.sem = ent[0]
            h.cnt = ent[1]
            self.live.append(h)
        return h.sem

    def dma(self, q, out, in_, reads=(), writes=(), awrites=(), owner=None, is_store=False, **kw):
        E = self.E[q]
        self._emit_waits(E, self._deps(E, reads, writes, awrites))
        sem = self._hsem(owner)
        owner.cnt += 16
        ticket = (sem, owner.cnt)
        E.prog.append(("dma", out, in_, sem, kw))
        self._update(ticket, reads, writes, awrites)
        if is_store:
            self.store_tickets[id(sem)] = ticket
        self.n_ops += 1
        return ticket

    def collective(self, kind, ins, outs, reads, writes, owner):
        E = self.E["pool"]
        self._emit_waits(E, self._deps(E, reads, writes))
        sem = self._hsem(owner)
        owner.cnt += 1
        ticket = (sem, owner.cnt)
        E.prog.append(("cc", kind, ins, outs, sem))
        self._update(ticket, reads, writes)
        return ticket

    def barrier(self, scratch_ap):
        P = self.E["pool"]
        deps = []
        for n in self.ENGS:
            E = self.E[n]
            if E.cnt > 0:
                deps.append((E.sem, E.cnt))
        for h in self.live:
            deps.append((h.sem, h.cnt))
        self._emit_waits(P, deps)
        P.cnt += 1
        P.prog.append(("op", lambda e: e.memset(scratch_ap, 0.0), True))
        t = (P.sem, P.cnt)
        for n in self.ENGS:
            if n != "pool":
                self._emit_waits(self.E[n], [t])
        for h in self.live:
            self.free_sems.append([h.sem, h.cnt])
            h.sem = None
        self.live = []

    def final_wait(self):
        self._emit_waits(self.E["pool"], list(self.store_tickets.values()))

    def emit(self):
        nc = self.nc
        S = self

        def run(eng_obj, E):
            for item in E.prog:
                k = item[0]
                if k == "wait":
                    eng_obj.wait_ge(item[1], item[2])
                elif k == "op":
                    ins = item[1](eng_obj)
                    if item[2]:
                        ins.then_inc(E.sem, 1)
                elif k == "dma":
                    eng_obj.dma_start(out=item[1], in_=item[2], **item[4]).then_inc(item[3], 16)
                elif k == "cc":
                    eng_obj.collective_compute(item[1], ALU.bypass, replica_groups=GROUPS,
                                               ins=item[2], outs=item[3]).then_inc(item[4], 1)

        with nc.Block() as block:
            @block.tensor
            def _(e):
                run(e, S.E["pe"])

            @block.scalar
            def _(e):
                run(e, S.E["act"])

            @block.vector
            def _(e):
                run(e, S.E["dve"])

            @block.gpsimd
            def _(e):
                run(e, S.E["pool"])

            @block.sync
            def _(e):
                run(e, S.E["sp"])


class Tile:
    __slots__ = ("t", "h")

    def __init__(self, t, name):
        self.t = t
        self.h = H(name)


DT_SIZE = {F32: 4, BF16: 2, I32: 4}


class Builder:
    def __init__(self, phases, fused):
        self.phases = phases
        self.fused = fused
        self.nc = bass.Bass("TRN2", target_bir_lowering=False)
        self.ext_in = {}
        self.ext_out = {}
        self.dram = {}
        self.DH = {}
        self.uid = 0

    def D(self, name, shape, dtype, writer):
        if name in self.dram:
            return self.dram[name]
        if writer == "host" or writer not in self.phases:
            t = self.nc.dram_tensor(name, list(shape), dtype, kind="ExternalInput")
            self.ext_in[name] = (tuple(shape), dtype)
        elif self.fused and name != "out":
            t = self.nc.dram_tensor(name, list(shape), dtype)
        else:
            t = self.nc.dram_tensor(name, list(shape), dtype, kind="ExternalOutput")
            self.ext_out[name] = (tuple(shape), dtype)
        self.dram[name] = t
        return t

    def dh(self, name, key=None):
        k = (name, key)
        if k not in self.DH:
            self.DH[k] = H("D_%s_%s" % (name, key))
        return self.DH[k]

    def sb_reset(self):
        self.sb_ptr = self.sb_base

    def sb(self, name, shape, dtype):
        per = 1
        for s in shape[1:]:
            per *= s
        nbytes = (per * DT_SIZE[dtype] + 63) // 64 * 64
        if self.sb_ptr + nbytes > self.sb_top:
            raise RuntimeError("SBUF overflow allocating %s (%d + %d > %d)" % (name, self.sb_ptr, nbytes, self.sb_top))
        self.uid += 1
        t = self.nc.alloc_sbuf_tensor_at("%s_%d" % (name, self.uid), list(shape), dtype, offset=self.sb_ptr)
        self.sb_ptr += nbytes
        return Tile(t, name)

    def ring(self, name, n, shape, dtype):
        return _Ring([self.sb("%s%d" % (name, i), shape, dtype) for i in range(n)])

    def build(self):
        nc = self.nc
        with ExitStack() as st:
            self.S = S = Sched(nc, st)
            self.sb_base = (nc.sbuf_base + 63) // 64 * 64
            self.sb_top = nc.sbuf_top
            self.sb_reset()
            self.ps = st.enter_context(nc.psum_tensor("ps", [128, 8, 512], F32))
            self.psh = [H("ps%d" % i) for i in range(8)]
            self.ps_rr = 0
            self.bar = self.sb("bar", [128, 8], F32)
            self.ones = self.sb("ones", [128, 128], BF16)
            self.zeros = self.sb("zeros", [128, 512], F32)
            self.pp = self.sb("pp", [128, NPP], F32)
            self.persist_ptr = None
            S.op("pool", lambda e: e.memset(self.ones.t[:], 1.0), writes=[self.ones.h])
            S.op("pool", lambda e: e.memset(self.zeros.t[:], 0.0), writes=[self.zeros.h])
            ppd = self.D("pp", [128, NPP], F32, "host")
            S.dma("sp", self.pp.t[:], ppd[:, :], writes=[self.pp.h], owner=self.pp.h)
            self.persist_ptr = self.sb_ptr
            for ph in self.phases:
                kind = ph[0]
                if not (self.fused and kind in ("X1", "A")):
                    self.sb_ptr = self.persist_ptr
                l = ph[1] if len(ph) > 1 else None
                getattr(self, "phase_" + kind)(*([l] if l is not None else []))
                S.barrier(self.bar.t[:, 0:1])
            S.final_wait()
            print("ops", S.n_ops, "waits", S.n_waits, {n: len(S.E[n].prog) for n in S.ENGS}, flush=True)
            S.emit()
        return nc

    def ps_next(self, banks=(0, 1, 2, 3, 4, 5, 6, 7)):
        b = banks[self.ps_rr % len(banks)]
        self.ps_rr += 1
        return b

    def mm_group(self, out_ap, bank, pairs, reads, **kw):
        S = self.S
        n = len(pairs)
        for i, (l, r) in enumerate(pairs):
            S.op("pe", lambda e, l=l, r=r, i=i: e.matmul(out_ap, l, r, start=(i == 0), stop=(i == n - 1), **kw),
                 reads=reads, writes=[self.psh[bank]], inc=(i == n - 1))

    def ppc(self, col, n=1, parts=128):
        return self.pp.t[0:parts, col:col + n]

    def load_modv(self, l):
        modd = self.D("modv", [128, 96], F32, ("M",))
        mv = self.sb("modv", [128, 96], F32)
        self.S.dma("sp", mv.t[:], modd[:, :], reads=[self.dh("modv")], writes=[mv.h], owner=mv.h)
        return mv

    def norm_block(self, xt, W, sc1_ap, sh_ap, hout, rs, sq_ring, tmp_ring, mvh=None):
        S = self.S
        bank = self.ps_next()
        sqs = []
        for c in range(NCH):
            sq = sq_ring.next()
            S.op("act", lambda e, sq=sq, c=c: e.activation(sq.t[:, 0:W], xt.t[:, c, 0:W], AF.Square),
                 reads=[xt.h], writes=[sq.h])
            S.op("pe", lambda e, sq=sq, c=c: e.matmul(self.ps[:, bank, 0:W], self.ones.t[:, :], sq.t[:, 0:W],
                                                       start=(c == 0), stop=(c == NCH - 1)),
                 reads=[sq.h, self.ones.h], writes=[self.psh[bank]], inc=True)
        S.op("act", lambda e: e.activation(rs.t[:, 0:W], self.ps[:, bank, 0:W], AF.Sqrt, bias=EPS, scale=1.0 / D),
             reads=[self.psh[bank]], writes=[rs.h])
        S.op("dve", lambda e: e.reciprocal(rs.t[:, 0:W], rs.t[:, 0:W]), reads=[rs.h], writes=[rs.h])
        for c in range(NCH):
            tmp = tmp_ring.next()
            S.op("dve", lambda e, tmp=tmp, c=c: e.tensor_tensor(tmp.t[:, 0:W], xt.t[:, c, 0:W], rs.t[:, 0:W], ALU.mult),
                 reads=[xt.h, rs.h], writes=[tmp.h])
            S.op("pool", lambda e, tmp=tmp, c=c: e.tensor_scalar(hout.t[:, c, 0:W], tmp.t[:, 0:W],
                                                                 sc1_ap[:, c:c + 1], sh_ap[:, c:c + 1], ALU.mult, ALU.add),
                 reads=[tmp.h, mvh], writes=[hout.h])

    def load_w(self, dst_ap, src_ap, tile):
        self.S.dma("pool", dst_ap, src_ap, writes=[tile.h], owner=tile.h)

    def phase_M(self):
        S = self.S
        w_ada = self.D("w_ada", [2, D, 6 * D], F32, "host")
        modd = self.D("modv", [128, 96], F32, ("M",))
        wr = self.ring("wada", 2, [128, 8, 1536], BF16)
        cbf = self.sb("cbf", [128, 8], BF16)
        mv = self.sb("mv", [128, 96], F32)
        S.op("dve", lambda e: e.tensor_copy(cbf.t[:], self.ppc(PP_C, 8)), reads=[self.pp.h], writes=[cbf.h])
        bank = 0
        for l in range(2):
            for g in range(4):
                wt = wr.next()
                src = w_ada[l].rearrange("(k p) n -> p k n", p=128)
                for k in range(8):
                    self.S.dma("pool", wt.t[:, k, :], src[:, k, g * 1536:(g + 1) * 1536], writes=[wt.h], owner=wt.h)
                for j in range(12):
                    J = g * 12 + j
                    self.mm_group(self.ps[:, bank, J:J + 1], bank,
                                  [(wt.t[:, k, j * 128:(j + 1) * 128], cbf.t[:, k:k + 1]) for k in range(8)],
                                  reads=[wt.h, cbf.h])
            S.op("dve", lambda e, l=l: e.tensor_tensor(mv.t[:, l * 48:(l + 1) * 48], self.ps[:, bank, 0:48],
                                                       self.ppc(l * PPL, 48), ALU.add),
                 reads=[self.psh[bank], self.pp.h], writes=[mv.h])
            for off in (8, 32):
                S.op("dve", lambda e, l=l, off=off: e.tensor_scalar(mv.t[:, l * 48 + off:l * 48 + off + 8],
                                                                    mv.t[:, l * 48 + off:l * 48 + off + 8],
                                                                    1.0, None, ALU.add),
                     reads=[mv.h], writes=[mv.h])
        S.dma("sp", modd[:, :], mv.t[:], reads=[mv.h], writes=[self.dh("modv")], owner=mv.h, is_store=True)

    def phase_R(self):
        S = self.S
        posd = self.D("pos", [1, NT], I32, "host")
        ropd = self.D("rope", [4, 64, NT], F32, ("R",))
        posi = self.ring("posi", 2, [64, TB], I32)
        posf = self.ring("posf", 2, [64, TB], F32)
        ang = self.ring("ang", 2, [64, TB], F32)
        tq = self.ring("tq", 2, [64, TB], F32)
        ti = self.ring("ti", 2, [64, TB], I32)
        out = self.ring("rout", 4, [64, 2, TB], F32)
        invf = self.ppc(PP_INVF, 1, 64)
        invf2 = self.ppc(PP_INVF + 1, 1, 64)
        for i in range(NB):
            pi = posi.next()
            pf = posf.next()
            S.dma("sp", pi.t[:], posd[0:1, i * TB:(i + 1) * TB].partition_broadcast(64), writes=[pi.h], owner=pi.h)
            S.op("dve", lambda e, pi=pi, pf=pf: e.tensor_copy(pf.t[:], pi.t[:]), reads=[pi.h], writes=[pf.h])
            for which, (aoff, toff) in enumerate(((0.0, 0.0), (np.pi / 2, 0.25))):
                a = ang.next()
                t = tq.next()
                tii = ti.next()
                o = out.next()
                S.op("dve", lambda e, a=a, pf=pf, aoff=aoff: e.tensor_scalar(a.t[:], pf.t[:], invf, aoff, ALU.mult, ALU.add),
                     reads=[pf.h, self.pp.h], writes=[a.h])
                S.op("dve", lambda e, t=t, pf=pf, toff=toff: e.tensor_scalar(t.t[:], pf.t[:], invf2, toff, ALU.mult, ALU.add),
                     reads=[pf.h, self.pp.h], writes=[t.h])
                S.op("dve", lambda e, t=t, tii=tii: e.tensor_copy(tii.t[:], t.t[:]), reads=[t.h], writes=[tii.h])
                S.op("dve", lambda e, t=t, tii=tii: e.tensor_copy(t.t[:], tii.t[:]), reads=[tii.h], writes=[t.h])
                S.op("dve", lambda e, t=t, a=a: e.scalar_tensor_tensor(a.t[:], t.t[:], -2 * np.pi, a.t[:], ALU.mult, ALU.add),
                     reads=[t.h, a.h], writes=[a.h])
                S.op("act", lambda e, o=o, a=a: e.activation(o.t[:, 0, :], a.t[:], AF.Sin), reads=[a.h], writes=[o.h])
                S.op("dve", lambda e, o=o: e.tensor_scalar(o.t[:, 1, :], o.t[:, 0, :], QSCALE, None, ALU.mult),
                     reads=[o.h], writes=[o.h])
                tidx = 1 if which == 0 else 0
                S.dma("sp", ropd[tidx, :, i * TB:(i + 1) * TB], o.t[:, 0, :], reads=[o.h],
                      awrites=[self.dh("rope")], owner=o.h, is_store=True)
                S.dma("sp", ropd[tidx + 2, :, i * TB:(i + 1) * TB], o.t[:, 1, :], reads=[o.h],
                      awrites=[self.dh("rope")], owner=o.h, is_store=True)

    def get_w_in(self, l, lru_only):
        key = ("w_in", l)
        if getattr(self, "_w_in_key", None) == key:
            return self._w_in
        w_in = self.D("w_in", [2, D, DIN], F32, "host")
        ncols = D if lru_only else DAUG
        wt = self.sb("w_in_sb", [128, 8, ncols], BF16)
        src = w_in[l].rearrange("(k p) n -> p k n", p=128)
        for k in range(8):
            if lru_only:
                self.load_w(wt.t[:, k, 0:D], src[:, k, 0:D], wt)
            else:
                self.load_w(wt.t[:, k, 0:1472], src[:, k, 0:1472], wt)
                self.load_w(wt.t[:, k, 1472:1504], src[:, k, 1440:1472], wt)
                self.load_w(wt.t[:, k, 1504:1536], src[:, k, 1408:1440], wt)
                self.load_w(wt.t[:, k, 1536:DAUG], src[:, k, 1472:DIN], wt)
        if not lru_only:
            self.S.op("dve", lambda e: e.tensor_scalar(wt.t[:, :, 1472:1504], wt.t[:, :, 1472:1504], -1.0, None, ALU.mult),
                      reads=[wt.h], writes=[wt.h])
        self._w_in_key = key
        self._w_in = wt
        return wt

    def phase_P(self, l):
        S = self.S
        xname, xw = ("xT", "host") if l == 0 else ("x2_0", ("C2", 0))
        xd = self.D(xname, [NCH, 128, NT], F32, xw)
        shd = self.D("send_halo_%d" % l, [128, 192], F32, ("P", l))
        wt = self.get_w_in(l, lru_only=not self.fused)
        mv = self.load_modv(l)
        xh = self.sb("xh", [128, 8, 24], F32)
        hh = self.sb("hh", [128, 8, 24], BF16)
        rs = self.sb("rsP", [128, 24], F32)
        sq_ring = self.ring("sqP", 2, [128, 24], BF16)
        tmp_ring = self.ring("tmpP", 2, [128, 24], F32)
        xlh = self.sb("xlh", [128, 8, 24], F32)
        for c in range(NCH):
            src = xd[c].rearrange("p (i t) -> p i t", t=TB)[:, :, TB - 3:TB]
            rd = [self.dh(xname, (i,)) for i in range(NB)]
            S.dma("sp", xh.t[:, c, :].rearrange("p (i t) -> p i t", t=3), src, reads=rd, writes=[xh.h], owner=xh.h)
        self.norm_block(xh, 24, mv.t[:, l * 48 + 8:l * 48 + 16], mv.t[:, l * 48 + 0:l * 48 + 8], hh, rs, sq_ring, tmp_ring, mvh=mv.h)
        for m in range(NCH):
            bank = self.ps_next()
            self.mm_group(self.ps[:, bank, 0:24], bank,
                          [(wt.t[:, k, m * 128:(m + 1) * 128], hh.t[:, k, :]) for k in range(8)],
                          reads=[wt.h, hh.h])
            S.op("act", lambda e, m=m, bank=bank: e.activation(xlh.t[:, m, :], self.ps[:, bank, 0:24], AF.Copy),
                 reads=[self.psh[bank]], writes=[xlh.h])
        S.dma("sp", shd[:, :], xlh.t[:].rearrange("p m t -> p (m t)"), reads=[xlh.h], writes=[self.dh("send_halo_%d" % l)],
              owner=xlh.h, is_store=True)

    def phase_X1(self, l):
        shd = self.D("send_halo_%d" % l, [128, 192], F32, ("P", l))
        gd = self.D("G_halo_%d" % l, [256, 192], F32, ("X1", l))
        o = H("cc1")
        self.S.collective("AllGather", [shd.ap().opt()], [gd.ap().opt()],
                          reads=[self.dh("send_halo_%d" % l)], writes=[self.dh("G_halo_%d" % l)], owner=o)

    def phase_X2(self, l):
        for nm, sshape, gshape, dt in (("kv", [192, NT], [384, NT], BF16), ("sum", [128, 128], [256, 128], F32)):
            sd = self.D("send_%s_%d" % (nm, l), sshape, dt, ("A", l))
            gd = self.D("G_%s_%d" % (nm, l), gshape, dt, ("X2", l))
            o = H("cc2" + nm)
            self.S.collective("AllGather", [sd.ap().opt()], [gd.ap().opt()],
                              reads=[self.dh("send_%s_%d" % (nm, l))], writes=[self.dh("G_%s_%d" % (nm, l))], owner=o)

    def phase_A(self, l):
        S = self.S
        xname, xw = ("xT", "host") if l == 0 else ("x2_0", ("C2", 0))
        xd = self.D(xname, [NCH, 128, NT], F32, xw)
        posd = self.D("pos", [1, NT], I32, "host")
        ropd = self.D("rope", [4, 64, NT], F32, ("R",))
        ghd = self.D("G_halo_%d" % l, [256, 192], F32, ("X1", l))
        gayd = self.D("ga_y_%d" % l, [NCH, 128, NT], BF16, ("A", l))
        gaAd = self.D("ga_A_%d" % l, [NCH, 128, NT], BF16, ("A", l))
        tgbd = self.D("tgb_%d" % l, [NCH, 128, NT], BF16, ("A", l))
        cqd = self.D("cq_%d" % l, [2, 128, NT], BF16, ("A", l))
        skvd = self.D("send_kv_%d" % l, [192, NT], BF16, ("A", l))
        ssumd = self.D("send_sum_%d" % l, [128, 128], F32, ("A", l))
        lwa = self.D("lru_wa", [2, 8, 128, 128], F32, "host")
        lwx = self.D("lru_wx", [2, 8, 128, 128], F32, "host")
        wt = self.get_w_in(l, lru_only=False)
        base = l * PPL
        wa = self.sb("wa", [128, 8, 128], BF16)
        wx = self.sb("wx", [128, 8, 128], BF16)
        self.load_w(wa.t[:], lwa[l].rearrange("n i j -> i n j"), wa)
        self.load_w(wx.t[:], lwx[l].rearrange("n i j -> i n j"), wx)
        mv = self.load_modv(l)
        hb = self.sb("hb", [128, 16], F32)
        c05 = self.sb("c05", [128, 8], F32)
        S.op("dve", lambda e: e.tensor_scalar(hb.t[:], self.ppc(base + 88, 16), 0.5, None, ALU.mult),
             reads=[self.pp.h], writes=[hb.h])
        S.op("act", lambda e: e.activation(c05.t[:], self.ppc(base + 104, 8), AF.Exp, scale=-1.0),
             reads=[self.pp.h], writes=[c05.h])
        S.op("act", lambda e: e.activation(c05.t[:], c05.t[:], AF.Ln, bias=1.0, scale=1.0), reads=[c05.h], writes=[c05.h])
        S.op("dve", lambda e: e.tensor_scalar(c05.t[:], c05.t[:], -4.0, None, ALU.mult), reads=[c05.h], writes=[c05.h])
        gh = self.sb("gh", [128, 2, 8, 8, 3], F32)
        halo = self.sb("halo", [128, 8, 8, 3], F32)
        for s in range(2):
            S.dma("sp", gh.t[:, s].rearrange("p m i t -> p (m i t)"), ghd[s * 128:(s + 1) * 128, :],
                  reads=[self.dh("G_halo_%d" % l)], writes=[gh.h], owner=gh.h)
        f0 = self.ppc(PP_FLAG, 1)
        f1 = self.ppc(PP_FLAG + 1, 1)
        S.op("dve", lambda e: e.memset(halo.t[:], 0.0), writes=[halo.h])
        S.op("dve", lambda e: e.tensor_scalar(halo.t[:, :, 1:8, :], gh.t[:, 1, :, 0:7, :], f0, None, ALU.mult),
             reads=[gh.h, self.pp.h], writes=[halo.h])
        S.op("dve", lambda e: e.scalar_tensor_tensor(halo.t[:], gh.t[:, 0], f1, halo.t[:], ALU.mult, ALU.add),
             reads=[gh.h, halo.h, self.pp.h], writes=[halo.h])
        summ = self.sb("summ", [128, 2, 8, 8], F32)
        xt = self.sb("xtA", [128, 8, TB], F32)
        hT = self.sb("hT", [128, 8, TB], BF16)
        rs = self.sb("rsA", [128, TB], F32)
        sq_ring = self.ring("sqA", 2, [128, TB], BF16)
        tmp_ring = self.ring("tmpA", 2, [128, TB], F32)
        posi = self.sb("posiA", [128, TB], I32)
        posf = self.sb("posfA", [128, TB], F32)
        mb2 = self.sb("mb2", [128, TB], F32)
        rope = self.sb("ropeA", [64, 2, TB], F32)
        ui = self.sb("ui", [128, 8, TB], F32)
        aT = self.sb("aT", [128, 8, TB], F32)
        tga = self.sb("tga", [128, 8, TB], BF16)
        xl_ring = self.ring("xl", 1, [128, TB + 3], F32)
        u_ring = self.ring("u", 2, [128, TB], F32)
        ubf_ring = self.ring("ubf", 1, [128, TB], BF16)
        tr_ring = self.ring("tr", 1, [128, TB], F32)
        tiv_ring = self.ring("tiv", 1, [128, TB], F32)
        m4_ring = self.ring("m4", 1, [128, TB], F32)
        b_ring = self.ring("bb", 1, [128, TB], F32)
        h0_ring = self.ring("h0", 1, [128, TB], F32)
        A_ring = self.ring("AA", 1, [128, TB], F32)
        ob_ring = self.ring("ob", 4, [128, TB], BF16)
        qd = self.sb("qd", [128, 2, TB], F32)
        kvd = self.sb("kvd", [128, TB], F32)
        rq = rs
        rkv = self.sb("rkv", [128, TB], F32)
        kp_ring = tmp_ring
        print("phase A sbuf used", self.sb_ptr - self.sb_base, "of", self.sb_top - self.sb_base, flush=True)
        cw = lambda k, c: self.ppc(base + 48 + k * 8 + c, 1)
        cb = lambda c: self.ppc(base + 80 + c, 1)
        for i in range(NB):
            sl = slice(i * TB, (i + 1) * TB)
            S.dma("sp", xt.t[:], xd.ap().rearrange("c p t -> p c t")[:, :, sl], reads=[self.dh(xname, (i,))],
                  writes=[xt.h], owner=xt.h)
            S.dma("sp", posi.t[:], posd[0:1, sl].partition_broadcast(128), writes=[posi.h], owner=posi.h)
            S.dma("sp", rope.t[:], ropd.ap().rearrange("f p t -> p f t")[:, 0:2, sl], reads=[self.dh("rope")],
                  writes=[rope.h], owner=rope.h)
            S.op("dve", lambda e: e.tensor_copy(posf.t[:], posi.t[:]), reads=[posi.h], writes=[posf.h])
            S.op("dve", lambda e: e.tensor_scalar(mb2.t[:], posf.t[:], 0.0, 2e6, ALU.is_equal, ALU.mult),
                 reads=[posf.h], writes=[mb2.h])
            self.norm_block(xt, TB, mv.t[:, l * 48 + 8:l * 48 + 16], mv.t[:, l * 48 + 0:l * 48 + 8], hT, rs, sq_ring, tmp_ring, mvh=mv.h)

            def proj(m0, msz, bank, poff=0):
                self.mm_group(self.ps[poff:poff + msz, bank, :], bank,
                              [(wt.t[:, k, m0:m0 + msz], hT.t[:, k, :]) for k in range(8)], reads=[wt.h, hT.h])

            banks = (0, 1, 2, 3, 4, 5)
            for c in range(NCH):
                xl = xl_ring.next()
                u = u_ring.next()
                ubf = ubf_ring.next()
                tr = tr_ring.next()
                tiv = tiv_ring.next()
                b0 = self.ps_next(banks)
                proj(c * 128, 128, b0)
                S.op("act", lambda e, xl=xl, b0=b0: e.activation(xl.t[:, 3:TB + 3], self.ps[:, b0, :], AF.Copy),
                     reads=[self.psh[b0]], writes=[xl.h])
                S.op("pool", lambda e, xl=xl, c=c, i=i: e.tensor_copy(xl.t[:, 0:3], halo.t[:, c, i, :]),
                     reads=[halo.h, xl.h], writes=[xl.h])
                S.op("pool", lambda e, xl=xl, u=u, c=c: e.tensor_scalar(u.t[:], xl.t[:, 0:TB], cw(0, c), cb(c), ALU.mult, ALU.add),
                     reads=[xl.h, self.pp.h], writes=[u.h])
                for k in range(1, 4):
                    S.op("dve", lambda e, xl=xl, u=u, c=c, k=k: e.scalar_tensor_tensor(u.t[:], xl.t[:, k:k + TB], cw(k, c), u.t[:],
                                                                                        ALU.mult, ALU.add),
                         reads=[xl.h, u.h, self.pp.h], writes=[u.h])
                S.op("act", lambda e, u=u, ubf=ubf: e.activation(ubf.t[:], u.t[:], AF.Copy), reads=[u.h], writes=[ubf.h])
                b1 = self.ps_next(banks)
                b2 = self.ps_next(banks)
                self.mm_group(self.ps[:, b1, :], b1, [(wa.t[:, c, :], ubf.t[:])], reads=[wa.h, ubf.h])
                self.mm_group(self.ps[:, b2, :], b2, [(wx.t[:, c, :], ubf.t[:])], reads=[wx.h, ubf.h])
                S.op("act", lambda e, tr=tr, b1=b1, c=c: e.activation(tr.t[:], self.ps[:, b1, :], AF.Tanh, bias=hb.t[:, c:c + 1], scale=0.5),
                     reads=[self.psh[b1], hb.h], writes=[tr.h])
                S.op("act", lambda e, tiv=tiv, b2=b2, c=c: e.activation(tiv.t[:], self.ps[:, b2, :], AF.Tanh, bias=hb.t[:, 8 + c:9 + c], scale=0.5),
                     reads=[self.psh[b2], hb.h], writes=[tiv.h])
                S.op("pool", lambda e, tr=tr: e.tensor_tensor(tr.t[:], tr.t[:], mb2.t[:], ALU.add),
                     reads=[tr.h, mb2.h], writes=[tr.h])
                S.op("act", lambda e, tr=tr, c=c: e.activation(aT.t[:, c, :], tr.t[:], AF.Exp, bias=c05.t[:, c:c + 1], scale=c05.t[:, c:c + 1]),
                     reads=[tr.h, c05.h], writes=[aT.h])
                S.op("dve", lambda e, tiv=tiv, u=u, c=c: e.scalar_tensor_tensor(ui.t[:, c, :], tiv.t[:], 1.0, u.t[:], ALU.add, ALU.mult),
                     reads=[tiv.h, u.h], writes=[ui.h])
                b3 = self.ps_next(banks)
                proj(1536 + c * 128, 128, b3)
                S.op("act", lambda e, b3=b3, c=c: e.activation(tga.t[:, c, :], self.ps[:, b3, :], AF.Tanh, scale=0.5),
                     reads=[self.psh[b3]], writes=[tga.h])
                b4 = self.ps_next(banks)
                proj(2560 + c * 128, 128, b4)
                ob = ob_ring.next()
                S.op("act", lambda e, b4=b4, ob=ob: e.activation(ob.t[:], self.ps[:, b4, :], AF.Tanh, scale=0.5),
                     reads=[self.psh[b4]], writes=[ob.h])
                S.dma("sp", tgbd[c, :, sl], ob.t[:], reads=[ob.h], writes=[self.dh("tgb_%d" % l, (c, i))], owner=ob.h, is_store=True)
            for k2 in range(2):
                b = self.ps_next(banks)
                proj(1024 + k2 * 128, 128, b)
                S.op("act", lambda e, b=b, k2=k2: e.activation(qd.t[:, k2, :], self.ps[:, b, :], AF.Copy),
                     reads=[self.psh[b]], writes=[qd.h])
            b = self.ps_next(banks)
            proj(1280, 128, b)
            S.op("act", lambda e, b=b: e.activation(kvd.t[:], self.ps[:, b, :], AF.Copy), reads=[self.psh[b]], writes=[kvd.h])
            bA = self.ps_next(banks)
            bB = self.ps_next(banks)
            proj(1408, 64, bA)
            proj(1472, 64, bB)
            kp1 = kp_ring.next()
            kp2 = kp_ring.next()
            ob = ob_ring.next()
            S.op("dve", lambda e, kp1=kp1, bA=bA: e.tensor_tensor(kp1.t[0:64, :], self.ps[0:64, bA, :], rope.t[:, 0, :], ALU.mult),
                 reads=[self.psh[bA], rope.h], writes=[kp1.h])
            S.op("dve", lambda e, kp2=kp2, bB=bB: e.tensor_tensor(kp2.t[0:64, :], self.ps[0:64, bB, :], rope.t[:, 1, :], ALU.mult),
                 reads=[self.psh[bB], rope.h], writes=[kp2.h])
            S.op("dve", lambda e, kp1=kp1, kp2=kp2, ob=ob: e.tensor_tensor(ob.t[0:64, :], kp1.t[0:64, :], kp2.t[0:64, :], ALU.add),
                 reads=[kp1.h, kp2.h], writes=[ob.h])
            S.dma("sp", skvd[128:192, sl], ob.t[0:64, :], reads=[ob.h], awrites=[self.dh("send_kv_%d" % l)], owner=ob.h, is_store=True)
            bq = self.ps_next(banks)
            for k2 in range(2):
                sq = sq_ring.next()
                S.op("act", lambda e, sq=sq, k2=k2: e.activation(sq.t[:], qd.t[:, k2, :], AF.Square), reads=[qd.h], writes=[sq.h])
                S.op("pe", lambda e, sq=sq, k2=k2, bq=bq: e.matmul(self.ps[:, bq, :], self.ones.t[:, :], sq.t[:], start=(k2 == 0), stop=(k2 == 1)),
                     reads=[sq.h, self.ones.h], writes=[self.psh[bq]])
            bk = self.ps_next(banks)
            sq = sq_ring.next()
            S.op("act", lambda e, sq=sq: e.activation(sq.t[:], kvd.t[:], AF.Square), reads=[kvd.h], writes=[sq.h])
            S.op("pe", lambda e, sq=sq, bk=bk: e.matmul(self.ps[:, bk, :], self.ones.t[:, :], sq.t[:], start=True, stop=True),
                 reads=[sq.h, self.ones.h], writes=[self.psh[bk]])
            S.op("act", lambda e, bq=bq: e.activation(rq.t[:], self.ps[:, bq, :], AF.Sqrt, bias=EPS, scale=1.0 / 256), reads=[self.psh[bq]], writes=[rq.h])
            S.op("act", lambda e, bk=bk: e.activation(rkv.t[:], self.ps[:, bk, :], AF.Sqrt, bias=EPS, scale=1.0 / 128), reads=[self.psh[bk]], writes=[rkv.h])
            m4s = []
            S.op("dve", lambda e: e.reciprocal(rq.t[:], rq.t[:]), reads=[rq.h], writes=[rq.h])
            S.op("dve", lambda e: e.reciprocal(rkv.t[:], rkv.t[:]), reads=[rkv.h], writes=[rkv.h])
            for k2 in range(2):
                tmp = tmp_ring.next()
                ob = ob_ring.next()
                S.op("dve", lambda e, tmp=tmp, k2=k2: e.tensor_tensor(tmp.t[:], qd.t[:, k2, :], rq.t[:], ALU.mult), reads=[qd.h, rq.h], writes=[tmp.h])
                S.op("dve", lambda e, tmp=tmp, ob=ob, k2=k2: e.tensor_scalar(ob.t[:], tmp.t[:], self.ppc(base + 112 + k2, 1), None, ALU.mult),
                     reads=[tmp.h, self.pp.h], writes=[ob.h])
                S.dma("sp", cqd[k2, :, sl], ob.t[:], reads=[ob.h], writes=[self.dh("cq_%d" % l, (k2, i))], owner=ob.h, is_store=True)
            tmp = tmp_ring.next()
            ob = ob_ring.next()
            S.op("dve", lambda e, tmp=tmp: e.tensor_tensor(tmp.t[:], kvd.t[:], rkv.t[:], ALU.mult), reads=[kvd.h, rkv.h], writes=[tmp.h])
            S.op("dve", lambda e, tmp=tmp, ob=ob: e.tensor_scalar(ob.t[:], tmp.t[:], self.ppc(base + 114, 1), None, ALU.mult),
                 reads=[tmp.h, self.pp.h], writes=[ob.h])
            S.dma("sp", skvd[0:128, sl], ob.t[:], reads=[ob.h], awrites=[self.dh("send_kv_%d" % l)], owner=ob.h, is_store=True)
            for c in range(NCH):
                m4 = m4_ring.next()
                bt = b_ring.next()
                h0 = h0_ring.next()
                AA = A_ring.next()
                S.op("pool", lambda e, m4=m4, c=c: e.tensor_tensor(m4.t[:], aT.t[:, c, :], aT.t[:, c, :], ALU.mult), reads=[aT.h], writes=[m4.h])
                S.op("act", lambda e, m4=m4: e.activation(m4.t[:], m4.t[:], AF.Sqrt, bias=1.0 / 16, scale=-1.0 / 16), reads=[m4.h], writes=[m4.h])
                S.op("dve", lambda e, bt=bt, m4=m4, c=c: e.tensor_tensor(bt.t[:], ui.t[:, c, :], m4.t[:], ALU.mult), reads=[ui.h, m4.h], writes=[bt.h])
                S.op("dve", lambda e, bt=bt, h0=h0, c=c: e.tensor_tensor_scan(h0.t[:], aT.t[:, c, :], bt.t[:], 0.0, ALU.mult, ALU.add),
                     reads=[aT.h, bt.h], writes=[h0.h])
                S.op("dve", lambda e, AA=AA, c=c: e.tensor_tensor_scan(AA.t[:], aT.t[:, c, :], self.zeros.t[:], 1.0, ALU.mult, ALU.add),
                     reads=[aT.h, self.zeros.h], writes=[AA.h])
                S.op("pool", lambda e, h0=h0, c=c, i=i: e.tensor_copy(summ.t[:, 0, c, i:i + 1], h0.t[:, TB - 1:TB]), reads=[h0.h], writes=[summ.h])
                S.op("pool", lambda e, AA=AA, c=c, i=i: e.tensor_copy(summ.t[:, 1, c, i:i + 1], AA.t[:, TB - 1:TB]), reads=[AA.h, summ.h], writes=[summ.h])
                ob1 = ob_ring.next()
                ob2 = ob_ring.next()
                S.op("dve", lambda e, ob1=ob1, h0=h0, c=c: e.scalar_tensor_tensor(ob1.t[:], tga.t[:, c, :], 1.0, h0.t[:], ALU.add, ALU.mult),
                     reads=[tga.h, h0.h], writes=[ob1.h])
                S.op("dve", lambda e, ob2=ob2, AA=AA, c=c: e.scalar_tensor_tensor(ob2.t[:], tga.t[:, c, :], 1.0, AA.t[:], ALU.add, ALU.mult),
                     reads=[tga.h, AA.h], writes=[ob2.h])
                S.dma("sp", gayd[c, :, sl], ob1.t[:], reads=[ob1.h], writes=[self.dh("ga_y_%d" % l, (c, i))], owner=ob1.h, is_store=True)
                S.dma("sp", gaAd[c, :, sl], ob2.t[:], reads=[ob2.h], writes=[self.dh("ga_A_%d" % l, (c, i))], owner=ob2.h, is_store=True)
        S.dma("sp", ssumd[:, :], summ.t[:].rearrange("p a c i -> p (a c i)"), reads=[summ.h], writes=[self.dh("send_sum_%d" % l)],
              owner=summ.h, is_store=True)
        self._w_in_key = None

    def phase_B(self, l):
        S = self.S
        gkvd = self.D("G_kv_%d" % l, [384, NT], BF16, ("X2", l))
        cqd = self.D("cq_%d" % l, [2, 128, NT], BF16, ("A", l))
        tgbd = self.D("tgb_%d" % l, [NCH, 128, NT], BF16, ("A", l))
        gbyd = self.D("gb_y_%d" % l, [NCH, 128, NT], BF16, ("B", l))
        ropd = self.D("rope", [4, 64, NT], F32, ("R",))
        maskd = self.D("masks", [128, 8 * TB], F32, "host")
        wuqd = self.D("w_uq", [2, 256, HEADS, 192], F32, "host")
        wukvd = self.D("w_ukv", [2, 128, HEADS, 256], F32, "host")
        ckvT = self.sb("ckvT", [128, 2 * NT], BF16)
        kpeT = self.sb("kpeT", [64, 2 * NT], BF16)
        cqT = self.sb("cqT", [128, 2, NT], BF16)
        wuq = self.sb("wuq", [128, 2, HEADS, 256], BF16)
        wukv = self.sb("wukv", [128, HEADS, 256], BF16)
        maskf = self.sb("maskf", [128, TB], F32)
        masks = self.sb("masks", [128, 8, TB], BF16)
        ident = self.sb("ident", [128, 128], F32)
        KnT = self.sb("KnT", [128, 2 * NT], BF16)
        Vaug = self.sb("Vaug", [128, 64, 132], BF16)
        qnT = self.sb("qnT", [128, NT], BF16)
        qpeT = self.sb("qpeT", [64, NT], BF16)
        rope_ring = self.ring("ropeB", 2, [64, 2, TB], F32)
        pT_ring = self.ring("pT", 4, [128, TB], BF16)
        t1_ring = self.ring("t1", 2, [64, TB], F32)
        t2_ring = self.ring("t2", 2, [64, TB], F32)
        rc_ring = self.ring("rc", 8, [128, 1], F32)
        o_ring = self.ring("osb", 8, [128, 128], F32)
        tgb_ring = self.ring("tgbB", 2, [128, TB], BF16)
        gby_ring = self.ring("gby", 2, [128, TB], BF16)
        print("phase B sbuf used", self.sb_ptr - self.sb_base, "of", self.sb_top - self.sb_base, flush=True)
        gkv = self.dh("G_kv_%d" % l)
        for s in range(2):
            S.dma("sp", ckvT.t[:, s * NT:(s + 1) * NT], gkvd[s * 192:s * 192 + 128, :], reads=[gkv], writes=[ckvT.h], owner=ckvT.h)
            S.dma("sp", kpeT.t[:, s * NT:(s + 1) * NT], gkvd[s * 192 + 128:s * 192 + 192, :], reads=[gkv], writes=[kpeT.h], owner=kpeT.h)
        for k2 in range(2):
            S.dma("sp", cqT.t[:, k2, :], cqd[k2, :, :], reads=[self.dh("cq_%d" % l, (k2, i)) for i in range(NB)], writes=[cqT.h], owner=cqT.h)
        src = wuqd[l].rearrange("(k p) h d -> p k h d", p=128)
        for k2 in range(2):
            self.load_w(wuq.t[:, k2, :, 0:192], src[:, k2, :, :], wuq)
            self.load_w(wuq.t[:, k2, :, 192:224], src[:, k2, :, 160:192], wuq)
            self.load_w(wuq.t[:, k2, :, 224:256], src[:, k2, :, 128:160], wuq)
        S.op("dve", lambda e: e.tensor_scalar(wuq.t[:, :, :, 192:224], wuq.t[:, :, :, 192:224], -1.0, None, ALU.mult),
             reads=[wuq.h], writes=[wuq.h])
        self.load_w(wukv.t[:], wukvd[l], wukv)
        for j in range(8):
            S.dma("sp", maskf.t[:], maskd[:, j * TB:(j + 1) * TB], writes=[maskf.h], owner=maskf.h)
            S.op("dve", lambda e, j=j: e.tensor_copy(masks.t[:, j, :], maskf.t[:]), reads=[maskf.h], writes=[masks.h])
        S.op("pool", lambda e: e.memset(ident.t[:], 0.0), writes=[ident.h])
        S.op("pool", lambda e: e.affine_select(out=ident.t[:], in_=ident.t[:], pattern=[[-1, 128]], compare_op=ALU.not_equal,
                                               fill=1.0, base=0, channel_multiplier=1), reads=[ident.h], writes=[ident.h])
        S.op("pool", lambda e: e.memset(Vaug.t[:, :, 128:132], 1.0), writes=[Vaug.h])
        sbanks = (0, 1, 2)
        for h in range(HEADS):
            for t in range(16):
                b = self.ps_next(sbanks)
                self.mm_group(self.ps[:, b, :], b, [(wukv.t[:, h, 0:128], ckvT.t[:, t * TB:(t + 1) * TB])], reads=[wukv.h, ckvT.h])
                eng = "act" if t % 2 == 0 else "dve"
                if eng == "act":
                    S.op("act", lambda e, b=b, t=t: e.activation(KnT.t[:, t * TB:(t + 1) * TB], self.ps[:, b, :], AF.Copy),
                         reads=[self.psh[b]], writes=[KnT.h])
                else:
                    S.op("dve", lambda e, b=b, t=t: e.tensor_copy(KnT.t[:, t * TB:(t + 1) * TB], self.ps[:, b, :]),
                         reads=[self.psh[b]], writes=[KnT.h])
            for t4 in range(16):
                b = self.ps_next(sbanks)
                for u4 in range(4):
                    t = t4 * 4 + u4
                    S.op("pe", lambda e, b=b, t=t, u4=u4, h=h: e.matmul(self.ps[:, b, u4 * 128:(u4 + 1) * 128], ckvT.t[:, t * 128:(t + 1) * 128],
                                                                   wukv.t[:, h, 128:256], start=True, stop=True),
                         reads=[wukv.h, ckvT.h], writes=[self.psh[b]], inc=(u4 == 3))
                if t4 % 2 == 0:
                    S.op("dve", lambda e, b=b, t4=t4: e.tensor_copy(Vaug.t[:, t4 * 4:t4 * 4 + 4, 0:128],
                                                                     self.ps[:, b, :].rearrange("p (u d) -> p u d", d=128)),
                         reads=[self.psh[b]], writes=[Vaug.h])
                else:
                    S.op("act", lambda e, b=b, t4=t4: e.activation(Vaug.t[:, t4 * 4:t4 * 4 + 4, 0:128],
                                                                    self.ps[:, b, :].rearrange("p (u d) -> p u d", d=128), AF.Copy),
                         reads=[self.psh[b]], writes=[Vaug.h])
            for i in range(NB):
                sl = slice(i * TB, (i + 1) * TB)
                rp = rope_ring.next()
                S.dma("sp", rp.t[:], ropd.ap().rearrange("f p t -> p f t")[:, 2:4, sl], reads=[self.dh("rope")], writes=[rp.h], owner=rp.h)
                b = self.ps_next(sbanks)
                self.mm_group(self.ps[:, b, :], b, [(wuq.t[:, k2, h, 0:128], cqT.t[:, k2, sl]) for k2 in range(2)], reads=[wuq.h, cqT.h])
                S.op("act", lambda e, b=b, sl=sl: e.activation(qnT.t[:, sl], self.ps[:, b, :], AF.Identity, scale=QSCALE),
                     reads=[self.psh[b]], writes=[qnT.h])
                bA = self.ps_next(sbanks)
                self.mm_group(self.ps[0:64, bA, :], bA, [(wuq.t[:, k2, h, 128:192], cqT.t[:, k2, sl]) for k2 in range(2)], reads=[wuq.h, cqT.h])
                bB = self.ps_next(sbanks)
                self.mm_group(self.ps[0:64, bB, :], bB, [(wuq.t[:, k2, h, 192:256], cqT.t[:, k2, sl]) for k2 in range(2)], reads=[wuq.h, cqT.h])
                t1 = t1_ring.next()
                t2 = t2_ring.next()
                S.op("dve", lambda e, t1=t1, bA=bA, rp=rp: e.tensor_tensor(t1.t[:], self.ps[0:64, bA, :], rp.t[:, 0, :], ALU.mult),
                     reads=[self.psh[bA], rp.h], writes=[t1.h])
                S.op("dve", lambda e, t2=t2, bB=bB, rp=rp: e.tensor_tensor(t2.t[:], self.ps[0:64, bB, :], rp.t[:, 1, :], ALU.mult),
                     reads=[self.psh[bB], rp.h], writes=[t2.h])
                S.op("dve", lambda e, t1=t1, t2=t2, sl=sl: e.tensor_tensor(qpeT.t[:, sl], t1.t[:], t2.t[:], ALU.add),
                     reads=[t1.h, t2.h], writes=[qpeT.h])
            tiles = []
            for i in range(NB):
                chunks = []
                for jp in range(i + 1):
                    for sp_ in range(2):
                        for cc in range(4):
                            chunks.append((sp_ * 32 + jp * 4 + cc, (sp_ * 4 + cc) if jp == i else None))
                for ci, (kc, mk) in enumerate(chunks):
                    tiles.append((i, ci, len(chunks), kc, mk))
            LA = 2

            def emit_qk(t):
                i, ci, nck, kc, mk = tiles[t]
                b = sbanks[t % 3]
                sl = slice(i * TB, (i + 1) * TB)
                ksl = slice(kc * 128, (kc + 1) * 128)
                self.mm_group(self.ps[:, b, :], b, [(KnT.t[:, ksl], qnT.t[:, sl]), (kpeT.t[:, ksl], qpeT.t[:, sl])],
                              reads=[KnT.h, qnT.h, kpeT.h, qpeT.h])

            def epilogue(i):
                sl = slice(i * TB, (i + 1) * TB)
                tg = tgb_ring.next()
                S.dma("sp", tg.t[:], tgbd[h, :, sl], reads=[self.dh("tgb_%d" % l, (h, i))], writes=[tg.h], owner=tg.h)
                for qs in range(4):
                    rc = rc_ring.next()
                    osb = o_ring.next()
                    S.op("dve", lambda e, rc=rc, qs=qs: e.reciprocal(rc.t[:], self.ps[:, 3 + qs, 128:129]), reads=[self.psh[3 + qs]], writes=[rc.h])
                    S.op("dve", lambda e, rc=rc, osb=osb, qs=qs: e.tensor_scalar(osb.t[:], self.ps[:, 3 + qs, 0:128], rc.t[:, 0:1], 0.5, ALU.mult, ALU.mult),
                         reads=[self.psh[3 + qs], rc.h], writes=[osb.h])
                for qs in range(4):
                    osb = o_ring.tiles[(o_ring.i - 4 + qs) % len(o_ring.tiles)]
                    S.op("pe", lambda e, osb=osb, qs=qs: e.transpose(self.ps[:, 7, qs * 128:(qs + 1) * 128], osb.t[:], ident.t[:]),
                         reads=[osb.h, ident.h], writes=[self.psh[7]])
                gb = gby_ring.next()
                S.op("dve", lambda e, gb=gb, tg=tg: e.scalar_tensor_tensor(gb.t[:], tg.t[:], 1.0, self.ps[:, 7, :], ALU.add, ALU.mult),
                     reads=[tg.h, self.psh[7]], writes=[gb.h])
                S.dma("sp", gbyd[h, :, sl], gb.t[:], reads=[gb.h], writes=[self.dh("gb_y_%d" % l, (h, i))], owner=gb.h, is_store=True)

            for t in range(min(LA, len(tiles))):
                emit_qk(t)
            for t in range(len(tiles)):
                i, ci, nck, kc, mk = tiles[t]
                if t + LA < len(tiles):
                    emit_qk(t + LA)
                b = sbanks[t % 3]
                pT = pT_ring.next()
                S.op("act", lambda e, pT=pT, b=b: e.activation(pT.t[:], self.ps[:, b, :], AF.Exp), reads=[self.psh[b]], writes=[pT.h])
                if mk is not None:
                    S.op("pool", lambda e, pT=pT, mk=mk: e.tensor_tensor(pT.t[:], pT.t[:], masks.t[:, mk, :], ALU.mult),
                         reads=[pT.h, masks.h], writes=[pT.h])
                for qs in range(4):
                    S.op("pe", lambda e, pT=pT, qs=qs, kc=kc, ci=ci, nck=nck: e.matmul(self.ps[:, 3 + qs, 0:129], pT.t[:, qs * 128:(qs + 1) * 128],
                                                                                       Vaug.t[:, kc, 0:129], start=(ci == 0), stop=(ci == nck - 1)),
                         reads=[pT.h, Vaug.h], writes=[self.psh[3 + qs]], inc=(qs == 3))
                if ci == nck - 1:
                    epilogue(i)

    def phase_C1(self, l):
        S = self.S
        xname, xw = ("xT", "host") if l == 0 else ("x2_0", ("C2", 0))
        xd = self.D(xname, [NCH, 128, NT], F32, xw)
        gayd = self.D("ga_y_%d" % l, [NCH, 128, NT], BF16, ("A", l))
        gaAd = self.D("ga_A_%d" % l, [NCH, 128, NT], BF16, ("A", l))
        gbyd = self.D("gb_y_%d" % l, [NCH, 128, NT], BF16, ("B", l))
        gsd = self.D("G_sum_%d" % l, [256, 128], F32, ("X2", l))
        x1d = self.D("x1_%d" % l, [NCH, 128, NT], F32, ("C1", l))
        h2d = self.D("h2_%d" % l, [NCH, 128, NT], BF16, ("C1", l))
        woutd = self.D("w_out", [2, D, D], F32, "host")
        wo = self.sb("wo", [128, 8, D], BF16)
        src = woutd[l].rearrange("(k p) n -> p k n", p=128)
        for k in range(8):
            self.load_w(wo.t[:, k, :], src[:, k, :], wo)
        mv = self.load_modv(l)
        gs = self.sb("gs", [128, 2, 2, 8, 8], F32)
        for s in range(2):
            S.dma("sp", gs.t[:, s].rearrange("p a c i -> p (a c i)"), gsd[s * 128:(s + 1) * 128, :], reads=[self.dh("G_sum_%d" % l)],
                  writes=[gs.h], owner=gs.h)
        inits = self.sb("inits", [128, 17, 8], F32)
        tmpc = self.sb("tmpc", [128, 8], F32)
        S.op("dve", lambda e: e.memset(inits.t[:], 0.0), writes=[inits.h])
        for g in range(15):
            s, j = g % 2, g // 2
            S.op("dve", lambda e, s=s, j=j, g=g: e.tensor_tensor(tmpc.t[:], gs.t[:, s, 1, :, j], inits.t[:, g, :], ALU.mult),
                 reads=[gs.h, inits.h], writes=[tmpc.h])
            S.op("dve", lambda e, s=s, j=j, g=g: e.tensor_tensor(inits.t[:, g + 1, :], tmpc.t[:], gs.t[:, s, 0, :, j], ALU.add),
                 reads=[gs.h, tmpc.h, inits.h], writes=[inits.h])
        io = self.sb("io", [128, 8, 8], F32)
        f0 = self.ppc(PP_FLAG, 1)
        f1 = self.ppc(PP_FLAG + 1, 1)
        iv = inits.t[:, 0:16, :].rearrange("p (i two) c -> p i two c", two=2)
        S.op("dve", lambda e: e.tensor_scalar(io.t[:], iv[:, :, 0, :], f0, None, ALU.mult), reads=[inits.h, self.pp.h], writes=[io.h])
        S.op("dve", lambda e: e.scalar_tensor_tensor(io.t[:], iv[:, :, 1, :], f1, io.t[:], ALU.mult, ALU.add),
             reads=[inits.h, io.h, self.pp.h], writes=[io.h])
        xt_r = self.ring("xtC", 2, [128, 8, TB], F32)
        gay_r = self.ring("gay", 2, [128, 8, TB], BF16)
        gaA_r = self.ring("gaA", 2, [128, 8, TB], BF16)
        gby_r = self.ring("gbyC", 2, [128, 8, TB], BF16)
        yT = self.sb("yT", [128, 8, TB], BF16)
        h2_r = self.ring("h2", 2, [128, 8, TB], BF16)
        rs = self.sb("rsC", [128, TB], F32)
        sq_ring = self.ring("sqC", 2, [128, TB], BF16)
        tmp_ring = self.ring("tmpC", 3, [128, TB], F32)
        print("phase C1 sbuf used", self.sb_ptr - self.sb_base, "of", self.sb_top - self.sb_base, flush=True)

        def loads(i):
            sl = slice(i * TB, (i + 1) * TB)
            xt, gay, gaA, gby = xt_r.tiles[i % 2], gay_r.tiles[i % 2], gaA_r.tiles[i % 2], gby_r.tiles[i % 2]
            S.dma("sp", gay.t[:], gayd.ap().rearrange("c p t -> p c t")[:, :, sl], reads=[self.dh("ga_y_%d" % l, (c, i)) for c in range(8)],
                  writes=[gay.h], owner=gay.h)
            S.dma("sp", gaA.t[:], gaAd.ap().rearrange("c p t -> p c t")[:, :, sl], reads=[self.dh("ga_A_%d" % l, (c, i)) for c in range(8)],
                  writes=[gaA.h], owner=gaA.h)
            S.dma("sp", gby.t[:], gbyd.ap().rearrange("c p t -> p c t")[:, :, sl], reads=[self.dh("gb_y_%d" % l, (c, i)) for c in range(8)],
                  writes=[gby.h], owner=gby.h)
            S.dma("sp", xt.t[:], xd.ap().rearrange("c p t -> p c t")[:, :, sl], reads=[self.dh(xname, (i,))], writes=[xt.h], owner=xt.h)

        loads(0)
        for i in range(NB):
            sl = slice(i * TB, (i + 1) * TB)
            if i + 1 < NB:
                loads(i + 1)
            xt, gay, gaA, gby, h2 = xt_r.tiles[i % 2], gay_r.tiles[i % 2], gaA_r.tiles[i % 2], gby_r.tiles[i % 2], h2_r.tiles[i % 2]
            for c in range(NCH):
                tmp = tmp_ring.next()
                S.op("dve", lambda e, tmp=tmp, c=c, i=i, gaA=gaA, gay=gay: e.scalar_tensor_tensor(tmp.t[:], gaA.t[:, c, :], io.t[:, i, c:c + 1], gay.t[:, c, :], ALU.mult, ALU.add),
                     reads=[gaA.h, gay.h, io.h], writes=[tmp.h])
                S.op("pool", lambda e, tmp=tmp, c=c, gby=gby: e.tensor_tensor(yT.t[:, c, :], tmp.t[:], gby.t[:, c, :], ALU.add),
                     reads=[tmp.h, gby.h], writes=[yT.h])
            for m in range(NCH):
                b = self.ps_next()
                self.mm_group(self.ps[:, b, :], b, [(wo.t[:, k, m * 128:(m + 1) * 128], yT.t[:, k, :]) for k in range(8)], reads=[wo.h, yT.h])
                S.op("dve", lambda e, b=b, m=m, xt=xt: e.scalar_tensor_tensor(xt.t[:, m, :], self.ps[:, b, :], mv.t[:, l * 48 + 16 + m:l * 48 + 17 + m],
                                                                              xt.t[:, m, :], ALU.mult, ALU.add),
                     reads=[self.psh[b], xt.h, mv.h], writes=[xt.h])
            S.dma("sp", x1d.ap().rearrange("c p t -> p c t")[:, :, sl], xt.t[:], reads=[xt.h], writes=[self.dh("x1_%d" % l, (i,))],
                  owner=xt.h, is_store=True)
            self.norm_block(xt, TB, mv.t[:, l * 48 + 32:l * 48 + 40], mv.t[:, l * 48 + 24:l * 48 + 32], h2, rs, sq_ring, tmp_ring, mvh=mv.h)
            S.dma("sp", h2d.ap().rearrange("c p t -> p c t")[:, :, sl], h2.t[:], reads=[h2.h], writes=[self.dh("h2_%d" % l, (i,))],
                  owner=h2.h, is_store=True)

    def phase_C2(self, l):
        S = self.S
        last = (l == 1)
        x1d = self.D("x1_%d" % l, [NCH, 128, NT], F32, ("C1", l))
        h2d = self.D("h2_%d" % l, [NCH, 128, NT], BF16, ("C1", l))
        oname = "out" if last else "x2_0"
        x2d = self.D(oname, [NCH, 128, NT], F32, ("C2", l))
        wfid = self.D("w_ffn_in", [2, D, 2 * DFF], F32, "host")
        wfod = self.D("w_ffn_out", [2, DFF, D], F32, "host")
        wfi = self.sb("wfi", [128, 8, 2 * DFF], BF16)
        wfo = self.sb("wfo", [128, NFF, D], BF16)
        src = wfid[l].rearrange("(k p) n -> p k n", p=128)
        for k in range(8):
            for half in range(2):
                self.load_w(wfi.t[:, k, half * DFF:(half + 1) * DFF], src[:, k, half * DFF:(half + 1) * DFF], wfi)
        src = wfod[l].rearrange("(k p) n -> p k n", p=128)
        for k in range(NFF):
            self.load_w(wfo.t[:, k, :], src[:, k, :], wfo)
        mv = self.load_modv(l)
        xt = self.sb("xtF", [128, 8, TB], F32)
        h2_r = self.ring("h2F", 2, [128, 8, TB], BF16)
        act = self.sb("actT", [128, NFF, TB], BF16)
        sg_ring = self.ring("sg", 2, [128, TB], F32)
        if last:
            rs = self.sb("rsF", [128, TB], F32)
            sq_ring = self.ring("sqF", 2, [128, TB], BF16)
            tmp_ring = self.ring("tmpF", 2, [128, TB], F32)
        print("phase C2 sbuf used", self.sb_ptr - self.sb_base, "of", self.sb_top - self.sb_base, flush=True)
        def load_h2(i):
            h2 = h2_r.tiles[i % 2]
            S.dma("sp", h2.t[:], h2d.ap().rearrange("c p t -> p c t")[:, :, i * TB:(i + 1) * TB], reads=[self.dh("h2_%d" % l, (i,))],
                  writes=[h2.h], owner=h2.h)

        load_h2(0)
        for i in range(NB):
            sl = slice(i * TB, (i + 1) * TB)
            h2 = h2_r.tiles[i % 2]
            if i + 1 < NB:
                load_h2(i + 1)
            S.dma("sp", xt.t[:], x1d.ap().rearrange("c p t -> p c t")[:, :, sl], reads=[self.dh("x1_%d" % l, (i,))], writes=[xt.h], owner=xt.h)
            for f in range(NFF):
                bA = self.ps_next()
                self.mm_group(self.ps[:, bA, :], bA, [(wfi.t[:, k, f * 128:(f + 1) * 128], h2.t[:, k, :]) for k in range(8)], reads=[wfi.h, h2.h])
                bB = self.ps_next()
                self.mm_group(self.ps[:, bB, :], bB, [(wfi.t[:, k, DFF + f * 128:DFF + (f + 1) * 128], h2.t[:, k, :]) for k in range(8)],
                              reads=[wfi.h, h2.h])
                sg = sg_ring.next()
                S.op("act", lambda e, sg=sg, bA=bA: e.activation(sg.t[:], self.ps[:, bA, :], AF.Silu), reads=[self.psh[bA]], writes=[sg.h])
                S.op("dve", lambda e, sg=sg, bB=bB, f=f: e.tensor_tensor(act.t[:, f, :], sg.t[:], self.ps[:, bB, :], ALU.mult),
                     reads=[sg.h, self.psh[bB]], writes=[act.h])
            for m in range(NCH):
                b = self.ps_next()
                self.mm_group(self.ps[:, b, :], b, [(wfo.t[:, k, m * 128:(m + 1) * 128], act.t[:, k, :]) for k in range(NFF)], reads=[wfo.h, act.h])
                S.op("dve", lambda e, b=b, m=m: e.scalar_tensor_tensor(xt.t[:, m, :], self.ps[:, b, :], mv.t[:, l * 48 + 40 + m:l * 48 + 41 + m],
                                                                       xt.t[:, m, :], ALU.mult, ALU.add),
                     reads=[self.psh[b], xt.h, mv.h], writes=[xt.h])
            if last:
                bank = self.ps_next()
                for c in range(NCH):
                    sq = sq_ring.next()
                    S.op("act", lambda e, sq=sq, c=c: e.activation(sq.t[:], xt.t[:, c, :], AF.Square), reads=[xt.h], writes=[sq.h])
                    S.op("pe", lambda e, sq=sq, c=c, bank=bank: e.matmul(self.ps[:, bank, :], self.ones.t[:, :], sq.t[:], start=(c == 0), stop=(c == NCH - 1)),
                         reads=[sq.h, self.ones.h], writes=[self.psh[bank]])
                S.op("act", lambda e, bank=bank: e.activation(rs.t[:], self.ps[:, bank, :], AF.Sqrt, bias=EPS, scale=1.0 / D),
                     reads=[self.psh[bank]], writes=[rs.h])
                S.op("dve", lambda e: e.reciprocal(rs.t[:], rs.t[:]), reads=[rs.h], writes=[rs.h])
                for c in range(NCH):
                    S.op("dve", lambda e, c=c: e.scalar_tensor_tensor(xt.t[:, c, :], xt.t[:, c, :], self.ppc(PP_FG + c, 1), rs.t[:], ALU.mult, ALU.mult),
                         reads=[xt.h, rs.h, self.pp.h], writes=[xt.h])
            S.dma("sp", x2d.ap().rearrange("c p t -> p c t")[:, :, sl], xt.t[:], reads=[xt.h], writes=[self.dh(oname, (i,))],
                  owner=xt.h, is_store=True)


class _Ring:
    def __init__(self, tiles):
        self.tiles = tiles
        self.i = 0

    def next(self):
        t = self.tiles[self.i % len(self.tiles)]
        self.i += 1
        return t


def _host_inputs(inp):
    x = np.asarray(inp["x"], np.float32)
    cores = []
    shared = {
        "w_ada": np.ascontiguousarray(inp["w_ada"], np.float32),
        "w_in": np.ascontiguousarray(inp["w_in"], np.float32),
        "lru_wa": np.ascontiguousarray(inp["lru_wa"], np.float32),
        "lru_wx": np.ascontiguousarray(inp["lru_wx"], np.float32),
        "w_uq": np.ascontiguousarray(inp["w_uq"], np.float32),
        "w_ukv": np.ascontiguousarray(inp["w_ukv"], np.float32),
        "w_out": np.ascontiguousarray(inp["w_out"], np.float32),
        "w_ffn_in": np.ascontiguousarray(inp["w_ffn_in"], np.float32),
        "w_ffn_out": np.ascontiguousarray(inp["w_ffn_out"], np.float32),
    }
    half = 32
    invf = (10000.0 ** (-np.arange(0, 64, 2, dtype=np.float32) / 64)).astype(np.float32)
    invf64 = np.concatenate([invf, invf]).astype(np.float32)
    p = np.arange(128)[:, None]
    f = np.arange(TB)[None, :]
    diag = [((128 * j + p) <= f).astype(np.float32) for j in range(4)]
    onesm = np.ones((128, TB), np.float32)
    zerom = np.zeros((128, TB), np.float32)
    for core in range(8):
        b, r = core // 2, core % 2
        tok = np.concatenate([np.arange((2 * i + r) * TB, (2 * i + r + 1) * TB) for i in range(NB)])
        xs = x[b][tok]
        xT = np.ascontiguousarray(xs.T.reshape(NCH, 128, NT))
        pos = np.ascontiguousarray(np.asarray(inp["positions"])[b][tok].astype(np.int32)[None, :])
        pp = np.zeros((128, NPP), np.float32)
        for l in range(2):
            base = l * PPL
            pp[:, base:base + 48] = np.asarray(inp["b_ada"])[l].reshape(48, 128).T
            pp[:, base + 48:base + 80] = np.asarray(inp["conv_w"])[l].reshape(4, 8, 128).transpose(2, 0, 1).reshape(128, 32)
            pp[:, base + 80:base + 88] = np.asarray(inp["conv_b"])[l].reshape(8, 128).T
            pp[:, base + 88:base + 96] = np.asarray(inp["lru_ba"])[l].T
            pp[:, base + 96:base + 104] = np.asarray(inp["lru_bx"])[l].T
            pp[:, base + 104:base + 112] = np.asarray(inp["lru_a_param"])[l].reshape(8, 128).T
            pp[:, base + 112:base + 114] = np.asarray(inp["q_norm_g"])[l].reshape(2, 128).T
            pp[:, base + 114] = np.asarray(inp["kv_norm_g"])[l]
        pp[:, PP_FG:PP_FG + 8] = np.asarray(inp["final_norm_g"]).reshape(8, 128).T
        pp[:, PP_FLAG] = 1.0 - r
        pp[:, PP_FLAG + 1] = float(r)
        pp[0:64, PP_INVF] = invf64
        pp[0:64, PP_INVF + 1] = (invf64.astype(np.float64) / (2 * np.pi)).astype(np.float32)
        pp[:, PP_C:PP_C + 8] = np.asarray(inp["c"])[b].reshape(8, 128).T
        if r == 0:
            mk = diag + [zerom] * 4
        else:
            mk = [onesm] * 4 + diag
        masks = np.ascontiguousarray(np.concatenate(mk, axis=1))
        d = dict(shared)
        d.update({"xT": xT, "pos": pos, "pp": pp, "masks": masks})
        cores.append(d)
    return cores


def _assemble(outs):
    out = np.zeros((BATCH, SEQ, D), np.float32)
    for core in range(8):
        b, r = core // 2, core % 2
        oT = np.asarray(outs[core]).reshape(D, NT)
        for i in range(NB):
            g = 2 * i + r
            out[b, g * TB:(g + 1) * TB, :] = oT[:, i * TB:(i + 1) * TB].T
    return out


def _all_phases():
    ph = [("M",), ("R",)]
    for l in range(2):
        ph += [("P", l), ("X1", l), ("A", l), ("X2", l), ("B", l), ("C1", l), ("C2", l)]
    return ph


def kernel(**inputs):
    cores = _host_inputs(inputs)
    if MODE == "fused":
        bld = Builder(_all_phases(), fused=True)
        nc = bld.build()
        in_maps = [{k: c[k] for k in bld.ext_in} for c in cores]
        res = run_bass_kernel_spmd(nc, in_maps, core_ids=list(range(8)))
        return _assemble([res.results[i]["out"] for i in range(8)])
    store = [dict(c) for c in cores]
    for ph in _all_phases():
        if ph[0] == "X1":
            l = ph[1]
            for pair in range(4):
                g = np.concatenate([store[2 * pair]["send_halo_%d" % l], store[2 * pair + 1]["send_halo_%d" % l]], axis=0)
                store[2 * pair]["G_halo_%d" % l] = g
                store[2 * pair + 1]["G_halo_%d" % l] = g
            continue
        if ph[0] == "X2":
            l = ph[1]
            for pair in range(4):
                for nm in ("kv", "sum"):
                    g = np.concatenate([store[2 * pair]["send_%s_%d" % (nm, l)], store[2 * pair + 1]["send_%s_%d" % (nm, l)]], axis=0)
                    store[2 * pair]["G_%s_%d" % (nm, l)] = g
                    store[2 * pair + 1]["G_%s_%d" % (nm, l)] = g
            continue
        bld = Builder([ph], fused=False)
        nc = bld.build()
        in_maps = [{k: s[k] for k in bld.ext_in} for s in store]
        res = run_bass_kernel_spmd(nc, in_maps, core_ids=list(range(8)))
        for i in range(8):
            for k in bld.ext_out:
                store[i][k] = res.results[i][k]
    return _assemble([store[i]["out"] for i in range(8)])
```

```python
import numpy as np
from contextlib import ExitStack
import concourse.bass as bass
import concourse.mybir as mybir
from concourse.bass_utils import run_bass_kernel_spmd

F32 = mybir.dt.float32
BF16 = mybir.dt.bfloat16
I32 = mybir.dt.int32
AF = mybir.ActivationFunctionType
ALU = mybir.AluOpType

D = 1024
NCH = 8
SEQ = 8192
BATCH = 4
TB = 512
NB = 8
NT = NB * TB
HEADS = 8
DFF = 2816
NFF = 22
DIN = 3520
DAUG = 3584
EPS = 1e-6
QSCALE = 192 ** -0.5
GROUPS = [[0, 1], [2, 3], [4, 5], [6, 7]]
PPL = 116
PP_FG = 232
PP_FLAG = 240
PP_INVF = 242
PP_C = 244
NPP = 252

MODE = "fused"


class H:
    __slots__ = ("name", "w", "r", "sem", "cnt")

    def __init__(self, name):
        self.name = name
        self.w = []
        self.r = []
        self.sem = None
        self.cnt = 0


class _Eng:
    def __init__(self, name, sem):
        self.name = name
        self.sem = sem
        self.cnt = 0
        self.known = {}
        self.prog = []


class Sched:
    ENGS = ("pe", "act", "dve", "pool", "sp")

    def __init__(self, nc, stack, n_dma_sems=80):
        self.nc = nc
        self.E = {}
        for n in self.ENGS:
            sem = stack.enter_context(nc.semaphore("s_" + n))
            self.E[n] = _Eng(n, sem)
        self.free_sems = []
        for i in range(n_dma_sems):
            self.free_sems.append([stack.enter_context(nc.semaphore("d%d" % i)), 0])
        self.live = []
        self.store_tickets = {}
        self.n_ops = 0
        self.n_waits = 0

    def _deps(self, E, reads, writes, awrites=()):
        deps = []
        for h in reads:
            deps.extend(h.w)
        for h in writes:
            deps.extend(h.w)
            for t in h.r:
                if t[0] is E.sem:
                    continue
                deps.append(t)
        for h in awrites:
            for t in h.r:
                deps.append(t)
        return deps

    def _emit_waits(self, E, deps):
        best = {}
        for (sem, val) in deps:
            k = id(sem)
            if sem is E.sem and (E.name == "pe" or val > E.cnt):
                continue
            if E.known.get(k, 0) >= val:
                continue
            if k not in best or best[k][1] < val:
                best[k] = (sem, val)
        for k, (sem, val) in best.items():
            E.known[k] = val
            E.prog.append(("wait", sem, val))
            self.n_waits += 1

    def _update(self, ticket, reads, writes, awrites=()):
        for h in reads:
            h.r = [t for t in h.r if t[0] is not ticket[0]]
            h.r.append(ticket)
        for h in writes:
            h.w = [ticket]
            h.r = []
        for h in awrites:
            h.w = [t for t in h.w if t[0] is not ticket[0]]
            h.w.append(ticket)
            h.r = []

    def op(self, ename, fn, reads=(), writes=(), inc=True):
        E = self.E[ename]
        self._emit_waits(E, self._deps(E, reads, writes))
        if inc:
            E.cnt += 1
            ticket = (E.sem, E.cnt)
        else:
            ticket = (E.sem, E.cnt + 1)
        E.prog.append(("op", fn, inc))
        self._update(ticket, reads, writes)
        self.n_ops += 1
        return ticket

    def _hsem(self, h):
        if h.sem is None:
            if not self.free_sems:
                raise RuntimeError("out of DMA semaphores")
            ent = self.free_sems.pop()
            h.sem = ent[0]
            h.cnt = ent[1]
            self.live.append(h)
        return h.sem

    def dma(self, q, out, in_, reads=(), writes=(), awrites=(), owner=None, is_store=False, **kw):
        E = self.E[q]
        self._emit_waits(E, self._deps(E, reads, writes, awrites))
        sem = self._hsem(owner)
        owner.cnt += 16
        ticket = (sem, owner.cnt)
        E.prog.append(("dma", out, in_, sem, kw))
        self._update(ticket, reads, writes, awrites)
        if is_store:
            self.store_tickets[id(sem)] = ticket
        self.n_ops += 1
        return ticket

    def collective(self, kind, ins, outs, reads, writes, owner):
        E = self.E["pool"]
        self._emit_waits(E, self._deps(E, reads, writes))
        sem = self._hsem(owner)
        owner.cnt += 1
        ticket = (sem, owner.cnt)
        E.prog.append(("cc", kind, ins, outs, sem))
        self._update(ticket, reads, writes)
        return ticket

    def barrier(self, scratch_ap):
        P = self.E["pool"]
        deps = []
        for n in self.ENGS:
            E = self.E[n]
            if E.cnt > 0:
                deps.append((E.sem, E.cnt))
        for h in self.live:
            deps.append((h.sem, h.cnt))
        self._emit_waits(P, deps)
        P.cnt += 1
        P.prog.append(("op", lambda e: e.memset(scratch_ap, 0.0), True))
        t = (P.sem, P.cnt)
        for n in self.ENGS:
            if n != "pool":
                self._emit_waits(self.E[n], [t])
        for h in self.live:
            self.free_sems.append([h.sem, h.cnt])
            h.sem = None
        self.live = []

    def final_wait(self):
        self._emit_waits(self.E["pool"], list(self.store_tickets.values()))

    def emit(self):
        nc = self.nc
        S = self

        def run(eng_obj, E):
            for item in E.prog:
                k = item[0]
                if k == "wait":
                    eng_obj.wait_ge(item[1], item[2])
                elif k == "op":
                    ins = item[1](eng_obj)
                    if item[2]:
                        ins.then_inc(E.sem, 1)
                elif k == "dma":
                    eng_obj.dma_start(out=item[1], in_=item[2], **item[4]).then_inc(item[3], 16)
                elif k == "cc":
                    eng_obj.collective_compute(item[1], ALU.bypass, replica_groups=GROUPS,
                                               ins=item[2], outs=item[3]).then_inc(item[4], 1)

        with nc.Block() as block:
            @block.tensor
            def _(e):
                run(e, S.E["pe"])

            @block.scalar
            def _(e):
                run(e, S.E["act"])

            @block.vector
            def _(e):
                run(e, S.E["dve"])

            @block.gpsimd
            def _(e):
                run(e, S.E["pool"])

            @block.sync
            def _(e):
                run(e, S.E["sp"])


class Tile:
    __slots__ = ("t", "h")

    def __init__(self, t, name):
        self.t = t
        self.h = H(name)


DT_SIZE = {F32: 4, BF16: 2, I32: 4}


class Builder:
    def __init__(self, phases, fused):
        self.phases = phases
        self.fused = fused
        self.nc = bass.Bass("TRN2", target_bir_lowering=False)
        self.ext_in = {}
        self.ext_out = {}
        self.dram = {}
        self.DH = {}
        self.uid = 0

    def D(self, name, shape, dtype, writer):
        if name in self.dram:
            return self.dram[name]
        if writer == "host" or writer not in self.phases:
            t = self.nc.dram_tensor(name, list(shape), dtype, kind="ExternalInput")
            self.ext_in[name] = (tuple(shape), dtype)
        elif self.fused and name != "out":
            t = self.nc.dram_tensor(name, list(shape), dtype)
        else:
            t = self.nc.dram_tensor(name, list(shape), dtype, kind="ExternalOutput")
            self.ext_out[name] = (tuple(shape), dtype)
        self.dram[name] = t
        return t

    def dh(self, name, key=None):
        k = (name, key)
        if k not in self.DH:
            self.DH[k] = H("D_%s_%s" % (name, key))
        return self.DH[k]

    def sb_reset(self):
        self.sb_ptr = self.sb_base

    def sb(self, name, shape, dtype):
        per = 1
        for s in shape[1:]:
            per *= s
        nbytes = (per * DT_SIZE[dtype] + 63) // 64 * 64
        if self.sb_ptr + nbytes > self.sb_top:
            raise RuntimeError("SBUF overflow allocating %s (%d + %d > %d)" % (name, self.sb_ptr, nbytes, self.sb_top))
        self.uid += 1
        t = self.nc.alloc_sbuf_tensor_at("%s_%d" % (name, self.uid), list(shape), dtype, offset=self.sb_ptr)
        self.sb_ptr += nbytes
        return Tile(t, name)

    def ring(self, name, n, shape, dtype):
        return _Ring([self.sb("%s%d" % (name, i), shape, dtype) for i in range(n)])

    def build(self):
        nc = self.nc
        with ExitStack() as st:
            self.S = S = Sched(nc, st)
            self.sb_base = (nc.sbuf_base + 63) // 64 * 64
            self.sb_top = nc.sbuf_top
            self.sb_reset()
            self.ps = st.enter_context(nc.psum_tensor("ps", [128, 8, 512], F32))
            self.psh = [H("ps%d" % i) for i in range(8)]
            self.ps_rr = 0
            self.bar = self.sb("bar", [128, 8], F32)
            self.ones = self.sb("ones", [128, 128], BF16)
            self.zeros = self.sb("zeros", [128, 512], F32)
            self.pp = self.sb("pp", [128, NPP], F32)
            self.persist_ptr = None
            S.op("pool", lambda e: e.memset(self.ones.t[:], 1.0), writes=[self.ones.h])
            S.op("pool", lambda e: e.memset(self.zeros.t[:], 0.0), writes=[self.zeros.h])
            ppd = self.D("pp", [128, NPP], F32, "host")
            S.dma("sp", self.pp.t[:], ppd[:, :], writes=[self.pp.h], owner=self.pp.h)
            self.persist_ptr = self.sb_ptr
            for ph in self.phases:
                kind = ph[0]
                if not (self.fused and kind in ("X1", "A")):
                    self.sb_ptr = self.persist_ptr
                l = ph[1] if len(ph) > 1 else None
                getattr(self, "phase_" + kind)(*([l] if l is not None else []))
                S.barrier(self.bar.t[:, 0:1])
            S.final_wait()
            print("ops", S.n_ops, "waits", S.n_waits, {n: len(S.E[n].prog) for n in S.ENGS}, flush=True)
            S.emit()
        return nc

    def ps_next(self, banks=(0, 1, 2, 3, 4, 5, 6, 7)):
        b = banks[self.ps_rr % len(banks)]
        self.ps_rr += 1
        return b

    def mm_group(self, out_ap, bank, pairs, reads, **kw):
        S = self.S
        n = len(pairs)
        for i, (l, r) in enumerate(pairs):
            S.op("pe", lambda e, l=l, r=r, i=i: e.matmul(out_ap, l, r, start=(i == 0), stop=(i == n - 1), **kw),
                 reads=reads, writes=[self.psh[bank]], inc=(i == n - 1))

    def ppc(self, col, n=1, parts=128):
        return self.pp.t[0:parts, col:col + n]

    def load_modv(self, l):
        modd = self.D("modv", [128, 96], F32, ("M",))
        mv = self.sb("modv", [128, 96], F32)
        self.S.dma("sp", mv.t[:], modd[:, :], reads=[self.dh("modv")], writes=[mv.h], owner=mv.h)
        return mv

    def norm_block(self, xt, W, sc1_ap, sh_ap, hout, rs, sq_ring, tmp_ring, mvh=None):
        S = self.S
        bank = self.ps_next()
        sqs = []
        for c in range(NCH):
            sq = sq_ring.next()
            S.op("act", lambda e, sq=sq, c=c: e.activation(sq.t[:, 0:W], xt.t[:, c, 0:W], AF.Square),
                 reads=[xt.h], writes=[sq.h])
            S.op("pe", lambda e, sq=sq, c=c: e.matmul(self.ps[:, bank, 0:W], self.ones.t[:, :], sq.t[:, 0:W],
                                                       start=(c == 0), stop=(c == NCH - 1)),
                 reads=[sq.h, self.ones.h], writes=[self.psh[bank]], inc=True)
        S.op("act", lambda e: e.activation(rs.t[:, 0:W], self.ps[:, bank, 0:W], AF.Sqrt, bias=EPS, scale=1.0 / D),
             reads=[self.psh[bank]], writes=[rs.h])
        S.op("dve", lambda e: e.reciprocal(rs.t[:, 0:W], rs.t[:, 0:W]), reads=[rs.h], writes=[rs.h])
        for c in range(NCH):
            tmp = tmp_ring.next()
            S.op("dve", lambda e, tmp=tmp, c=c: e.tensor_tensor(tmp.t[:, 0:W], xt.t[:, c, 0:W], rs.t[:, 0:W], ALU.mult),
                 reads=[xt.h, rs.h], writes=[tmp.h])
            S.op("pool", lambda e, tmp=tmp, c=c: e.tensor_scalar(hout.t[:, c, 0:W], tmp.t[:, 0:W],
                                                                 sc1_ap[:, c:c + 1], sh_ap[:, c:c + 1], ALU.mult, ALU.add),
                 reads=[tmp.h, mvh], writes=[hout.h])

    def load_w(self, dst_ap, src_ap, tile):
        self.S.dma("pool", dst_ap, src_ap, writes=[tile.h], owner=tile.h)

    def phase_M(self):
        S = self.S
        w_ada = self.D("w_ada", [2, D, 6 * D], F32, "host")
        modd = self.D("modv", [128, 96], F32, ("M",))
        wr = self.ring("wada", 2, [128, 8, 1536], BF16)
        cbf = self.sb("cbf", [128, 8], BF16)
        mv = self.sb("mv", [128, 96], F32)
        S.op("dve", lambda e: e.tensor_copy(cbf.t[:], self.ppc(PP_C, 8)), reads=[self.pp.h], writes=[cbf.h])
        bank = 0
        for l in range(2):
            for g in range(4):
                wt = wr.next()
                src = w_ada[l].rearrange("(k p) n -> p k n", p=128)
                for k in range(8):
                    self.S.dma("pool", wt.t[:, k, :], src[:, k, g * 1536:(g + 1) * 1536], writes=[wt.h], owner=wt.h)
                for j in range(12):
                    J = g * 12 + j
                    self.mm_group(self.ps[:, bank, J:J + 1], bank,
                                  [(wt.t[:, k, j * 128:(j + 1) * 128], cbf.t[:, k:k + 1]) for k in range(8)],
                                  reads=[wt.h, cbf.h])
            S.op("dve", lambda e, l=l: e.tensor_tensor(mv.t[:, l * 48:(l + 1) * 48], self.ps[:, bank, 0:48],
                                                       self.ppc(l * PPL, 48), ALU.add),
                 reads=[self.psh[bank], self.pp.h], writes=[mv.h])
            for off in (8, 32):
                S.op("dve", lambda e, l=l, off=off: e.tensor_scalar(mv.t[:, l * 48 + off:l * 48 + off + 8],
                                                                    mv.t[:, l * 48 + off:l * 48 + off + 8],
                                                                    1.0, None, ALU.add),
                     reads=[mv.h], writes=[mv.h])
        S.dma("sp", modd[:, :], mv.t[:], reads=[mv.h], writes=[self.dh("modv")], owner=mv.h, is_store=True)

    def phase_R(self):
        S = self.S
        posd = self.D("pos", [1, NT], I32, "host")
        ropd = self.D("rope", [4, 64, NT], F32, ("R",))
        posi = self.ring("posi", 2, [64, TB], I32)
        posf = self.ring("posf", 2, [64, TB], F32)
        ang = self.ring("ang", 2, [64, TB], F32)
        tq = self.ring("tq", 2, [64, TB], F32)
        ti = self.ring("ti", 2, [64, TB], I32)
        out = self.ring("rout", 4, [64, 2, TB], F32)
        invf = self.ppc(PP_INVF, 1, 64)
        invf2 = self.ppc(PP_INVF + 1, 1, 64)
        for i in range(NB):
            pi = posi.next()
            pf = posf.next()
            S.dma("sp", pi.t[:], posd[0:1, i * TB:(i + 1) * TB].partition_broadcast(64), writes=[pi.h], owner=pi.h)
            S.op("dve", lambda e, pi=pi, pf=pf: e.tensor_copy(pf.t[:], pi.t[:]), reads=[pi.h], writes=[pf.h])
            for which, (aoff, toff) in enumerate(((0.0, 0.0), (np.pi / 2, 0.25))):
                a = ang.next()
                t = tq.next()
                tii = ti.next()
                o = out.next()
                S.op("dve", lambda e, a=a, pf=pf, aoff=aoff: e.tensor_scalar(a.t[:], pf.t[:], invf, aoff, ALU.mult, ALU.add),
                     reads=[pf.h, self.pp.h], writes=[a.h])
                S.op("dve", lambda e, t=t, pf=pf, toff=toff: e.tensor_scalar(t.t[:], pf.t[:], invf2, toff, ALU.mult, ALU.add),
                     reads=[pf.h, self.pp.h], writes=[t.h])
                S.op("dve", lambda e, t=t, tii=tii: e.tensor_copy(tii.t[:], t.t[:]), reads=[t.h], writes=[tii.h])
                S.op("dve", lambda e, t=t, tii=tii: e.tensor_copy(t.t[:], tii.t[:]), reads=[tii.h], writes=[t.h])
                S.op("dve", lambda e, t=t, a=a: e.scalar_tensor_tensor(a.t[:], t.t[:], -2 * np.pi, a.t[:], ALU.mult, ALU.add),
                     reads=[t.h, a.h], writes=[a.h])
                S.op("act", lambda e, o=o, a=a: e.activation(o.t[:, 0, :], a.t[:], AF.Sin), reads=[a.h], writes=[o.h])
                S.op("dve", lambda e, o=o: e.tensor_scalar(o.t[:, 1, :], o.t[:, 0, :], QSCALE, None, ALU.mult),
                     reads=[o.h], writes=[o.h])
                tidx = 1 if which == 0 else 0
                S.dma("sp", ropd[tidx, :, i * TB:(i + 1) * TB], o.t[:, 0, :], reads=[o.h],
                      awrites=[self.dh("rope")], owner=o.h, is_store=True)
                S.dma("sp", ropd[tidx + 2, :, i * TB:(i + 1) * TB], o.t[:, 1, :], reads=[o.h],
                      awrites=[self.dh("rope")], owner=o.h, is_store=True)

    def get_w_in(self, l, lru_only):
        key = ("w_in", l)
        if getattr(self, "_w_in_key", None) == key:
            return self._w_in
        w_in = self.D("w_in", [2, D, DIN], F32, "host")
        ncols = D if lru_only else DAUG
        wt = self.sb("w_in_sb", [128, 8, ncols], BF16)
        src = w_in[l].rearrange("(k p) n -> p k n", p=128)
        for k in range(8):
            if lru_only:
                self.load_w(wt.t[:, k, 0:D], src[:, k, 0:D], wt)
            else:
                self.load_w(wt.t[:, k, 0:1472], src[:, k, 0:1472], wt)
                self.load_w(wt.t[:, k, 1472:1504], src[:, k, 1440:1472], wt)
                self.load_w(wt.t[:, k, 1504:1536], src[:, k, 1408:1440], wt)
                self.load_w(wt.t[:, k, 1536:DAUG], src[:, k, 1472:DIN], wt)
        if not lru_only:
            self.S.op("dve", lambda e: e.tensor_scalar(wt.t[:, :, 1472:1504], wt.t[:, :, 1472:1504], -1.0, None, ALU.mult),
                      reads=[wt.h], writes=[wt.h])
        self._w_in_key = key
        self._w_in = wt
        return wt

    def phase_P(self, l):
        S = self.S
        xname, xw = ("xT", "host") if l == 0 else ("x2_0", ("C2", 0))
        xd = self.D(xname, [NCH, 128, NT], F32, xw)
        shd = self.D("send_halo_%d" % l, [128, 192], F32, ("P", l))
        wt = self.get_w_in(l, lru_only=not self.fused)
        mv = self.load_modv(l)
        xh = self.sb("xh", [128, 8, 24], F32)
        hh = self.sb("hh", [128, 8, 24], BF16)
        rs = self.sb("rsP", [128, 24], F32)
        sq_ring = self.ring("sqP", 2, [128, 24], BF16)
        tmp_ring = self.ring("tmpP", 2, [128, 24], F32)
        xlh = self.sb("xlh", [128, 8, 24], F32)
        for c in range(NCH):
            src = xd[c].rearrange("p (i t) -> p i t", t=TB)[:, :, TB - 3:TB]
            rd = [self.dh(xname, (i,)) for i in range(NB)]
            S.dma("sp", xh.t[:, c, :].rearrange("p (i t) -> p i t", t=3), src, reads=rd, writes=[xh.h], owner=xh.h)
        self.norm_block(xh, 24, mv.t[:, l * 48 + 8:l * 48 + 16], mv.t[:, l * 48 + 0:l * 48 + 8], hh, rs, sq_ring, tmp_ring, mvh=mv.h)
        for m in range(NCH):
            bank = self.ps_next()
            self.mm_group(self.ps[:, bank, 0:24], bank,
                          [(wt.t[:, k, m * 128:(m + 1) * 128], hh.t[:, k, :]) for k in range(8)],
                          reads=[wt.h, hh.h])
            S.op("act", lambda e, m=m, bank=bank: e.activation(xlh.t[:, m, :], self.ps[:, bank, 0:24], AF.Copy),
                 reads=[self.psh[bank]], writes=[xlh.h])
        S.dma("sp", shd[:, :], xlh.t[:].rearrange("p m t -> p (m t)"), reads=[xlh.h], writes=[self.dh("send_halo_%d" % l)],
              owner=xlh.h, is_store=True)

    def phase_X1(self, l):
        shd = self.D("send_halo_%d" % l, [128, 192], F32, ("P", l))
        gd = self.D("G_halo_%d" % l, [256, 192], F32, ("X1", l))
        o = H("cc1")
        self.S.collective("AllGather", [shd.ap().opt()], [gd.ap().opt()],
                          reads=[self.dh("send_halo_%d" % l)], writes=[self.dh("G_halo_%d" % l)], owner=o)

    def phase_X2(self, l):
        for nm, sshape, gshape, dt in (("kv", [192, NT], [384, NT], BF16), ("sum", [128, 128], [256, 128], F32)):
            sd = self.D("send_%s_%d" % (nm, l), sshape, dt, ("A", l))
            gd = self.D("G_%s_%d" % (nm, l), gshape, dt, ("X2", l))
            o = H("cc2" + nm)
            self.S.collective("AllGather", [sd.ap().opt()], [gd.ap().opt()],
                              reads=[self.dh("send_%s_%d" % (nm, l))], writes=[self.dh("G_%s_%d" % (nm, l))], owner=o)

    def phase_A(self, l):
        S = self.S
        xname, xw = ("xT", "host") if l == 0 else ("x2_0", ("C2", 0))
        xd = self.D(xname, [NCH, 128, NT], F32, xw)
        posd = self.D("pos", [1, NT], I32, "host")
        ropd = self.D("rope", [4, 64, NT], F32, ("R",))
        ghd = self.D("G_halo_%d" % l, [256, 192], F32, ("X1", l))
        gayd = self.D("ga_y_%d" % l, [NCH, 128, NT], BF16, ("A", l))
        gaAd = self.D("ga_A_%d" % l, [NCH, 128, NT], BF16, ("A", l))
        tgbd = self.D("tgb_%d" % l, [NCH, 128, NT], BF16, ("A", l))
        cqd = self.D("cq_%d" % l, [2, 128, NT], BF16, ("A", l))
        skvd = self.D("send_kv_%d" % l, [192, NT], BF16, ("A", l))
        ssumd = self.D("send_sum_%d" % l, [128, 128], F32, ("A", l))
        lwa = self.D("lru_wa", [2, 8, 128, 128], F32, "host")
        lwx = self.D("lru_wx", [2, 8, 128, 128], F32, "host")
        wt = self.get_w_in(l, lru_only=False)
        base = l * PPL
        wa = self.sb("wa", [128, 8, 128], BF16)
        wx = self.sb("wx", [128, 8, 128], BF16)
        self.load_w(wa.t[:], lwa[l].rearrange("n i j -> i n j"), wa)
        self.load_w(wx.t[:], lwx[l].rearrange("n i j -> i n j"), wx)
        mv = self.load_modv(l)
        hb = self.sb("hb", [128, 16], F32)
        c05 = self.sb("c05", [128, 8], F32)
        S.op("dve", lambda e: e.tensor_scalar(hb.t[:], self.ppc(base + 88, 16), 0.5, None, ALU.mult),
             reads=[self.pp.h], writes=[hb.h])
        S.op("act", lambda e: e.activation(c05.t[:], self.ppc(base + 104, 8), AF.Exp, scale=-1.0),
             reads=[self.pp.h], writes=[c05.h])
        S.op("act", lambda e: e.activation(c05.t[:], c05.t[:], AF.Ln, bias=1.0, scale=1.0), reads=[c05.h], writes=[c05.h])
        S.op("dve", lambda e: e.tensor_scalar(c05.t[:], c05.t[:], -4.0, None, ALU.mult), reads=[c05.h], writes=[c05.h])
        gh = self.sb("gh", [128, 2, 8, 8, 3], F32)
        halo = self.sb("halo", [128, 8, 8, 3], F32)
        for s in range(2):
            S.dma("sp", gh.t[:, s].rearrange("p m i t -> p (m i t)"), ghd[s * 128:(s + 1) * 128, :],
                  reads=[self.dh("G_halo_%d" % l)], writes=[gh.h], owner=gh.h)
        f0 = self.ppc(PP_FLAG, 1)
        f1 = self.ppc(PP_FLAG + 1, 1)
        S.op("dve", lambda e: e.memset(halo.t[:], 0.0), writes=[halo.h])
        S.op("dve", lambda e: e.tensor_scalar(halo.t[:, :, 1:8, :], gh.t[:, 1, :, 0:7, :], f0, None, ALU.mult),
             reads=[gh.h, self.pp.h], writes=[halo.h])
        S.op("dve", lambda e: e.scalar_tensor_tensor(halo.t[:], gh.t[:, 0], f1, halo.t[:], ALU.mult, ALU.add),
             reads=[gh.h, halo.h, self.pp.h], writes=[halo.h])
        summ = self.sb("summ", [128, 2, 8, 8], F32)
        xt = self.sb("xtA", [128, 8, TB], F32)
        hT = self.sb("hT", [128, 8, TB], BF16)
        rs = self.sb("rsA", [128, TB], F32)
        sq_ring = self.ring("sqA", 2, [128, TB], BF16)
        tmp_ring = self.ring("tmpA", 2, [128, TB], F32)
        posi = self.sb("posiA", [128, TB], I32)
        posf = self.sb("posfA", [128, TB], F32)
        mb2 = self.sb("mb2", [128, TB], F32)
        rope = self.sb("ropeA", [64, 2, TB], F32)
        ui = self.sb("ui", [128, 8, TB], F32)
        aT = self.sb("aT", [128, 8, TB], F32)
        tga = self.sb("tga", [128, 8, TB], BF16)
        xl_ring = self.ring("xl", 1, [128, TB + 3], F32)
        u_ring = self.ring("u", 2, [128, TB], F32)
        ubf_ring = self.ring("ubf", 1, [128, TB], BF16)
        tr_ring = self.ring("tr", 1, [128, TB], F32)
        tiv_ring = self.ring("tiv", 1, [128, TB], F32)
        m4_ring = self.ring("m4", 1, [128, TB], F32)
        b_ring = self.ring("bb", 1, [128, TB], F32)
        h0_ring = self.ring("h0", 1, [128, TB], F32)
        A_ring = self.ring("AA", 1, [128, TB], F32)
        ob_ring = self.ring("ob", 4, [128, TB], BF16)
        qd = self.sb("qd", [128, 2, TB], F32)
        kvd = self.sb("kvd", [128, TB], F32)
        rq = rs
        rkv = self.sb("rkv", [128, TB], F32)
        kp_ring = tmp_ring
        print("phase A sbuf used", self.sb_ptr - self.sb_base, "of", self.sb_top - self.sb_base, flush=True)
        cw = lambda k, c: self.ppc(base + 48 + k * 8 + c, 1)
        cb = lambda c: self.ppc(base + 80 + c, 1)
        for i in range(NB):
            sl = slice(i * TB, (i + 1) * TB)
            S.dma("sp", xt.t[:], xd.ap().rearrange("c p t -> p c t")[:, :, sl], reads=[self.dh(xname, (i,))],
                  writes=[xt.h], owner=xt.h)
            S.dma("sp", posi.t[:], posd[0:1, sl].partition_broadcast(128), writes=[posi.h], owner=posi.h)
            S.dma("sp", rope.t[:], ropd.ap().rearrange("f p t -> p f t")[:, 0:2, sl], reads=[self.dh("rope")],
                  writes=[rope.h], owner=rope.h)
            S.op("dve", lambda e: e.tensor_copy(posf.t[:], posi.t[:]), reads=[posi.h], writes=[posf.h])
            S.op("dve", lambda e: e.tensor_scalar(mb2.t[:], posf.t[:], 0.0, 2e6, ALU.is_equal, ALU.mult),
                 reads=[posf.h], writes=[mb2.h])
            self.norm_block(xt, TB, mv.t[:, l * 48 + 8:l * 48 + 16], mv.t[:, l * 48 + 0:l * 48 + 8], hT, rs, sq_ring, tmp_ring, mvh=mv.h)

            def proj(m0, msz, bank, poff=0):
                self.mm_group(self.ps[poff:poff + msz, bank, :], bank,
                              [(wt.t[:, k, m0:m0 + msz], hT.t[:, k, :]) for k in range(8)], reads=[wt.h, hT.h])

            banks = (0, 1, 2, 3, 4, 5)
            for c in range(NCH):
                xl = xl_ring.next()
                u = u_ring.next()
                ubf = ubf_ring.next()
                tr = tr_ring.next()
                tiv = tiv_ring.next()
                b0 = self.ps_next(banks)
                proj(c * 128, 128, b0)
                S.op("act", lambda e, xl=xl, b0=b0: e.activation(xl.t[:, 3:TB + 3], self.ps[:, b0, :], AF.Copy),
                     reads=[self.psh[b0]], writes=[xl.h])
                S.op("pool", lambda e, xl=xl, c=c, i=i: e.tensor_copy(xl.t[:, 0:3], halo.t[:, c, i, :]),
                     reads=[halo.h, xl.h], writes=[xl.h])
                S.op("pool", lambda e, xl=xl, u=u, c=c: e.tensor_scalar(u.t[:], xl.t[:, 0:TB], cw(0, c), cb(c), ALU.mult, ALU.add),
                     reads=[xl.h, self.pp.h], writes=[u.h])
                for k in range(1, 4):
                    S.op("dve", lambda e, xl=xl, u=u, c=c, k=k: e.scalar_tensor_tensor(u.t[:], xl.t[:, k:k + TB], cw(k, c), u.t[:],
                                                                                        ALU.mult, ALU.add),
                         reads=[xl.h, u.h, self.pp.h], writes=[u.h])
                S.op("act", lambda e, u=u, ubf=ubf: e.activation(ubf.t[:], u.t[:], AF.Copy), reads=[u.h], writes=[ubf.h])
                b1 = self.ps_next(banks)
                b2 = self.ps_next(banks)
                self.mm_group(self.ps[:, b1, :], b1, [(wa.t[:, c, :], ubf.t[:])], reads=[wa.h, ubf.h])
                self.mm_group(self.ps[:, b2, :], b2, [(wx.t[:, c, :], ubf.t[:])], reads=[wx.h, ubf.h])
                S.op("act", lambda e, tr=tr, b1=b1, c=c: e.activation(tr.t[:], self.ps[:, b1, :], AF.Tanh, bias=hb.t[:, c:c + 1], scale=0.5),
                     reads=[self.psh[b1], hb.h], writes=[tr.h])
                S.op("act", lambda e, tiv=tiv, b2=b2, c=c: e.activation(tiv.t[:], self.ps[:, b2, :], AF.Tanh, bias=hb.t[:, 8 + c:9 + c], scale=0.5),
                     reads=[self.psh[b2], hb.h], writes=[tiv.h])
                S.op("pool", lambda e, tr=tr: e.tensor_tensor(tr.t[:], tr.t[:], mb2.t[:], ALU.add),
                     reads=[tr.h, mb2.h], writes=[tr.h])
                S.op("act", lambda e, tr=tr, c=c: e.activation(aT.t[:, c, :], tr.t[:], AF.Exp, bias=c05.t[:, c:c + 1], scale=c05.t[:, c:c + 1]),
                     reads=[tr.h, c05.h], writes=[aT.h])
                S.op("dve", lambda e, tiv=tiv, u=u, c=c: e.scalar_tensor_tensor(ui.t[:, c, :], tiv.t[:], 1.0, u.t[:], ALU.add, ALU.mult),
                     reads=[tiv.h, u.h], writes=[ui.h])
                b3 = self.ps_next(banks)
                proj(1536 + c * 128, 128, b3)
                S.op("act", lambda e, b3=b3, c=c: e.activation(tga.t[:, c, :], self.ps[:, b3, :], AF.Tanh, scale=0.5),
                     reads=[self.psh[b3]], writes=[tga.h])
                b4 = self.ps_next(banks)
                proj(2560 + c * 128, 128, b4)
                ob = ob_ring.next()
                S.op("act", lambda e, b4=b4, ob=ob: e.activation(ob.t[:], self.ps[:, b4, :], AF.Tanh, scale=0.5),
                     reads=[self.psh[b4]], writes=[ob.h])
                S.dma("sp", tgbd[c, :, sl], ob.t[:], reads=[ob.h], writes=[self.dh("tgb_%d" % l, (c, i))], owner=ob.h, is_store=True)
            for k2 in range(2):
                b = self.ps_next(banks)
                proj(1024 + k2 * 128, 128, b)
                S.op("act", lambda e, b=b, k2=k2: e.activation(qd.t[:, k2, :], self.ps[:, b, :], AF.Copy),
                     reads=[self.psh[b]], writes=[qd.h])
            b = self.ps_next(banks)
            proj(1280, 128, b)
            S.op("act", lambda e, b=b: e.activation(kvd.t[:], self.ps[:, b, :], AF.Copy), reads=[self.psh[b]], writes=[kvd.h])
            bA = self.ps_next(banks)
            bB = self.ps_next(banks)
            proj(1408, 64, bA)
            proj(1472, 64, bB)
            kp1 = kp_ring.next()
            kp2 = kp_ring.next()
            ob = ob_ring.next()
            S.op("dve", lambda e, kp1=kp1, bA=bA: e.tensor_tensor(kp1.t[0:64, :], self.ps[0:64, bA, :], rope.t[:, 0, :], ALU.mult),
                 reads=[self.psh[bA], rope.h], writes=[kp1.h])
            S.op("dve", lambda e, kp2=kp2, bB=bB: e.tensor_tensor(kp2.t[0:64, :], self.ps[0:64, bB, :], rope.t[:, 1, :], ALU.mult),
                 reads=[self.psh[bB], rope.h], writes=[kp2.h])
            S.op("dve", lambda e, kp1=kp1, kp2=kp2, ob=ob: e.tensor_tensor(ob.t[0:64, :], kp1.t[0:64, :], kp2.t[0:64, :], ALU.add),
                 reads=[kp1.h, kp2.h], writes=[ob.h])
            S.dma("sp", skvd[128:192, sl], ob.t[0:64, :], reads=[ob.h], awrites=[self.dh("send_kv_%d" % l)], owner=ob.h, is_store=True)
            bq = self.ps_next(banks)
            for k2 in range(2):
                sq = sq_ring.next()
                S.op("act", lambda e, sq=sq, k2=k2: e.activation(sq.t[:], qd.t[:, k2, :], AF.Square), reads=[qd.h], writes=[sq.h])
                S.op("pe", lambda e, sq=sq, k2=k2, bq=bq: e.matmul(self.ps[:, bq, :], self.ones.t[:, :], sq.t[:], start=(k2 == 0), stop=(k2 == 1)),
                     reads=[sq.h, self.ones.h], writes=[self.psh[bq]])
            bk = self.ps_next(banks)
            sq = sq_ring.next()
            S.op("act", lambda e, sq=sq: e.activation(sq.t[:], kvd.t[:], AF.Square), reads=[kvd.h], writes=[sq.h])
            S.op("pe", lambda e, sq=sq, bk=bk: e.matmul(self.ps[:, bk, :], self.ones.t[:, :], sq.t[:], start=True, stop=True),
                 reads=[sq.h, self.ones.h], writes=[self.psh[bk]])
            S.op("act", lambda e, bq=bq: e.activation(rq.t[:], self.ps[:, bq, :], AF.Sqrt, bias=EPS, scale=1.0 / 256), reads=[self.psh[bq]], writes=[rq.h])
            S.op("act", lambda e, bk=bk: e.activation(rkv.t[:], self.ps[:, bk, :], AF.Sqrt, bias=EPS, scale=1.0 / 128), reads=[self.psh[bk]], writes=[rkv.h])
            m4s = []
            S.op("dve", lambda e: e.reciprocal(rq.t[:], rq.t[:]), reads=[rq.h], writes=[rq.h])
            S.op("dve", lambda e: e.reciprocal(rkv.t[:], rkv.t[:]), reads=[rkv.h], writes=[rkv.h])
            for k2 in range(2):
                tmp = tmp_ring.next()
                ob = ob_ring.next()
                S.op("dve", lambda e, tmp=tmp, k2=k2: e.tensor_tensor(tmp.t[:], qd.t[:, k2, :], rq.t[:], ALU.mult), reads=[qd.h, rq.h], writes=[tmp.h])
                S.op("dve", lambda e, tmp=tmp, ob=ob, k2=k2: e.tensor_scalar(ob.t[:], tmp.t[:], self.ppc(base + 112 + k2, 1), None, ALU.mult),
                     reads=[tmp.h, self.pp.h], writes=[ob.h])
                S.dma("sp", cqd[k2, :, sl], ob.t[:], reads=[ob.h], writes=[self.dh("cq_%d" % l, (k2, i))], owner=ob.h, is_store=True)
            tmp = tmp_ring.next()
            ob = ob_ring.next()
            S.op("dve", lambda e, tmp=tmp: e.tensor_tensor(tmp.t[:], kvd.t[:], rkv.t[:], ALU.mult), reads=[kvd.h, rkv.h], writes=[tmp.h])
            S.op("dve", lambda e, tmp=tmp, ob=ob: e.tensor_scalar(ob.t[:], tmp.t[:], self.ppc(base + 114, 1), None, ALU.mult),
                 reads=[tmp.h, self.pp.h], writes=[ob.h])
            S.dma("sp", skvd[0:128, sl], ob.t[:], reads=[ob.h], awrites=[self.dh("send_kv_%d" % l)], owner=ob.h, is_store=True)
            for c in range(NCH):
                m4 = m4_ring.next()
                bt = b_ring.next()
                h0 = h0_ring.next()
                AA = A_ring.next()
                S.op("pool", lambda e, m4=m4, c=c: e.tensor_tensor(m4.t[:], aT.t[:, c, :], aT.t[:, c, :], ALU.mult), reads=[aT.h], writes=[m4.h])
                S.op("act", lambda e, m4=m4: e.activation(m4.t[:], m4.t[:], AF.Sqrt, bias=1.0 / 16, scale=-1.0 / 16), reads=[m4.h], writes=[m4.h])
                S.op("dve", lambda e, bt=bt, m4=m4, c=c: e.tensor_tensor(bt.t[:], ui.t[:, c, :], m4.t[:], ALU.mult), reads=[ui.h, m4.h], writes=[bt.h])
                S.op("dve", lambda e, bt=bt, h0=h0, c=c: e.tensor_tensor_scan(h0.t[:], aT.t[:, c, :], bt.t[:], 0.0, ALU.mult, ALU.add),
                     reads=[aT.h, bt.h], writes=[h0.h])
                S.op("dve", lambda e, AA=AA, c=c: e.tensor_tensor_scan(AA.t[:], aT.t[:, c, :], self.zeros.t[:], 1.0, ALU.mult, ALU.add),
                     reads=[aT.h, self.zeros.h], writes=[AA.h])
                S.op("pool", lambda e, h0=h0, c=c, i=i: e.tensor_copy(summ.t[:, 0, c, i:i + 1], h0.t[:, TB - 1:TB]), reads=[h0.h], writes=[summ.h])
                S.op("pool", lambda e, AA=AA, c=c, i=i: e.tensor_copy(summ.t[:, 1, c, i:i + 1], AA.t[:, TB - 1:TB]), reads=[AA.h, summ.h], writes=[summ.h])
                ob1 = ob_ring.next()
                ob2 = ob_ring.next()
                S.op("dve", lambda e, ob1=ob1, h0=h0, c=c: e.scalar_tensor_tensor(ob1.t[:], tga.t[:, c, :], 1.0, h0.t[:], ALU.add, ALU.mult),
                     reads=[tga.h, h0.h], writes=[ob1.h])
                S.op("dve", lambda e, ob2=ob2, AA=AA, c=c: e.scalar_tensor_tensor(ob2.t[:], tga.t[:, c, :], 1.0, AA.t[:], ALU.add, ALU.mult),
                     reads=[tga.h, AA.h], writes=[ob2.h])
                S.dma("sp", gayd[c, :, sl], ob1.t[:], reads=[ob1.h], writes=[self.dh("ga_y_%d" % l, (c, i))], owner=ob1.h, is_store=True)
                S.dma("sp", gaAd[c, :, sl], ob2.t[:], reads=[ob2.h], writes=[self.dh("ga_A_%d" % l, (c, i))], owner=ob2.h, is_store=True)
        S.dma("sp", ssumd[:, :], summ.t[:].rearrange("p a c i -> p (a c i)"), reads=[summ.h], writes=[self.dh("send_sum_%d" % l)],
              owner=summ.h, is_store=True)
        self._w_in_key = None

    def phase_B(self, l):
        S = self.S
        gkvd = self.D("G_kv_%d" % l, [384, NT], BF16, ("X2", l))
        cqd = self.D("cq_%d" % l, [2, 128, NT], BF16, ("A", l))
        tgbd = self.D("tgb_%d" % l, [NCH, 128, NT], BF16, ("A", l))
        gbyd = self.D("gb_y_%d" % l, [NCH, 128, NT], BF16, ("B", l))
        ropd = self.D("rope", [4, 64, NT], F32, ("R",))
        maskd = self.D("masks", [128, 8 * TB], F32, "host")
        wuqd = self.D("w_uq", [2, 256, HEADS, 192], F32, "host")
        wukvd = self.D("w_ukv", [2, 128, HEADS, 256], F32, "host")
        ckvT = self.sb("ckvT", [128, 2 * NT], BF16)
        kpeT = self.sb("kpeT", [128, 2 * NT], BF16)
        cqT = self.sb("cqT", [128, 2, NT], BF16)
        wuq = self.sb("wuq", [128, 2, HEADS, 256], BF16)
        wukv = self.sb("wukv", [128, HEADS, 256], BF16)
        maskf = self.sb("maskf", [128, TB], F32)
        masks = self.sb("masks", [128, 8, TB], BF16)
        KnT = self.sb("KnT", [128, 2 * NT], BF16)
        Vt = self.sb("Vt", [128, 64, 128], BF16)
        twos = self.sb("twos", [128, 128], BF16)
        qnT = self.sb("qnT", [128, NT], BF16)
        qpeT = self.sb("qpeT", [128, NT], BF16)
        rope_ring = self.ring("ropeB", 2, [64, 2, TB], F32)
        pT_ring = self.ring("pT", 6, [128, TB], BF16)
        t1_ring = self.ring("t1", 2, [64, TB], F32)
        t2_ring = self.ring("t2", 2, [64, TB], F32)
        ssum_r = self.ring("ssum", 4, [128, TB], F32)
        hl_ring = self.ring("hilo", 4, [128, TB], BF16)
        rs_ring = self.ring("rsB", 2, [128, TB], F32)
        y_ring = self.ring("yB", 2, [128, TB], F32)
        tgb_ring = self.ring("tgbB", 2, [128, TB], BF16)
        gby_ring = self.ring("gby", 2, [128, TB], BF16)
        print("phase B sbuf used", self.sb_ptr - self.sb_base, "of", self.sb_top - self.sb_base, flush=True)
        gkv = self.dh("G_kv_%d" % l)
        S.op("pool", lambda e: e.memset(kpeT.t[64:128, :], 0.0), writes=[kpeT.h])
        S.op("pool", lambda e: e.memset(qpeT.t[64:128, :], 0.0), writes=[qpeT.h])
        for s in range(2):
            S.dma("sp", ckvT.t[:, s * NT:(s + 1) * NT], gkvd[s * 192:s * 192 + 128, :], reads=[gkv], writes=[ckvT.h], owner=ckvT.h)
            S.dma("sp", kpeT.t[0:64, s * NT:(s + 1) * NT], gkvd[s * 192 + 128:s * 192 + 192, :], reads=[gkv], writes=[kpeT.h], owner=kpeT.h)
        for k2 in range(2):
            S.dma("sp", cqT.t[:, k2, :], cqd[k2, :, :], reads=[self.dh("cq_%d" % l, (k2, i)) for i in range(NB)], writes=[cqT.h], owner=cqT.h)
        src = wuqd[l].rearrange("(k p) h d -> p k h d", p=128)
        for k2 in range(2):
            self.load_w(wuq.t[:, k2, :, 0:192], src[:, k2, :, :], wuq)
            self.load_w(wuq.t[:, k2, :, 192:224], src[:, k2, :, 160:192], wuq)
            self.load_w(wuq.t[:, k2, :, 224:256], src[:, k2, :, 128:160], wuq)
        S.op("dve", lambda e: e.tensor_scalar(wuq.t[:, :, :, 192:224], wuq.t[:, :, :, 192:224], -1.0, None, ALU.mult),
             reads=[wuq.h], writes=[wuq.h])
        self.load_w(wukv.t[:], wukvd[l], wukv)
        for j in range(8):
            S.dma("sp", maskf.t[:], maskd[:, j * TB:(j + 1) * TB], writes=[maskf.h], owner=maskf.h)
            S.op("dve", lambda e, j=j: e.tensor_copy(masks.t[:, j, :], maskf.t[:]), reads=[maskf.h], writes=[masks.h])
        S.op("pool", lambda e: e.memset(twos.t[:], 2.0), writes=[twos.h])
        sbanks = (0, 1, 2, 3)
        for h in range(HEADS):
            for t in range(16):
                b = self.ps_next(sbanks)
                self.mm_group(self.ps[:, b, :], b, [(wukv.t[:, h, 0:128], ckvT.t[:, t * TB:(t + 1) * TB])], reads=[wukv.h, ckvT.h])
                eng = "act" if t % 2 == 0 else "dve"
                if eng == "act":
                    S.op("act", lambda e, b=b, t=t: e.activation(KnT.t[:, t * TB:(t + 1) * TB], self.ps[:, b, :], AF.Copy),
                         reads=[self.psh[b]], writes=[KnT.h])
                else:
                    S.op("dve", lambda e, b=b, t=t: e.tensor_copy(KnT.t[:, t * TB:(t + 1) * TB], self.ps[:, b, :]),
                         reads=[self.psh[b]], writes=[KnT.h])
            for t4 in range(16):
                b = self.ps_next(sbanks)
                for u4 in range(4):
                    t = t4 * 4 + u4
                    S.op("pe", lambda e, b=b, t=t, u4=u4, h=h: e.matmul(self.ps[:, b, u4 * 128:(u4 + 1) * 128], ckvT.t[:, t * 128:(t + 1) * 128],
                                                                   wukv.t[:, h, 128:256], start=True, stop=True),
                         reads=[wukv.h, ckvT.h], writes=[self.psh[b]], inc=(u4 == 3))
                if t4 % 2 == 0:
                    S.op("dve", lambda e, b=b, t4=t4: e.tensor_copy(Vt.t[:, t4 * 4:t4 * 4 + 4, :],
                                                                     self.ps[:, b, :].rearrange("p (u d) -> p u d", d=128)),
                         reads=[self.psh[b]], writes=[Vt.h])
                else:
                    S.op("act", lambda e, b=b, t4=t4: e.activation(Vt.t[:, t4 * 4:t4 * 4 + 4, :],
                                                                    self.ps[:, b, :].rearrange("p (u d) -> p u d", d=128), AF.Copy),
                         reads=[self.psh[b]], writes=[Vt.h])
            for i in range(NB):
                sl = slice(i * TB, (i + 1) * TB)
                rp = rope_ring.next()
                S.dma("sp", rp.t[:], ropd.ap().rearrange("f p t -> p f t")[:, 2:4, sl], reads=[self.dh("rope")], writes=[rp.h], owner=rp.h)
                b = self.ps_next(sbanks)
                self.mm_group(self.ps[:, b, :], b, [(wuq.t[:, k2, h, 0:128], cqT.t[:, k2, sl]) for k2 in range(2)], reads=[wuq.h, cqT.h])
                S.op("act", lambda e, b=b, sl=sl: e.activation(qnT.t[:, sl], self.ps[:, b, :], AF.Identity, scale=QSCALE),
                     reads=[self.psh[b]], writes=[qnT.h])
                bA = self.ps_next(sbanks)
                self.mm_group(self.ps[0:64, bA, :], bA, [(wuq.t[:, k2, h, 128:192], cqT.t[:, k2, sl]) for k2 in range(2)], reads=[wuq.h, cqT.h])
                bB = self.ps_next(sbanks)
                self.mm_group(self.ps[0:64, bB, :], bB, [(wuq.t[:, k2, h, 192:256], cqT.t[:, k2, sl]) for k2 in range(2)], reads=[wuq.h, cqT.h])
                t1 = t1_ring.next()
                t2 = t2_ring.next()
                S.op("dve", lambda e, t1=t1, bA=bA, rp=rp: e.tensor_tensor(t1.t[:], self.ps[0:64, bA, :], rp.t[:, 0, :], ALU.mult),
                     reads=[self.psh[bA], rp.h], writes=[t1.h])
                S.op("dve", lambda e, t2=t2, bB=bB, rp=rp: e.tensor_tensor(t2.t[:], self.ps[0:64, bB, :], rp.t[:, 1, :], ALU.mult),
                     reads=[self.psh[bB], rp.h], writes=[t2.h])
                S.op("dve", lambda e, t1=t1, t2=t2, sl=sl: e.tensor_tensor(qpeT.t[0:64, sl], t1.t[:], t2.t[:], ALU.add),
                     reads=[t1.h, t2.h], writes=[qpeT.h])
            tiles = []
            for i in range(NB):
                chunks = []
                for jp in range(i + 1):
                    for sp_ in range(2):
                        for cc in range(4):
                            chunks.append((sp_ * 32 + jp * 4 + cc, (sp_ * 4 + cc) if jp == i else None))
                for ci, (kc, mk) in enumerate(chunks):
                    tiles.append((i, ci, len(chunks), kc, mk))
            LA = 3

            def emit_qk(t):
                i, ci, nck, kc, mk = tiles[t]
                b = sbanks[t % 4]
                sl = slice(i * TB, (i + 1) * TB)
                ksl = slice(kc * 128, (kc + 1) * 128)
                self.mm_group(self.ps[:, b, :], b, [(KnT.t[:, ksl], qnT.t[:, sl]), (kpeT.t[:, ksl], qpeT.t[:, sl])],
                              reads=[KnT.h, qnT.h, kpeT.h, qpeT.h])

            def epilogue(i):
                sl = slice(i * TB, (i + 1) * TB)
                ab = 4 + (i % 2)
                sa, sb_ = ssum_r.tiles[2 * (i % 2)], ssum_r.tiles[2 * (i % 2) + 1]
                tg = tgb_ring.next()
                S.dma("sp", tg.t[:], tgbd[h, :, sl], reads=[self.dh("tgb_%d" % l, (h, i))], writes=[tg.h], owner=tg.h)
                hi = hl_ring.next()
                lo = hl_ring.next()
                S.op("dve", lambda e, sa=sa, sb_=sb_: e.tensor_tensor(sa.t[:], sa.t[:], sb_.t[:], ALU.add), reads=[sa.h, sb_.h], writes=[sa.h])
                S.op("pool", lambda e, hi=hi, sa=sa: e.tensor_copy(hi.t[:], sa.t[:]), reads=[sa.h], writes=[hi.h])
                S.op("pool", lambda e, hi=hi, lo=lo, sa=sa: e.tensor_tensor(lo.t[:], sa.t[:], hi.t[:], ALU.subtract), reads=[sa.h, hi.h], writes=[lo.h])
                self.mm_group(self.ps[:, 6, :], 6, [(twos.t[:, :], hi.t[:]), (twos.t[:, :], lo.t[:])], reads=[twos.h, hi.h, lo.h])
                rsb = rs_ring.next()
                yb = y_ring.next()
                S.op("dve", lambda e, rsb=rsb: e.reciprocal(rsb.t[:], self.ps[:, 6, :]), reads=[self.psh[6]], writes=[rsb.h])
                S.op("dve", lambda e, rsb=rsb, yb=yb, ab=ab: e.tensor_tensor(yb.t[:], self.ps[:, ab, :], rsb.t[:], ALU.mult),
                     reads=[self.psh[ab], rsb.h], writes=[yb.h])
                gb = gby_ring.next()
                S.op("dve", lambda e, gb=gb, tg=tg, yb=yb: e.scalar_tensor_tensor(gb.t[:], tg.t[:], 1.0, yb.t[:], ALU.add, ALU.mult),
                     reads=[tg.h, yb.h], writes=[gb.h])
                S.dma("sp", gbyd[h, :, sl], gb.t[:], reads=[gb.h], writes=[self.dh("gb_y_%d" % l, (h, i))], owner=gb.h, is_store=True)

            for t in range(min(LA, len(tiles))):
                emit_qk(t)
            for t in range(len(tiles)):
                i, ci, nck, kc, mk = tiles[t]
                if t + LA < len(tiles):
                    emit_qk(t + LA)
                b = sbanks[t % 4]
                ab = 4 + (i % 2)
                pT = pT_ring.next()
                S.op("act", lambda e, pT=pT, b=b: e.activation(pT.t[:], self.ps[:, b, :], AF.Exp), reads=[self.psh[b]], writes=[pT.h])
                if mk is not None:
                    S.op("pool", lambda e, pT=pT, mk=mk: e.tensor_tensor(pT.t[:], pT.t[:], masks.t[:, mk, :], ALU.mult),
                         reads=[pT.h, masks.h], writes=[pT.h])
                S.op("pe", lambda e, pT=pT, kc=kc, ci=ci, nck=nck, ab=ab: e.matmul(self.ps[:, ab, :], Vt.t[:, kc, :], pT.t[:],
                                                                                   start=(ci == 0), stop=(ci == nck - 1)),
                     reads=[pT.h, Vt.h], writes=[self.psh[ab]])
                ss = ssum_r.tiles[2 * (i % 2) + (ci % 2)]
                if ci < 2:
                    S.op("dve", lambda e, ss=ss, pT=pT: e.tensor_copy(ss.t[:], pT.t[:]), reads=[pT.h], writes=[ss.h])
                else:
                    S.op("dve", lambda e, ss=ss, pT=pT: e.tensor_tensor(ss.t[:], ss.t[:], pT.t[:], ALU.add), reads=[ss.h, pT.h], writes=[ss.h])
                if ci == nck - 1:
                    epilogue(i)

    def phase_C1(self, l):
        S = self.S
        xname, xw = ("xT", "host") if l == 0 else ("x2_0", ("C2", 0))
        xd = self.D(xname, [NCH, 128, NT], F32, xw)
        gayd = self.D("ga_y_%d" % l, [NCH, 128, NT], BF16, ("A", l))
        gaAd = self.D("ga_A_%d" % l, [NCH, 128, NT], BF16, ("A", l))
        gbyd = self.D("gb_y_%d" % l, [NCH, 128, NT], BF16, ("B", l))
        gsd = self.D("G_sum_%d" % l, [256, 128], F32, ("X2", l))
        x1d = self.D("x1_%d" % l, [NCH, 128, NT], F32, ("C1", l))
        h2d = self.D("h2_%d" % l, [NCH, 128, NT], BF16, ("C1", l))
        woutd = self.D("w_out", [2, D, D], F32, "host")
        wo = self.sb("wo", [128, 8, D], BF16)
        src = woutd[l].rearrange("(k p) n -> p k n", p=128)
        for k in range(8):
            self.load_w(wo.t[:, k, :], src[:, k, :], wo)
        mv = self.load_modv(l)
        gs = self.sb("gs", [128, 2, 2, 8, 8], F32)
        for s in range(2):
            S.dma("sp", gs.t[:, s].rearrange("p a c i -> p (a c i)"), gsd[s * 128:(s + 1) * 128, :], reads=[self.dh("G_sum_%d" % l)],
                  writes=[gs.h], owner=gs.h)
        inits = self.sb("inits", [128, 17, 8], F32)
        tmpc = self.sb("tmpc", [128, 8], F32)
        S.op("dve", lambda e: e.memset(inits.t[:], 0.0), writes=[inits.h])
        for g in range(15):
            s, j = g % 2, g // 2
            S.op("dve", lambda e, s=s, j=j, g=g: e.tensor_tensor(tmpc.t[:], gs.t[:, s, 1, :, j], inits.t[:, g, :], ALU.mult),
                 reads=[gs.h, inits.h], writes=[tmpc.h])
            S.op("dve", lambda e, s=s, j=j, g=g: e.tensor_tensor(inits.t[:, g + 1, :], tmpc.t[:], gs.t[:, s, 0, :, j], ALU.add),
                 reads=[gs.h, tmpc.h, inits.h], writes=[inits.h])
        io = self.sb("io", [128, 8, 8], F32)
        f0 = self.ppc(PP_FLAG, 1)
        f1 = self.ppc(PP_FLAG + 1, 1)
        iv = inits.t[:, 0:16, :].rearrange("p (i two) c -> p i two c", two=2)
        S.op("dve", lambda e: e.tensor_scalar(io.t[:], iv[:, :, 0, :], f0, None, ALU.mult), reads=[inits.h, self.pp.h], writes=[io.h])
        S.op("dve", lambda e: e.scalar_tensor_tensor(io.t[:], iv[:, :, 1, :], f1, io.t[:], ALU.mult, ALU.add),
             reads=[inits.h, io.h, self.pp.h], writes=[io.h])
        xt_r = self.ring("xtC", 2, [128, 8, TB], F32)
        gay_r = self.ring("gay", 2, [128, 8, TB], BF16)
        gaA_r = self.ring("gaA", 2, [128, 8, TB], BF16)
        gby_r = self.ring("gbyC", 2, [128, 8, TB], BF16)
        yT = self.sb("yT", [128, 8, TB], BF16)
        h2_r = self.ring("h2", 2, [128, 8, TB], BF16)
        rs = self.sb("rsC", [128, TB], F32)
        sq_ring = self.ring("sqC", 2, [128, TB], BF16)
        tmp_ring = self.ring("tmpC", 3, [128, TB], F32)
        print("phase C1 sbuf used", self.sb_ptr - self.sb_base, "of", self.sb_top - self.sb_base, flush=True)

        def loads(i):
            sl = slice(i * TB, (i + 1) * TB)
            xt, gay, gaA, gby = xt_r.tiles[i % 2], gay_r.tiles[i % 2], gaA_r.tiles[i % 2], gby_r.tiles[i % 2]
            S.dma("sp", gay.t[:], gayd.ap().rearrange("c p t -> p c t")[:, :, sl], reads=[self.dh("ga_y_%d" % l, (c, i)) for c in range(8)],
                  writes=[gay.h], owner=gay.h)
            S.dma("sp", gaA.t[:], gaAd.ap().rearrange("c p t -> p c t")[:, :, sl], reads=[self.dh("ga_A_%d" % l, (c, i)) for c in range(8)],
                  writes=[gaA.h], owner=gaA.h)
            S.dma("sp", gby.t[:], gbyd.ap().rearrange("c p t -> p c t")[:, :, sl], reads=[self.dh("gb_y_%d" % l, (c, i)) for c in range(8)],
                  writes=[gby.h], owner=gby.h)
            S.dma("sp", xt.t[:], xd.ap().rearrange("c p t -> p c t")[:, :, sl], reads=[self.dh(xname, (i,))], writes=[xt.h], owner=xt.h)

        loads(0)
        for i in range(NB):
            sl = slice(i * TB, (i + 1) * TB)
            if i + 1 < NB:
                loads(i + 1)
            xt, gay, gaA, gby, h2 = xt_r.tiles[i % 2], gay_r.tiles[i % 2], gaA_r.tiles[i % 2], gby_r.tiles[i % 2], h2_r.tiles[i % 2]
            for c in range(NCH):
                tmp = tmp_ring.next()
                S.op("dve", lambda e, tmp=tmp, c=c, i=i, gaA=gaA, gay=gay: e.scalar_tensor_tensor(tmp.t[:], gaA.t[:, c, :], io.t[:, i, c:c + 1], gay.t[:, c, :], ALU.mult, ALU.add),
                     reads=[gaA.h, gay.h, io.h], writes=[tmp.h])
                S.op("pool", lambda e, tmp=tmp, c=c, gby=gby: e.tensor_tensor(yT.t[:, c, :], tmp.t[:], gby.t[:, c, :], ALU.add),
                     reads=[tmp.h, gby.h], writes=[yT.h])
            for m in range(NCH):
                b = self.ps_next()
                self.mm_group(self.ps[:, b, :], b, [(wo.t[:, k, m * 128:(m + 1) * 128], yT.t[:, k, :]) for k in range(8)], reads=[wo.h, yT.h])
                S.op("dve", lambda e, b=b, m=m, xt=xt: e.scalar_tensor_tensor(xt.t[:, m, :], self.ps[:, b, :], mv.t[:, l * 48 + 16 + m:l * 48 + 17 + m],
                                                                              xt.t[:, m, :], ALU.mult, ALU.add),
                     reads=[self.psh[b], xt.h, mv.h], writes=[xt.h])
            S.dma("sp", x1d.ap().rearrange("c p t -> p c t")[:, :, sl], xt.t[:], reads=[xt.h], writes=[self.dh("x1_%d" % l, (i,))],
                  owner=xt.h, is_store=True)
            self.norm_block(xt, TB, mv.t[:, l * 48 + 32:l * 48 + 40], mv.t[:, l * 48 + 24:l * 48 + 32], h2, rs, sq_ring, tmp_ring, mvh=mv.h)
            S.dma("sp", h2d.ap().rearrange("c p t -> p c t")[:, :, sl], h2.t[:], reads=[h2.h], writes=[self.dh("h2_%d" % l, (i,))],
                  owner=h2.h, is_store=True)

    def phase_C2(self, l):
        S = self.S
        last = (l == 1)
        x1d = self.D("x1_%d" % l, [NCH, 128, NT], F32, ("C1", l))
        h2d = self.D("h2_%d" % l, [NCH, 128, NT], BF16, ("C1", l))
        oname = "out" if last else "x2_0"
        x2d = self.D(oname, [NCH, 128, NT], F32, ("C2", l))
        wfid = self.D("w_ffn_in", [2, D, 2 * DFF], F32, "host")
        wfod = self.D("w_ffn_out", [2, DFF, D], F32, "host")
        wfi = self.sb("wfi", [128, 8, 2 * DFF], BF16)
        wfo = self.sb("wfo", [128, NFF, D], BF16)
        src = wfid[l].rearrange("(k p) n -> p k n", p=128)
        for k in range(8):
            for half in range(2):
                self.load_w(wfi.t[:, k, half * DFF:(half + 1) * DFF], src[:, k, half * DFF:(half + 1) * DFF], wfi)
        src = wfod[l].rearrange("(k p) n -> p k n", p=128)
        for k in range(NFF):
            self.load_w(wfo.t[:, k, :], src[:, k, :], wfo)
        mv = self.load_modv(l)
        xt = self.sb("xtF", [128, 8, TB], F32)
        h2_r = self.ring("h2F", 2, [128, 8, TB], BF16)
        act = self.sb("actT", [128, NFF, TB], BF16)
        sg_ring = self.ring("sg", 2, [128, TB], F32)
        if last:
            rs = self.sb("rsF", [128, TB], F32)
            sq_ring = self.ring("sqF", 2, [128, TB], BF16)
            tmp_ring = self.ring("tmpF", 2, [128, TB], F32)
        print("phase C2 sbuf used", self.sb_ptr - self.sb_base, "of", self.sb_top - self.sb_base, flush=True)
        def load_h2(i):
            h2 = h2_r.tiles[i % 2]
            S.dma("sp", h2.t[:], h2d.ap().rearrange("c p t -> p c t")[:, :, i * TB:(i + 1) * TB], reads=[self.dh("h2_%d" % l, (i,))],
                  writes=[h2.h], owner=h2.h)

        load_h2(0)
        for i in range(NB):
            sl = slice(i * TB, (i + 1) * TB)
            h2 = h2_r.tiles[i % 2]
            if i + 1 < NB:
                load_h2(i + 1)
            S.dma("sp", xt.t[:], x1d.ap().rearrange("c p t -> p c t")[:, :, sl], reads=[self.dh("x1_%d" % l, (i,))], writes=[xt.h], owner=xt.h)
            for f in range(NFF):
                bA = self.ps_next()
                self.mm_group(self.ps[:, bA, :], bA, [(wfi.t[:, k, f * 128:(f + 1) * 128], h2.t[:, k, :]) for k in range(8)], reads=[wfi.h, h2.h])
                bB = self.ps_next()
                self.mm_group(self.ps[:, bB, :], bB, [(wfi.t[:, k, DFF + f * 128:DFF + (f + 1) * 128], h2.t[:, k, :]) for k in range(8)],
                              reads=[wfi.h, h2.h])
                sg = sg_ring.next()
                S.op("act", lambda e, sg=sg, bA=bA: e.activation(sg.t[:], self.ps[:, bA, :], AF.Silu), reads=[self.psh[bA]], writes=[sg.h])
                S.op("dve", lambda e, sg=sg, bB=bB, f=f: e.tensor_tensor(act.t[:, f, :], sg.t[:], self.ps[:, bB, :], ALU.mult),
                     reads=[sg.h, self.psh[bB]], writes=[act.h])
            for m in range(NCH):
                b = self.ps_next()
                self.mm_group(self.ps[:, b, :], b, [(wfo.t[:, k, m * 128:(m + 1) * 128], act.t[:, k, :]) for k in range(NFF)], reads=[wfo.h, act.h])
                S.op("dve", lambda e, b=b, m=m: e.scalar_tensor_tensor(xt.t[:, m, :], self.ps[:, b, :], mv.t[:, l * 48 + 40 + m:l * 48 + 41 + m],
                                                                       xt.t[:, m, :], ALU.mult, ALU.add),
                     reads=[self.psh[b], xt.h, mv.h], writes=[xt.h])
            if last:
                bank = self.ps_next()
                for c in range(NCH):
                    sq = sq_ring.next()
                    S.op("act", lambda e, sq=sq, c=c: e.activation(sq.t[:], xt.t[:, c, :], AF.Square), reads=[xt.h], writes=[sq.h])
                    S.op("pe", lambda e, sq=sq, c=c, bank=bank: e.matmul(self.ps[:, bank, :], self.ones.t[:, :], sq.t[:], start=(c == 0), stop=(c == NCH - 1)),
                         reads=[sq.h, self.ones.h], writes=[self.psh[bank]])
                S.op("act", lambda e, bank=bank: e.activation(rs.t[:], self.ps[:, bank, :], AF.Sqrt, bias=EPS, scale=1.0 / D),
                     reads=[self.psh[bank]], writes=[rs.h])
                S.op("dve", lambda e: e.reciprocal(rs.t[:], rs.t[:]), reads=[rs.h], writes=[rs.h])
                for c in range(NCH):
                    S.op("dve", lambda e, c=c: e.scalar_tensor_tensor(xt.t[:, c, :], xt.t[:, c, :], self.ppc(PP_FG + c, 1), rs.t[:], ALU.mult, ALU.mult),
                         reads=[xt.h, rs.h, self.pp.h], writes=[xt.h])
            S.dma("sp", x2d.ap().rearrange("c p t -> p c t")[:, :, sl], xt.t[:], reads=[xt.h], writes=[self.dh(oname, (i,))],
                  owner=xt.h, is_store=True)


class _Ring:
    def __init__(self, tiles):
        self.tiles = tiles
        self.i = 0

    def next(self):
        t = self.tiles[self.i % len(self.tiles)]
        self.i += 1
        return t


def _host_inputs(inp):
    x = np.asarray(inp["x"], np.float32)
    cores = []
    shared = {
        "w_ada": np.ascontiguousarray(inp["w_ada"], np.float32),
        "w_in": np.ascontiguousarray(inp["w_in"], np.float32),
        "lru_wa": np.ascontiguousarray(inp["lru_wa"], np.float32),
        "lru_wx": np.ascontiguousarray(inp["lru_wx"], np.float32),
        "w_uq": np.ascontiguousarray(inp["w_uq"], np.float32),
        "w_ukv": np.ascontiguousarray(inp["w_ukv"], np.float32),
        "w_out": np.ascontiguousarray(inp["w_out"], np.float32),
        "w_ffn_in": np.ascontiguousarray(inp["w_ffn_in"], np.float32),
        "w_ffn_out": np.ascontiguousarray(inp["w_ffn_out"], np.float32),
    }
    half = 32
    invf = (10000.0 ** (-np.arange(0, 64, 2, dtype=np.float32) / 64)).astype(np.float32)
    invf64 = np.concatenate([invf, invf]).astype(np.float32)
    p = np.arange(128)[:, None]
    f = np.arange(TB)[None, :]
    diag = [((128 * j + p) <= f).astype(np.float32) for j in range(4)]
    onesm = np.ones((128, TB), np.float32)
    zerom = np.zeros((128, TB), np.float32)
    for core in range(8):
        b, r = core // 2, core % 2
        tok = np.concatenate([np.arange((2 * i + r) * TB, (2 * i + r + 1) * TB) for i in range(NB)])
        xs = x[b][tok]
        xT = np.ascontiguousarray(xs.T.reshape(NCH, 128, NT))
        pos = np.ascontiguousarray(np.asarray(inp["positions"])[b][tok].astype(np.int32)[None, :])
        pp = np.zeros((128, NPP), np.float32)
        for l in range(2):
            base = l * PPL
            pp[:, base:base + 48] = np.asarray(inp["b_ada"])[l].reshape(48, 128).T
            pp[:, base + 48:base + 80] = np.asarray(inp["conv_w"])[l].reshape(4, 8, 128).transpose(2, 0, 1).reshape(128, 32)
            pp[:, base + 80:base + 88] = np.asarray(inp["conv_b"])[l].reshape(8, 128).T
            pp[:, base + 88:base + 96] = np.asarray(inp["lru_ba"])[l].T
            pp[:, base + 96:base + 104] = np.asarray(inp["lru_bx"])[l].T
            pp[:, base + 104:base + 112] = np.asarray(inp["lru_a_param"])[l].reshape(8, 128).T
            pp[:, base + 112:base + 114] = np.asarray(inp["q_norm_g"])[l].reshape(2, 128).T
            pp[:, base + 114] = np.asarray(inp["kv_norm_g"])[l]
        pp[:, PP_FG:PP_FG + 8] = np.asarray(inp["final_norm_g"]).reshape(8, 128).T
        pp[:, PP_FLAG] = 1.0 - r
        pp[:, PP_FLAG + 1] = float(r)
        pp[0:64, PP_INVF] = invf64
        pp[0:64, PP_INVF + 1] = (invf64.astype(np.float64) / (2 * np.pi)).astype(np.float32)
        pp[:, PP_C:PP_C + 8] = np.asarray(inp["c"])[b].reshape(8, 128).T
        if r == 0:
            mk = diag + [zerom] * 4
        else:
            mk = [onesm] * 4 + diag
        masks = np.ascontiguousarray(np.concatenate(mk, axis=1))
        d = dict(shared)
        d.update({"xT": xT, "pos": pos, "pp": pp, "masks": masks})
        cores.append(d)
    return cores


def _assemble(outs):
    out = np.zeros((BATCH, SEQ, D), np.float32)
    for core in range(8):
        b, r = core // 2, core % 2
        oT = np.asarray(outs[core]).reshape(D, NT)
        for i in range(NB):
            g = 2 * i + r
            out[b, g * TB:(g + 1) * TB, :] = oT[:, i * TB:(i + 1) * TB].T
    return out


def _all_phases():
    ph = [("M",), ("R",)]
    for l in range(2):
        ph += [("P", l), ("X1", l), ("A", l), ("X2", l), ("B", l), ("C1", l), ("C2", l)]
    return ph


def kernel(**inputs):
    cores = _host_inputs(inputs)
    if MODE == "fused":
        bld = Builder(_all_phases(), fused=True)
        nc = bld.build()
        in_maps = [{k: c[k] for k in bld.ext_in} for c in cores]
        res = run_bass_kernel_spmd(nc, in_maps, core_ids=list(range(8)))
        return _assemble([res.results[i]["out"] for i in range(8)])
    store = [dict(c) for c in cores]
    for ph in _all_phases():
        if ph[0] == "X1":
            l = ph[1]
            for pair in range(4):
                g = np.concatenate([store[2 * pair]["send_halo_%d" % l], store[2 * pair + 1]["send_halo_%d" % l]], axis=0)
                store[2 * pair]["G_halo_%d" % l] = g
                store[2 * pair + 1]["G_halo_%d" % l] = g
            continue
        if ph[0] == "X2":
            l = ph[1]
            for pair in range(4):
                for nm in ("kv", "sum"):
                    g = np.concatenate([store[2 * pair]["send_%s_%d" % (nm, l)], store[2 * pair + 1]["send_%s_%d" % (nm, l)]], axis=0)
                    store[2 * pair]["G_%s_%d" % (nm, l)] = g
                    store[2 * pair + 1]["G_%s_%d" % (nm, l)] = g
            continue
        bld = Builder([ph], fused=False)
        nc = bld.build()
        in_maps = [{k: s[k] for k in bld.ext_in} for s in store]
        res = run_bass_kernel_spmd(nc, in_maps, core_ids=list(range(8)))
        for i in range(8):
            for k in bld.ext_out:
                store[i][k] = res.results[i][k]
    return _assemble([store[i]["out"] for i in range(8)])
```

```python
import numpy as np
from contextlib import ExitStack
import concourse.bass as bass
import concourse.mybir as mybir
from concourse.bass_utils import run_bass_kernel_spmd

F32 = mybir.dt.float32
BF16 = mybir.dt.bfloat16
I32 = mybir.dt.int32
AF = mybir.ActivationFunctionType
ALU = mybir.AluOpType

D = 1024
NCH = 8
SEQ = 8192
BATCH = 4
TB = 512
NB = 8
NT = NB * TB
HEADS = 8
DFF = 2816
NFF = 22
DIN = 3520
DAUG = 3584
EPS = 1e-6
QSCALE = 192 ** -0.5
GROUPS = [[0, 1], [2, 3], [4, 5], [6, 7]]
PPL = 116
PP_FG = 232
PP_FLAG = 240
PP_INVF = 242
PP_C = 244
NPP = 252

MODE = "fused"


class H:
    __slots__ = ("name", "w", "r", "sem", "cnt")

    def __init__(self, name):
        self.name = name
        self.w = []
        self.r = []
        self.sem = None
        self.cnt = 0


class _Eng:
    def __init__(self, name, sem):
        self.name = name
        self.sem = sem
        self.cnt = 0
        self.known = {}
        self.prog = []


class Sched:
    ENGS = ("pe", "act", "dve", "pool", "sp")

    def __init__(self, nc, stack, n_dma_sems=80):
        self.nc = nc
        self.E = {}
        for n in self.ENGS:
            sem = stack.enter_context(nc.semaphore("s_" + n))
            self.E[n] = _Eng(n, sem)
        self.free_sems = []
        for i in range(n_dma_sems):
            self.free_sems.append([stack.enter_context(nc.semaphore("d%d" % i)), 0])
        self.live = []
        self.store_tickets = {}
        self.n_ops = 0
        self.n_waits = 0

    def _deps(self, E, reads, writes, awrites=()):
        deps = []
        for h in reads:
            deps.extend(h.w)
        for h in writes:
            deps.extend(h.w)
            for t in h.r:
                if t[0] is E.sem:
                    continue
                deps.append(t)
        for h in awrites:
            for t in h.r:
                deps.append(t)
        return deps

    def _emit_waits(self, E, deps):
        best = {}
        for (sem, val) in deps:
            k = id(sem)
            if sem is E.sem and (E.name == "pe" or val > E.cnt):
                continue
            if E.known.get(k, 0) >= val:
                continue
            if k not in best or best[k][1] < val:
                best[k] = (sem, val)
        for k, (sem, val) in best.items():
            E.known[k] = val
            E.prog.append(("wait", sem, val))
            self.n_waits += 1

    def _update(self, ticket, reads, writes, awrites=()):
        for h in reads:
            h.r = [t for t in h.r if t[0] is not ticket[0]]
            h.r.append(ticket)
        for h in writes:
            h.w = [ticket]
            h.r = []
        for h in awrites:
            h.w = [t for t in h.w if t[0] is not ticket[0]]
            h.w.append(ticket)
            h.r = []

    def op(self, ename, fn, reads=(), writes=(), inc=True):
        E = self.E[ename]
        self._emit_waits(E, self._deps(E, reads, writes))
        if inc:
            E.cnt += 1
            ticket = (E.sem, E.cnt)
        else:
            ticket = (E.sem, E.cnt + 1)
        E.prog.append(("op", fn, inc))
        self._update(ticket, reads, writes)
        self.n_ops += 1
        return ticket

    def _hsem(self, h):
        if h.sem is None:
            if not self.free_sems:
                raise RuntimeError("out of DMA semaphores")
            ent = self.free_sems.pop()
            h.sem = ent[0]
            h.cnt = ent[1]
            self.live.append(h)
        return h.sem

    def dma(self, q, out, in_, reads=(), writes=(), awrites=(), owner=None, is_store=False, **kw):
        E = self.E[q]
        self._emit_waits(E, self._deps(E, reads, writes, awrites))
        sem = self._hsem(owner)
        owner.cnt += 16
        ticket = (sem, owner.cnt)
        E.prog.append(("dma", out, in_, sem, kw))
        self._update(ticket, reads, writes, awrites)
        if is_store:
            self.store_tickets[id(sem)] = ticket
        self.n_ops += 1
        return ticket

    def collective(self, kind, ins, outs, reads, writes, owner):
        E = self.E["pool"]
        self._emit_waits(E, self._deps(E, reads, writes))
        sem = self._hsem(owner)
        owner.cnt += 1
        ticket = (sem, owner.cnt)
        E.prog.append(("cc", kind, ins, outs, sem))
        self._update(ticket, reads, writes)
        return ticket

    def barrier(self, scratch_ap):
        P = self.E["pool"]
        deps = []
        for n in self.ENGS:
            E = self.E[n]
            if E.cnt > 0:
                deps.append((E.sem, E.cnt))
        for h in self.live:
            deps.append((h.sem, h.cnt))
        self._emit_waits(P, deps)
        P.cnt += 1
        P.prog.append(("op", lambda e: e.memset(scratch_ap, 0.0), True))
        t = (P.sem, P.cnt)
        for n in self.ENGS:
            if n != "pool":
                self._emit_waits(self.E[n], [t])
        for h in self.live:
            self.free_sems.append([h.sem, h.cnt])
            h.sem = None
        self.live = []

    def final_wait(self):
        self._emit_waits(self.E["pool"], list(self.store_tickets.values()))

    def emit(self):
        nc = self.nc
        S = self

        def run(eng_obj, E):
            for item in E.prog:
                k = item[0]
                if k == "wait":
                    eng_obj.wait_ge(item[1], item[2])
                elif k == "op":
                    ins = item[1](eng_obj)
                    if item[2]:
                        ins.then_inc(E.sem, 1)
                elif k == "dma":
                    eng_obj.dma_start(out=item[1], in_=item[2], **item[4]).then_inc(item[3], 16)
                elif k == "cc":
                    eng_obj.collective_compute(item[1], ALU.bypass, replica_groups=GROUPS,
                                               ins=item[2], outs=item[3]).then_inc(item[4], 1)

        with nc.Block() as block:
            @block.tensor
            def _(e):
                run(e, S.E["pe"])

            @block.scalar
            def _(e):
                run(e, S.E["act"])

            @block.vector
            def _(e):
                run(e, S.E["dve"])

            @block.gpsimd
            def _(e):
                run(e, S.E["pool"])

            @block.sync
            def _(e):
                run(e, S.E["sp"])


class Tile:
    __slots__ = ("t", "h")

    def __init__(self, t, name):
        self.t = t
        self.h = H(name)


DT_SIZE = {F32: 4, BF16: 2, I32: 4}


class Builder:
    def __init__(self, phases, fused):
        self.phases = phases
        self.fused = fused
        self.nc = bass.Bass("TRN2", target_bir_lowering=False)
        self.ext_in = {}
        self.ext_out = {}
        self.dram = {}
        self.DH = {}
        self.uid = 0

    def D(self, name, shape, dtype, writer):
        if name in self.dram:
            return self.dram[name]
        if writer == "host" or writer not in self.phases:
            t = self.nc.dram_tensor(name, list(shape), dtype, kind="ExternalInput")
            self.ext_in[name] = (tuple(shape), dtype)
        elif self.fused and name != "out":
            t = self.nc.dram_tensor(name, list(shape), dtype)
        else:
            t = self.nc.dram_tensor(name, list(shape), dtype, kind="ExternalOutput")
            self.ext_out[name] = (tuple(shape), dtype)
        self.dram[name] = t
        return t

    def dh(self, name, key=None):
        k = (name, key)
        if k not in self.DH:
            self.DH[k] = H("D_%s_%s" % (name, key))
        return self.DH[k]

    def sb_reset(self):
        self.sb_ptr = self.sb_base

    def sb(self, name, shape, dtype):
        per = 1
        for s in shape[1:]:
            per *= s
        nbytes = (per * DT_SIZE[dtype] + 63) // 64 * 64
        if self.sb_ptr + nbytes > self.sb_top:
            raise RuntimeError("SBUF overflow allocating %s (%d + %d > %d)" % (name, self.sb_ptr, nbytes, self.sb_top))
        self.uid += 1
        t = self.nc.alloc_sbuf_tensor_at("%s_%d" % (name, self.uid), list(shape), dtype, offset=self.sb_ptr)
        self.sb_ptr += nbytes
        return Tile(t, name)

    def ring(self, name, n, shape, dtype):
        return _Ring([self.sb("%s%d" % (name, i), shape, dtype) for i in range(n)])

    def build(self):
        nc = self.nc
        with ExitStack() as st:
            self.S = S = Sched(nc, st)
            self.sb_base = (nc.sbuf_base + 63) // 64 * 64
            self.sb_top = nc.sbuf_top
            self.sb_reset()
            self.ps = st.enter_context(nc.psum_tensor("ps", [128, 8, 512], F32))
            self.psh = [H("ps%d" % i) for i in range(8)]
            self.ps_rr = 0
            self.bar = self.sb("bar", [128, 8], F32)
            self.ones = self.sb("ones", [128, 128], BF16)
            self.zeros = self.sb("zeros", [128, 512], F32)
            self.pp = self.sb("pp", [128, NPP], F32)
            self.persist_ptr = None
            S.op("pool", lambda e: e.memset(self.ones.t[:], 1.0), writes=[self.ones.h])
            S.op("pool", lambda e: e.memset(self.zeros.t[:], 0.0), writes=[self.zeros.h])
            ppd = self.D("pp", [128, NPP], F32, "host")
            S.dma("sp", self.pp.t[:], ppd[:, :], writes=[self.pp.h], owner=self.pp.h)
            self.persist_ptr = self.sb_ptr
            for ph in self.phases:
                kind = ph[0]
                if not (self.fused and kind in ("X1", "A")):
                    self.sb_ptr = self.persist_ptr
                l = ph[1] if len(ph) > 1 else None
                getattr(self, "phase_" + kind)(*([l] if l is not None else []))
                S.barrier(self.bar.t[:, 0:1])
            S.final_wait()
            print("ops", S.n_ops, "waits", S.n_waits, {n: len(S.E[n].prog) for n in S.ENGS}, flush=True)
            S.emit()
        return nc

    def ps_next(self, banks=(0, 1, 2, 3, 4, 5, 6, 7)):
        b = banks[self.ps_rr % len(banks)]
        self.ps_rr += 1
        return b

    def mm_group(self, out_ap, bank, pairs, reads, **kw):
        S = self.S
        n = len(pairs)
        for i, (l, r) in enumerate(pairs):
            S.op("pe", lambda e, l=l, r=r, i=i: e.matmul(out_ap, l, r, start=(i == 0), stop=(i == n - 1), **kw),
                 reads=reads, writes=[self.psh[bank]], inc=(i == n - 1))

    def ppc(self, col, n=1, parts=128):
        return self.pp.t[0:parts, col:col + n]

    def load_modv(self, l):
        modd = self.D("modv", [128, 96], F32, ("M",))
        mv = self.sb("modv", [128, 96], F32)
        self.S.dma("sp", mv.t[:], modd[:, :], reads=[self.dh("modv")], writes=[mv.h], owner=mv.h)
        return mv

    def norm_block(self, xt, W, sc1_ap, sh_ap, hout, rs, sq_ring, tmp_ring, mvh=None):
        S = self.S
        bank = self.ps_next()
        sqs = []
        for c in range(NCH):
            sq = sq_ring.next()
            S.op("act", lambda e, sq=sq, c=c: e.activation(sq.t[:, 0:W], xt.t[:, c, 0:W], AF.Square),
                 reads=[xt.h], writes=[sq.h])
            S.op("pe", lambda e, sq=sq, c=c: e.matmul(self.ps[:, bank, 0:W], self.ones.t[:, :], sq.t[:, 0:W],
                                                       start=(c == 0), stop=(c == NCH - 1)),
                 reads=[sq.h, self.ones.h], writes=[self.psh[bank]], inc=True)
        S.op("act", lambda e: e.activation(rs.t[:, 0:W], self.ps[:, bank, 0:W], AF.Sqrt, bias=EPS, scale=1.0 / D),
             reads=[self.psh[bank]], writes=[rs.h])
        S.op("dve", lambda e: e.reciprocal(rs.t[:, 0:W], rs.t[:, 0:W]), reads=[rs.h], writes=[rs.h])
        for c in range(NCH):
            tmp = tmp_ring.next()
            S.op("dve", lambda e, tmp=tmp, c=c: e.scalar_tensor_tensor(tmp.t[:, 0:W], xt.t[:, c, 0:W], sc1_ap[:, c:c + 1], rs.t[:, 0:W], ALU.mult, ALU.mult),
                 reads=[xt.h, rs.h, mvh], writes=[tmp.h])
            S.op("act", lambda e, tmp=tmp, c=c: e.activation(hout.t[:, c, 0:W], tmp.t[:, 0:W], AF.Identity, bias=sh_ap[:, c:c + 1], scale=1.0),
                 reads=[tmp.h, mvh], writes=[hout.h])

    def load_w(self, dst_ap, src_ap, tile):
        self.S.dma("pool", dst_ap, src_ap, writes=[tile.h], owner=tile.h)

    def phase_M(self):
        S = self.S
        w_ada = self.D("w_ada", [2, D, 6 * D], F32, "host")
        modd = self.D("modv", [128, 96], F32, ("M",))
        wr = self.ring("wada", 2, [128, 8, 1536], BF16)
        cbf = self.sb("cbf", [128, 8], BF16)
        mv = self.sb("mv", [128, 96], F32)
        S.op("dve", lambda e: e.tensor_copy(cbf.t[:], self.ppc(PP_C, 8)), reads=[self.pp.h], writes=[cbf.h])
        bank = 0
        for l in range(2):
            for g in range(4):
                wt = wr.next()
                src = w_ada[l].rearrange("(k p) n -> p k n", p=128)
                for k in range(8):
                    self.S.dma("pool", wt.t[:, k, :], src[:, k, g * 1536:(g + 1) * 1536], writes=[wt.h], owner=wt.h)
                for j in range(12):
                    J = g * 12 + j
                    self.mm_group(self.ps[:, bank, J:J + 1], bank,
                                  [(wt.t[:, k, j * 128:(j + 1) * 128], cbf.t[:, k:k + 1]) for k in range(8)],
                                  reads=[wt.h, cbf.h])
            S.op("dve", lambda e, l=l: e.tensor_tensor(mv.t[:, l * 48:(l + 1) * 48], self.ps[:, bank, 0:48],
                                                       self.ppc(l * PPL, 48), ALU.add),
                 reads=[self.psh[bank], self.pp.h], writes=[mv.h])
            for off in (8, 32):
                S.op("dve", lambda e, l=l, off=off: e.tensor_scalar(mv.t[:, l * 48 + off:l * 48 + off + 8],
                                                                    mv.t[:, l * 48 + off:l * 48 + off + 8],
                                                                    1.0, None, ALU.add),
                     reads=[mv.h], writes=[mv.h])
        S.dma("sp", modd[:, :], mv.t[:], reads=[mv.h], writes=[self.dh("modv")], owner=mv.h, is_store=True)

    def phase_R(self):
        S = self.S
        posd = self.D("pos", [1, NT], I32, "host")
        ropd = self.D("rope", [4, 64, NT], F32, ("R",))
        posi = self.ring("posi", 2, [64, TB], I32)
        posf = self.ring("posf", 2, [64, TB], F32)
        ang = self.ring("ang", 2, [64, TB], F32)
        tq = self.ring("tq", 2, [64, TB], F32)
        ti = self.ring("ti", 2, [64, TB], I32)
        out = self.ring("rout", 4, [64, 2, TB], F32)
        invf = self.ppc(PP_INVF, 1, 64)
        invf2 = self.ppc(PP_INVF + 1, 1, 64)
        for i in range(NB):
            pi = posi.next()
            pf = posf.next()
            S.dma("sp", pi.t[:], posd[0:1, i * TB:(i + 1) * TB].partition_broadcast(64), writes=[pi.h], owner=pi.h)
            S.op("dve", lambda e, pi=pi, pf=pf: e.tensor_copy(pf.t[:], pi.t[:]), reads=[pi.h], writes=[pf.h])
            for which, (aoff, toff) in enumerate(((0.0, 0.0), (np.pi / 2, 0.25))):
                a = ang.next()
                t = tq.next()
                tii = ti.next()
                o = out.next()
                S.op("dve", lambda e, a=a, pf=pf, aoff=aoff: e.tensor_scalar(a.t[:], pf.t[:], invf, aoff, ALU.mult, ALU.add),
                     reads=[pf.h, self.pp.h], writes=[a.h])
                S.op("dve", lambda e, t=t, pf=pf, toff=toff: e.tensor_scalar(t.t[:], pf.t[:], invf2, toff, ALU.mult, ALU.add),
                     reads=[pf.h, self.pp.h], writes=[t.h])
                S.op("dve", lambda e, t=t, tii=tii: e.tensor_copy(tii.t[:], t.t[:]), reads=[t.h], writes=[tii.h])
                S.op("dve", lambda e, t=t, tii=tii: e.tensor_copy(t.t[:], tii.t[:]), reads=[tii.h], writes=[t.h])
                S.op("dve", lambda e, t=t, a=a: e.scalar_tensor_tensor(a.t[:], t.t[:], -2 * np.pi, a.t[:], ALU.mult, ALU.add),
                     reads=[t.h, a.h], writes=[a.h])
                S.op("act", lambda e, o=o, a=a: e.activation(o.t[:, 0, :], a.t[:], AF.Sin), reads=[a.h], writes=[o.h])
                S.op("dve", lambda e, o=o: e.tensor_scalar(o.t[:, 1, :], o.t[:, 0, :], QSCALE, None, ALU.mult),
                     reads=[o.h], writes=[o.h])
                tidx = 1 if which == 0 else 0
                S.dma("sp", ropd[tidx, :, i * TB:(i + 1) * TB], o.t[:, 0, :], reads=[o.h],
                      awrites=[self.dh("rope")], owner=o.h, is_store=True)
                S.dma("sp", ropd[tidx + 2, :, i * TB:(i + 1) * TB], o.t[:, 1, :], reads=[o.h],
                      awrites=[self.dh("rope")], owner=o.h, is_store=True)

    def get_w_in(self, l, lru_only):
        key = ("w_in", l)
        if getattr(self, "_w_in_key", None) == key:
            return self._w_in
        w_in = self.D("w_in", [2, D, DIN], F32, "host")
        ncols = D if lru_only else DAUG
        wt = self.sb("w_in_sb", [128, 8, ncols], BF16)
        src = w_in[l].rearrange("(k p) n -> p k n", p=128)
        for k in range(8):
            if lru_only:
                self.load_w(wt.t[:, k, 0:D], src[:, k, 0:D], wt)
            else:
                self.load_w(wt.t[:, k, 0:1472], src[:, k, 0:1472], wt)
                self.load_w(wt.t[:, k, 1472:1504], src[:, k, 1440:1472], wt)
                self.load_w(wt.t[:, k, 1504:1536], src[:, k, 1408:1440], wt)
                self.load_w(wt.t[:, k, 1536:DAUG], src[:, k, 1472:DIN], wt)
        if not lru_only:
            self.S.op("dve", lambda e: e.tensor_scalar(wt.t[:, :, 1472:1504], wt.t[:, :, 1472:1504], -1.0, None, ALU.mult),
                      reads=[wt.h], writes=[wt.h])
        self._w_in_key = key
        self._w_in = wt
        return wt

    def phase_P(self, l):
        S = self.S
        xname, xw = ("xT", "host") if l == 0 else ("x2_0", ("C2", 0))
        xd = self.D(xname, [NCH, 128, NT], F32, xw)
        shd = self.D("send_halo_%d" % l, [128, 192], F32, ("P", l))
        wt = self.get_w_in(l, lru_only=not self.fused)
        mv = self.load_modv(l)
        xh = self.sb("xh", [128, 8, 24], F32)
        hh = self.sb("hh", [128, 8, 24], BF16)
        rs = self.sb("rsP", [128, 24], F32)
        sq_ring = self.ring("sqP", 2, [128, 24], BF16)
        tmp_ring = self.ring("tmpP", 2, [128, 24], F32)
        xlh = self.sb("xlh", [128, 8, 24], F32)
        for c in range(NCH):
            src = xd[c].rearrange("p (i t) -> p i t", t=TB)[:, :, TB - 3:TB]
            rd = [self.dh(xname, (i,)) for i in range(NB)]
            S.dma("sp", xh.t[:, c, :].rearrange("p (i t) -> p i t", t=3), src, reads=rd, writes=[xh.h], owner=xh.h)
        self.norm_block(xh, 24, mv.t[:, l * 48 + 8:l * 48 + 16], mv.t[:, l * 48 + 0:l * 48 + 8], hh, rs, sq_ring, tmp_ring, mvh=mv.h)
        for m in range(NCH):
            bank = self.ps_next()
            self.mm_group(self.ps[:, bank, 0:24], bank,
                          [(wt.t[:, k, m * 128:(m + 1) * 128], hh.t[:, k, :]) for k in range(8)],
                          reads=[wt.h, hh.h])
            S.op("act", lambda e, m=m, bank=bank: e.activation(xlh.t[:, m, :], self.ps[:, bank, 0:24], AF.Copy),
                 reads=[self.psh[bank]], writes=[xlh.h])
        S.dma("sp", shd[:, :], xlh.t[:].rearrange("p m t -> p (m t)"), reads=[xlh.h], writes=[self.dh("send_halo_%d" % l)],
              owner=xlh.h, is_store=True)

    def phase_X1(self, l):
        shd = self.D("send_halo_%d" % l, [128, 192], F32, ("P", l))
        gd = self.D("G_halo_%d" % l, [256, 192], F32, ("X1", l))
        o = H("cc1")
        self.S.collective("AllGather", [shd.ap().opt()], [gd.ap().opt()],
                          reads=[self.dh("send_halo_%d" % l)], writes=[self.dh("G_halo_%d" % l)], owner=o)

    def phase_X2(self, l):
        for nm, sshape, gshape, dt in (("kv", [192, NT], [384, NT], BF16), ("sum", [128, 128], [256, 128], F32)):
            sd = self.D("send_%s_%d" % (nm, l), sshape, dt, ("A", l))
            gd = self.D("G_%s_%d" % (nm, l), gshape, dt, ("X2", l))
            o = H("cc2" + nm)
            self.S.collective("AllGather", [sd.ap().opt()], [gd.ap().opt()],
                              reads=[self.dh("send_%s_%d" % (nm, l))], writes=[self.dh("G_%s_%d" % (nm, l))], owner=o)

    def phase_A(self, l):
        S = self.S
        xname, xw = ("xT", "host") if l == 0 else ("x2_0", ("C2", 0))
        xd = self.D(xname, [NCH, 128, NT], F32, xw)
        posd = self.D("pos", [1, NT], I32, "host")
        ropd = self.D("rope", [4, 64, NT], F32, ("R",))
        ghd = self.D("G_halo_%d" % l, [256, 192], F32, ("X1", l))
        gayd = self.D("ga_y_%d" % l, [NCH, 128, NT], BF16, ("A", l))
        gaAd = self.D("ga_A_%d" % l, [NCH, 128, NT], BF16, ("A", l))
        tgbd = self.D("tgb_%d" % l, [NCH, 128, NT], BF16, ("A", l))
        cqd = self.D("cq_%d" % l, [2, 128, NT], BF16, ("A", l))
        skvd = self.D("send_kv_%d" % l, [192, NT], BF16, ("A", l))
        ssumd = self.D("send_sum_%d" % l, [128, 128], F32, ("A", l))
        lwa = self.D("lru_wa", [2, 8, 128, 128], F32, "host")
        lwx = self.D("lru_wx", [2, 8, 128, 128], F32, "host")
        wt = self.get_w_in(l, lru_only=False)
        base = l * PPL
        wa = self.sb("wa", [128, 8, 128], BF16)
        wx = self.sb("wx", [128, 8, 128], BF16)
        self.load_w(wa.t[:], lwa[l].rearrange("n i j -> i n j"), wa)
        self.load_w(wx.t[:], lwx[l].rearrange("n i j -> i n j"), wx)
        mv = self.load_modv(l)
        hb = self.sb("hb", [128, 16], F32)
        c05 = self.sb("c05", [128, 8], F32)
        S.op("dve", lambda e: e.tensor_scalar(hb.t[:], self.ppc(base + 88, 16), 0.5, None, ALU.mult),
             reads=[self.pp.h], writes=[hb.h])
        S.op("act", lambda e: e.activation(c05.t[:], self.ppc(base + 104, 8), AF.Exp, scale=-1.0),
             reads=[self.pp.h], writes=[c05.h])
        S.op("act", lambda e: e.activation(c05.t[:], c05.t[:], AF.Ln, bias=1.0, scale=1.0), reads=[c05.h], writes=[c05.h])
        S.op("dve", lambda e: e.tensor_scalar(c05.t[:], c05.t[:], -4.0, None, ALU.mult), reads=[c05.h], writes=[c05.h])
        gh = self.sb("gh", [128, 2, 8, 8, 3], F32)
        halo = self.sb("halo", [128, 8, 8, 3], F32)
        for s in range(2):
            S.dma("sp", gh.t[:, s].rearrange("p m i t -> p (m i t)"), ghd[s * 128:(s + 1) * 128, :],
                  reads=[self.dh("G_halo_%d" % l)], writes=[gh.h], owner=gh.h)
        f0 = self.ppc(PP_FLAG, 1)
        f1 = self.ppc(PP_FLAG + 1, 1)
        S.op("dve", lambda e: e.memset(halo.t[:], 0.0), writes=[halo.h])
        S.op("dve", lambda e: e.tensor_scalar(halo.t[:, :, 1:8, :], gh.t[:, 1, :, 0:7, :], f0, None, ALU.mult),
             reads=[gh.h, self.pp.h], writes=[halo.h])
        S.op("dve", lambda e: e.scalar_tensor_tensor(halo.t[:], gh.t[:, 0], f1, halo.t[:], ALU.mult, ALU.add),
             reads=[gh.h, halo.h, self.pp.h], writes=[halo.h])
        summ = self.sb("summ", [128, 2, 8, 8], F32)
        xt = self.sb("xtA", [128, 8, TB], F32)
        hT_r = self.ring("hT", 2, [128, 8, TB], BF16)
        rs = self.sb("rsA", [128, TB], F32)
        rq = self.sb("rqA", [128, TB], F32)
        rkv = self.sb("rkv", [128, TB], F32)
        sq_ring = self.ring("sqA", 2, [128, TB], BF16)
        tmp_ring = self.ring("tmpA", 3, [128, TB], F32)
        posi = self.sb("posiA", [128, TB], I32)
        mb2_r = self.ring("mb2", 2, [128, TB], F32)
        rope_r = self.ring("ropeA", 2, [64, 2, TB], F32)
        ui = [self.sb("ui%d" % c, [128, TB], F32) for c in range(NCH)]
        aT = [self.sb("aT%d" % c, [128, TB], F32) for c in range(NCH)]
        tga = [self.sb("tga%d" % c, [128, TB], BF16) for c in range(NCH)]
        xl_ring = self.ring("xl", 2, [128, TB + 3], F32)
        u_ring = self.ring("u", 2, [128, TB], F32)
        ubf_ring = self.ring("ubf", 2, [128, TB], BF16)
        tr_ring = self.ring("tr", 1, [128, TB], F32)
        tiv_ring = self.ring("tiv", 1, [128, TB], F32)
        m4_ring = self.ring("m4", 2, [128, TB], F32)
        h0_ring = self.ring("h0", 1, [128, TB], F32)
        A_ring = self.ring("AA", 1, [128, TB], F32)
        ob_ring = self.ring("ob", 4, [128, TB], BF16)
        qd = self.sb("qd", [128, 2, TB], F32)
        kvd = self.sb("kvd", [128, TB], F32)
        print("phase A sbuf used", self.sb_ptr - self.sb_base, "of", self.sb_top - self.sb_base, flush=True)
        cw = lambda k, c: self.ppc(base + 48 + k * 8 + c, 1)
        cb = lambda c: self.ppc(base + 80 + c, 1)
        banks = (0, 1, 2, 3, 4, 5, 6, 7)
        sc1_ap = mv.t[:, l * 48 + 8:l * 48 + 16]
        sh_ap = mv.t[:, l * 48 + 0:l * 48 + 8]
        st = {}

        def load(i):
            sl = slice(i * TB, (i + 1) * TB)
            mb2 = mb2_r.tiles[i % 2]
            rope = rope_r.tiles[i % 2]
            S.dma("sp", xt.t[:], xd.ap().rearrange("c p t -> p c t")[:, :, sl], reads=[self.dh(xname, (i,))],
                  writes=[xt.h], owner=xt.h)
            S.dma("sp", posi.t[:], posd[0:1, sl].partition_broadcast(128), writes=[posi.h], owner=posi.h)
            S.dma("sp", rope.t[:], ropd.ap().rearrange("f p t -> p f t")[:, 0:2, sl], reads=[self.dh("rope")],
                  writes=[rope.h], owner=rope.h)
            pf = tmp_ring.next()
            S.op("dve", lambda e, pf=pf: e.tensor_copy(pf.t[:], posi.t[:]), reads=[posi.h], writes=[pf.h])
            S.op("dve", lambda e, pf=pf, mb2=mb2: e.tensor_scalar(mb2.t[:], pf.t[:], 0.0, 2e6, ALU.is_equal, ALU.mult),
                 reads=[pf.h], writes=[mb2.h])

        def norm_act(i):
            bank = self.ps_next(banks)
            st[("nb", i)] = bank
            for c in range(NCH):
                sq = sq_ring.next()
                S.op("act", lambda e, sq=sq, c=c: e.activation(sq.t[:], xt.t[:, c, :], AF.Square), reads=[xt.h], writes=[sq.h])
                S.op("pe", lambda e, sq=sq, c=c, bank=bank: e.matmul(self.ps[:, bank, :], self.ones.t[:, :], sq.t[:],
                                                                     start=(c == 0), stop=(c == NCH - 1)),
                     reads=[sq.h, self.ones.h], writes=[self.psh[bank]])

        def norm_sqrt(i):
            bank = st[("nb", i)]
            S.op("act", lambda e, bank=bank: e.activation(rs.t[:], self.ps[:, bank, :], AF.Sqrt, bias=EPS, scale=1.0 / D),
                 reads=[self.psh[bank]], writes=[rs.h])

        def norm_fin(i):
            hT = hT_r.tiles[i % 2]
            S.op("dve", lambda e: e.reciprocal(rs.t[:], rs.t[:]), reads=[rs.h], writes=[rs.h])
            for c in range(NCH):
                tmp = tmp_ring.next()
                S.op("dve", lambda e, tmp=tmp, c=c: e.scalar_tensor_tensor(tmp.t[:], xt.t[:, c, :], sc1_ap[:, c:c + 1], rs.t[:], ALU.mult, ALU.mult),
                     reads=[xt.h, rs.h, mv.h], writes=[tmp.h])
                S.op("act", lambda e, tmp=tmp, c=c, hT=hT: e.activation(hT.t[:, c, :], tmp.t[:], AF.Identity, bias=sh_ap[:, c:c + 1], scale=1.0),
                     reads=[tmp.h, mv.h], writes=[hT.h])

        def proj(i, m0, msz, bank):
            hT = hT_r.tiles[i % 2]
            self.mm_group(self.ps[0:msz, bank, :], bank,
                          [(wt.t[:, k, m0:m0 + msz], hT.t[:, k, :]) for k in range(8)], reads=[wt.h, hT.h])

        def proj_x(i, c):
            b0 = self.ps_next(banks)
            st[("bx", i, c)] = b0
            proj(i, c * 128, 128, b0)

        def s1_front(i, c):
            xl = xl_ring.next()
            u = u_ring.next()
            st[("u", i, c)] = u
            b0 = st[("bx", i, c)]
            S.op("act", lambda e, xl=xl, b0=b0: e.activation(xl.t[:, 3:TB + 3], self.ps[:, b0, :], AF.Copy),
                 reads=[self.psh[b0]], writes=[xl.h])
            S.op("pool", lambda e, xl=xl, c=c, i=i: e.tensor_copy(xl.t[:, 0:3], halo.t[:, c, i, :]),
                 reads=[halo.h, xl.h], writes=[xl.h])
            S.op("pool", lambda e, xl=xl, u=u, c=c: e.tensor_scalar(u.t[:], xl.t[:, 0:TB], cw(0, c), cb(c), ALU.mult, ALU.add),
                 reads=[xl.h, self.pp.h], writes=[u.h])
            for k in range(1, 4):
                S.op("dve", lambda e, xl=xl, u=u, c=c, k=k: e.scalar_tensor_tensor(u.t[:], xl.t[:, k:k + TB], cw(k, c), u.t[:],
                                                                                   ALU.mult, ALU.add),
                     reads=[xl.h, u.h, self.pp.h], writes=[u.h])

        def s1_ubf(i, c):
            u = st[("u", i, c)]
            ubf = ubf_ring.next()
            st[("ubf", i, c)] = ubf
            S.op("act", lambda e, u=u, ubf=ubf: e.activation(ubf.t[:], u.t[:], AF.Copy), reads=[u.h], writes=[ubf.h])

        def s1_back(i, c):
            sl = slice(i * TB, (i + 1) * TB)
            mb2 = mb2_r.tiles[i % 2]
            u = st[("u", i, c)]
            ubf = st[("ubf", i, c)]
            tr = tr_ring.next()
            tiv = tiv_ring.next()
            b1 = self.ps_next(banks)
            b2 = self.ps_next(banks)
            self.mm_group(self.ps[:, b1, :], b1, [(wa.t[:, c, :], ubf.t[:])], reads=[wa.h, ubf.h])
            self.mm_group(self.ps[:, b2, :], b2, [(wx.t[:, c, :], ubf.t[:])], reads=[wx.h, ubf.h])
            b3 = self.ps_next(banks)
            proj(i, 1536 + c * 128, 128, b3)
            b4 = self.ps_next(banks)
            proj(i, 2560 + c * 128, 128, b4)
            S.op("act", lambda e, tr=tr, b1=b1, c=c: e.activation(tr.t[:], self.ps[:, b1, :], AF.Tanh, bias=hb.t[:, c:c + 1], scale=0.5),
                 reads=[self.psh[b1], hb.h], writes=[tr.h])
            S.op("act", lambda e, tiv=tiv, b2=b2, c=c: e.activation(tiv.t[:], self.ps[:, b2, :], AF.Tanh, bias=hb.t[:, 8 + c:9 + c], scale=0.5),
                 reads=[self.psh[b2], hb.h], writes=[tiv.h])
            S.op("pool", lambda e, tr=tr, mb2=mb2: e.tensor_tensor(tr.t[:], tr.t[:], mb2.t[:], ALU.add),
                 reads=[tr.h, mb2.h], writes=[tr.h])
            S.op("act", lambda e, tr=tr, c=c: e.activation(aT[c].t[:], tr.t[:], AF.Exp, bias=c05.t[:, c:c + 1], scale=c05.t[:, c:c + 1]),
                 reads=[tr.h, c05.h], writes=[aT[c].h])
            S.op("dve", lambda e, tiv=tiv, u=u, c=c: e.scalar_tensor_tensor(ui[c].t[:], tiv.t[:], 1.0, u.t[:], ALU.add, ALU.mult),
                 reads=[tiv.h, u.h], writes=[ui[c].h])
            S.op("act", lambda e, b3=b3, c=c: e.activation(tga[c].t[:], self.ps[:, b3, :], AF.Tanh, scale=0.5),
                 reads=[self.psh[b3]], writes=[tga[c].h])
            ob = ob_ring.next()
            S.op("act", lambda e, b4=b4, ob=ob: e.activation(ob.t[:], self.ps[:, b4, :], AF.Tanh, scale=0.5),
                 reads=[self.psh[b4]], writes=[ob.h])
            S.dma("sp", tgbd[c, :, sl], ob.t[:], reads=[ob.h], writes=[self.dh("tgb_%d" % l, (c, i))], owner=ob.h, is_store=True)

        def latent(i):
            sl = slice(i * TB, (i + 1) * TB)
            rope = rope_r.tiles[i % 2]
            for k2 in range(2):
                b = self.ps_next(banks)
                proj(i, 1024 + k2 * 128, 128, b)
                S.op("act", lambda e, b=b, k2=k2: e.activation(qd.t[:, k2, :], self.ps[:, b, :], AF.Copy),
                     reads=[self.psh[b]], writes=[qd.h])
            b = self.ps_next(banks)
            proj(i, 1280, 128, b)
            S.op("act", lambda e, b=b: e.activation(kvd.t[:], self.ps[:, b, :], AF.Copy), reads=[self.psh[b]], writes=[kvd.h])
            bA = self.ps_next(banks)
            bB = self.ps_next(banks)
            proj(i, 1408, 64, bA)
            proj(i, 1472, 64, bB)
            kp1 = tmp_ring.next()
            kp2 = tmp_ring.next()
            ob = ob_ring.next()
            S.op("dve", lambda e, kp1=kp1, bA=bA, rope=rope: e.tensor_tensor(kp1.t[0:64, :], self.ps[0:64, bA, :], rope.t[:, 0, :], ALU.mult),
                 reads=[self.psh[bA], rope.h], writes=[kp1.h])
            S.op("dve", lambda e, kp2=kp2, bB=bB, rope=rope: e.tensor_tensor(kp2.t[0:64, :], self.ps[0:64, bB, :], rope.t[:, 1, :], ALU.mult),
                 reads=[self.psh[bB], rope.h], writes=[kp2.h])
            S.op("dve", lambda e, kp1=kp1, kp2=kp2, ob=ob: e.tensor_tensor(ob.t[0:64, :], kp1.t[0:64, :], kp2.t[0:64, :], ALU.add),
                 reads=[kp1.h, kp2.h], writes=[ob.h])
            S.dma("sp", skvd[128:192, sl], ob.t[0:64, :], reads=[ob.h], awrites=[self.dh("send_kv_%d" % l)], owner=ob.h, is_store=True)
            bq = self.ps_next(banks)
            for k2 in range(2):
                sq = sq_ring.next()
                S.op("act", lambda e, sq=sq, k2=k2: e.activation(sq.t[:], qd.t[:, k2, :], AF.Square), reads=[qd.h], writes=[sq.h])
                S.op("pe", lambda e, sq=sq, k2=k2, bq=bq: e.matmul(self.ps[:, bq, :], self.ones.t[:, :], sq.t[:], start=(k2 == 0), stop=(k2 == 1)),
                     reads=[sq.h, self.ones.h], writes=[self.psh[bq]])
            bk = self.ps_next(banks)
            sq = sq_ring.next()
            S.op("act", lambda e, sq=sq: e.activation(sq.t[:], kvd.t[:], AF.Square), reads=[kvd.h], writes=[sq.h])
            S.op("pe", lambda e, sq=sq, bk=bk: e.matmul(self.ps[:, bk, :], self.ones.t[:, :], sq.t[:], start=True, stop=True),
                 reads=[sq.h, self.ones.h], writes=[self.psh[bk]])
            st[("bq", i)] = bq
            st[("bk", i)] = bk

        def batch(i, with_next_norm):
            sl = slice(i * TB, (i + 1) * TB)
            bq, bk = st[("bq", i)], st[("bk", i)]
            if with_next_norm:
                norm_sqrt(i + 1)
            S.op("act", lambda e, bq=bq: e.activation(rq.t[:], self.ps[:, bq, :], AF.Sqrt, bias=EPS, scale=1.0 / 256), reads=[self.psh[bq]], writes=[rq.h])
            S.op("act", lambda e, bk=bk: e.activation(rkv.t[:], self.ps[:, bk, :], AF.Sqrt, bias=EPS, scale=1.0 / 128), reads=[self.psh[bk]], writes=[rkv.h])
            if with_next_norm:
                norm_fin(i + 1)
            for c in range(NCH):
                m4 = m4_ring.next()
                S.op("pool", lambda e, m4=m4, c=c: e.tensor_tensor(m4.t[:], aT[c].t[:], aT[c].t[:], ALU.mult), reads=[aT[c].h], writes=[m4.h])
                S.op("act", lambda e, m4=m4: e.activation(m4.t[:], m4.t[:], AF.Sqrt, bias=1.0 / 16, scale=-1.0 / 16), reads=[m4.h], writes=[m4.h])
                S.op("dve", lambda e, m4=m4, c=c: e.tensor_tensor(ui[c].t[:], ui[c].t[:], m4.t[:], ALU.mult), reads=[ui[c].h, m4.h], writes=[ui[c].h])
            S.op("dve", lambda e: e.reciprocal(rq.t[:], rq.t[:]), reads=[rq.h], writes=[rq.h])
            S.op("dve", lambda e: e.reciprocal(rkv.t[:], rkv.t[:]), reads=[rkv.h], writes=[rkv.h])
            for k2 in range(2):
                tmp = tmp_ring.next()
                ob = ob_ring.next()
                S.op("dve", lambda e, tmp=tmp, k2=k2: e.tensor_tensor(tmp.t[:], qd.t[:, k2, :], rq.t[:], ALU.mult), reads=[qd.h, rq.h], writes=[tmp.h])
                S.op("dve", lambda e, tmp=tmp, ob=ob, k2=k2: e.tensor_scalar(ob.t[:], tmp.t[:], self.ppc(base + 112 + k2, 1), None, ALU.mult),
                     reads=[tmp.h, self.pp.h], writes=[ob.h])
                S.dma("sp", cqd[k2, :, sl], ob.t[:], reads=[ob.h], writes=[self.dh("cq_%d" % l, (k2, i))], owner=ob.h, is_store=True)
            tmp = tmp_ring.next()
            ob = ob_ring.next()
            S.op("dve", lambda e, tmp=tmp: e.tensor_tensor(tmp.t[:], kvd.t[:], rkv.t[:], ALU.mult), reads=[kvd.h, rkv.h], writes=[tmp.h])
            S.op("dve", lambda e, tmp=tmp, ob=ob: e.tensor_scalar(ob.t[:], tmp.t[:], self.ppc(base + 114, 1), None, ALU.mult),
                 reads=[tmp.h, self.pp.h], writes=[ob.h])
            S.dma("sp", skvd[0:128, sl], ob.t[:], reads=[ob.h], awrites=[self.dh("send_kv_%d" % l)], owner=ob.h, is_store=True)

        def stage2(i, c):
            sl = slice(i * TB, (i + 1) * TB)
            h0 = h0_ring.next()
            AA = A_ring.next()
            S.op("dve", lambda e, h0=h0, c=c: e.tensor_tensor_scan(h0.t[:], aT[c].t[:], ui[c].t[:], 0.0, ALU.mult, ALU.add),
                 reads=[aT[c].h, ui[c].h], writes=[h0.h])
            S.op("dve", lambda e, AA=AA, c=c: e.tensor_tensor_scan(AA.t[:], aT[c].t[:], self.zeros.t[:], 1.0, ALU.mult, ALU.add),
                 reads=[aT[c].h, self.zeros.h], writes=[AA.h])
            S.op("pool", lambda e, h0=h0, c=c, i=i: e.tensor_copy(summ.t[:, 0, c, i:i + 1], h0.t[:, TB - 1:TB]), reads=[h0.h], writes=[summ.h])
            S.op("pool", lambda e, AA=AA, c=c, i=i: e.tensor_copy(summ.t[:, 1, c, i:i + 1], AA.t[:, TB - 1:TB]), reads=[AA.h, summ.h], writes=[summ.h])
            ob1 = ob_ring.next()
            ob2 = ob_ring.next()
            S.op("dve", lambda e, ob1=ob1, h0=h0, c=c: e.scalar_tensor_tensor(ob1.t[:], tga[c].t[:], 1.0, h0.t[:], ALU.add, ALU.mult),
                 reads=[tga[c].h, h0.h], writes=[ob1.h])
            S.op("dve", lambda e, ob2=ob2, AA=AA, c=c: e.scalar_tensor_tensor(ob2.t[:], tga[c].t[:], 1.0, AA.t[:], ALU.add, ALU.mult),
                 reads=[tga[c].h, AA.h], writes=[ob2.h])
            S.dma("sp", gayd[c, :, sl], ob1.t[:], reads=[ob1.h], writes=[self.dh("ga_y_%d" % l, (c, i))], owner=ob1.h, is_store=True)
            S.dma("sp", gaAd[c, :, sl], ob2.t[:], reads=[ob2.h], writes=[self.dh("ga_A_%d" % l, (c, i))], owner=ob2.h, is_store=True)

        load(0)
        norm_act(0)
        norm_sqrt(0)
        norm_fin(0)
        for i in range(NB):
            if i + 1 < NB:
                load(i + 1)
            proj_x(i, 0)
            proj_x(i, 1)
            s1_front(i, 0)
            s1_ubf(i, 0)
            for c in range(NCH):
                if c + 1 < NCH:
                    s1_front(i, c + 1)
                if c + 2 < NCH:
                    proj_x(i, c + 2)
                if i > 0:
                    stage2(i - 1, c)
                s1_back(i, c)
                if c + 1 < NCH:
                    s1_ubf(i, c + 1)
            latent(i)
            if i + 1 < NB:
                norm_act(i + 1)
            batch(i, i + 1 < NB)
        for c in range(NCH):
            stage2(NB - 1, c)
        S.dma("sp", ssumd[:, :], summ.t[:].rearrange("p a c i -> p (a c i)"), reads=[summ.h], writes=[self.dh("send_sum_%d" % l)],
              owner=summ.h, is_store=True)
        self._w_in_key = None

    def phase_B(self, l):
        S = self.S
        gkvd = self.D("G_kv_%d" % l, [384, NT], BF16, ("X2", l))
        cqd = self.D("cq_%d" % l, [2, 128, NT], BF16, ("A", l))
        tgbd = self.D("tgb_%d" % l, [NCH, 128, NT], BF16, ("A", l))
        gbyd = self.D("gb_y_%d" % l, [NCH, 128, NT], BF16, ("B", l))
        ropd = self.D("rope", [4, 64, NT], F32, ("R",))
        maskd = self.D("masks", [128, 8 * TB], F32, "host")
        wuqd = self.D("w_uq", [2, 256, HEADS, 192], F32, "host")
        wukvd = self.D("w_ukv", [2, 128, HEADS, 256], F32, "host")
        ckvT = self.sb("ckvT", [128, 2 * NT], BF16)
        kpeT = self.sb("kpeT", [128, 2 * NT], BF16)
        cqT = self.sb("cqT", [128, 2, NT], BF16)
        wuq = self.sb("wuq", [128, 2, HEADS, 256], BF16)
        wukv = self.sb("wukv", [128, HEADS, 256], BF16)
        maskf = self.sb("maskf", [128, TB], F32)
        masks = self.sb("masks", [128, 8, TB], BF16)
        KnT = self.sb("KnT", [128, 2 * NT], BF16)
        Vt = self.sb("Vt", [128, 64, 128], BF16)
        twos = self.sb("twos", [128, 128], BF16)
        qnT = self.sb("qnT", [128, NT], BF16)
        qpeT = self.sb("qpeT", [128, NT], BF16)
        rope_ring = self.ring("ropeB", 2, [64, 2, TB], F32)
        pT_ring = self.ring("pT", 6, [128, TB], BF16)
        t1_ring = self.ring("t1", 2, [64, TB], F32)
        t2_ring = self.ring("t2", 2, [64, TB], F32)
        ssum_r = self.ring("ssum", 4, [128, TB], F32)
        hl_ring = self.ring("hilo", 4, [128, TB], BF16)
        rs_ring = self.ring("rsB", 2, [128, TB], F32)
        y_ring = self.ring("yB", 2, [128, TB], F32)
        tgb_ring = self.ring("tgbB", 2, [128, TB], BF16)
        gby_ring = self.ring("gby", 2, [128, TB], BF16)
        print("phase B sbuf used", self.sb_ptr - self.sb_base, "of", self.sb_top - self.sb_base, flush=True)
        gkv = self.dh("G_kv_%d" % l)
        S.op("pool", lambda e: e.memset(kpeT.t[64:128, :], 0.0), writes=[kpeT.h])
        S.op("pool", lambda e: e.memset(qpeT.t[64:128, :], 0.0), writes=[qpeT.h])
        for s in range(2):
            S.dma("sp", ckvT.t[:, s * NT:(s + 1) * NT], gkvd[s * 192:s * 192 + 128, :], reads=[gkv], writes=[ckvT.h], owner=ckvT.h)
            S.dma("sp", kpeT.t[0:64, s * NT:(s + 1) * NT], gkvd[s * 192 + 128:s * 192 + 192, :], reads=[gkv], writes=[kpeT.h], owner=kpeT.h)
        for k2 in range(2):
            S.dma("sp", cqT.t[:, k2, :], cqd[k2, :, :], reads=[self.dh("cq_%d" % l, (k2, i)) for i in range(NB)], writes=[cqT.h], owner=cqT.h)
        src = wuqd[l].rearrange("(k p) h d -> p k h d", p=128)
        for k2 in range(2):
            self.load_w(wuq.t[:, k2, :, 0:192], src[:, k2, :, :], wuq)
            self.load_w(wuq.t[:, k2, :, 192:224], src[:, k2, :, 160:192], wuq)
            self.load_w(wuq.t[:, k2, :, 224:256], src[:, k2, :, 128:160], wuq)
        S.op("dve", lambda e: e.tensor_scalar(wuq.t[:, :, :, 192:224], wuq.t[:, :, :, 192:224], -1.0, None, ALU.mult),
             reads=[wuq.h], writes=[wuq.h])
        self.load_w(wukv.t[:], wukvd[l], wukv)
        for j in range(8):
            S.dma("sp", maskf.t[:], maskd[:, j * TB:(j + 1) * TB], writes=[maskf.h], owner=maskf.h)
            S.op("dve", lambda e, j=j: e.tensor_scalar(masks.t[:, j, :], maskf.t[:], -1.0, 1.0, ALU.mult, ALU.add), reads=[maskf.h], writes=[masks.h])
        S.op("pool", lambda e: e.memset(twos.t[:], 2.0), writes=[twos.h])
        negI = self.sb("negI", [128, 128], BF16)
        S.op("pool", lambda e: e.memset(negI.t[:], 0.0), writes=[negI.h])
        S.op("pool", lambda e: e.affine_select(out=negI.t[:], in_=negI.t[:], pattern=[[-1, 128]], compare_op=ALU.not_equal,
                                               fill=-30000.0, base=0, channel_multiplier=1), reads=[negI.h], writes=[negI.h])
        sbanks = (0, 1, 2, 3)
        for h in range(HEADS):
            for t in range(16):
                b = self.ps_next(sbanks)
                self.mm_group(self.ps[:, b, :], b, [(wukv.t[:, h, 0:128], ckvT.t[:, t * TB:(t + 1) * TB])], reads=[wukv.h, ckvT.h])
                eng = "act" if t % 2 == 0 else "dve"
                if eng == "act":
                    S.op("act", lambda e, b=b, t=t: e.activation(KnT.t[:, t * TB:(t + 1) * TB], self.ps[:, b, :], AF.Copy),
                         reads=[self.psh[b]], writes=[KnT.h])
                else:
                    S.op("dve", lambda e, b=b, t=t: e.tensor_copy(KnT.t[:, t * TB:(t + 1) * TB], self.ps[:, b, :]),
                         reads=[self.psh[b]], writes=[KnT.h])
            for t4 in range(16):
                b = self.ps_next(sbanks)
                for u4 in range(4):
                    t = t4 * 4 + u4
                    S.op("pe", lambda e, b=b, t=t, u4=u4, h=h: e.matmul(self.ps[:, b, u4 * 128:(u4 + 1) * 128], ckvT.t[:, t * 128:(t + 1) * 128],
                                                                   wukv.t[:, h, 128:256], start=True, stop=True),
                         reads=[wukv.h, ckvT.h], writes=[self.psh[b]], inc=(u4 == 3))
                if t4 % 2 == 0:
                    S.op("dve", lambda e, b=b, t4=t4: e.tensor_copy(Vt.t[:, t4 * 4:t4 * 4 + 4, :],
                                                                     self.ps[:, b, :].rearrange("p (u d) -> p u d", d=128)),
                         reads=[self.psh[b]], writes=[Vt.h])
                else:
                    S.op("act", lambda e, b=b, t4=t4: e.activation(Vt.t[:, t4 * 4:t4 * 4 + 4, :],
                                                                    self.ps[:, b, :].rearrange("p (u d) -> p u d", d=128), AF.Copy),
                         reads=[self.psh[b]], writes=[Vt.h])
            for i in range(NB):
                sl = slice(i * TB, (i + 1) * TB)
                rp = rope_ring.next()
                S.dma("sp", rp.t[:], ropd.ap().rearrange("f p t -> p f t")[:, 2:4, sl], reads=[self.dh("rope")], writes=[rp.h], owner=rp.h)
                b = self.ps_next(sbanks)
                self.mm_group(self.ps[:, b, :], b, [(wuq.t[:, k2, h, 0:128], cqT.t[:, k2, sl]) for k2 in range(2)], reads=[wuq.h, cqT.h])
                S.op("act", lambda e, b=b, sl=sl: e.activation(qnT.t[:, sl], self.ps[:, b, :], AF.Identity, scale=QSCALE),
                     reads=[self.psh[b]], writes=[qnT.h])
                bA = self.ps_next(sbanks)
                self.mm_group(self.ps[0:64, bA, :], bA, [(wuq.t[:, k2, h, 128:192], cqT.t[:, k2, sl]) for k2 in range(2)], reads=[wuq.h, cqT.h])
                bB = self.ps_next(sbanks)
                self.mm_group(self.ps[0:64, bB, :], bB, [(wuq.t[:, k2, h, 192:256], cqT.t[:, k2, sl]) for k2 in range(2)], reads=[wuq.h, cqT.h])
                t1 = t1_ring.next()
                t2 = t2_ring.next()
                S.op("dve", lambda e, t1=t1, bA=bA, rp=rp: e.tensor_tensor(t1.t[:], self.ps[0:64, bA, :], rp.t[:, 0, :], ALU.mult),
                     reads=[self.psh[bA], rp.h], writes=[t1.h])
                S.op("dve", lambda e, t2=t2, bB=bB, rp=rp: e.tensor_tensor(t2.t[:], self.ps[0:64, bB, :], rp.t[:, 1, :], ALU.mult),
                     reads=[self.psh[bB], rp.h], writes=[t2.h])
                S.op("dve", lambda e, t1=t1, t2=t2, sl=sl: e.tensor_tensor(qpeT.t[0:64, sl], t1.t[:], t2.t[:], ALU.add),
                     reads=[t1.h, t2.h], writes=[qpeT.h])
            tiles = []
            for i in range(NB):
                chunks = []
                for jp in range(i + 1):
                    for sp_ in range(2):
                        for cc in range(4):
                            chunks.append((sp_ * 32 + jp * 4 + cc, (sp_ * 4 + cc) if jp == i else None))
                for ci, (kc, mk) in enumerate(chunks):
                    tiles.append((i, ci, len(chunks), kc, mk))
            LA = 3

            def emit_qk(t):
                i, ci, nck, kc, mk = tiles[t]
                b = sbanks[t % 4]
                sl = slice(i * TB, (i + 1) * TB)
                ksl = slice(kc * 128, (kc + 1) * 128)
                pairs = [(KnT.t[:, ksl], qnT.t[:, sl]), (kpeT.t[:, ksl], qpeT.t[:, sl])]
                if mk is not None:
                    pairs.append((negI.t[:, :], masks.t[:, mk, :]))
                self.mm_group(self.ps[:, b, :], b, pairs, reads=[KnT.h, qnT.h, kpeT.h, qpeT.h, negI.h, masks.h])

            def epilogue(i):
                sl = slice(i * TB, (i + 1) * TB)
                ab = 4 + (i % 2)
                sa, sb_ = ssum_r.tiles[2 * (i % 2)], ssum_r.tiles[2 * (i % 2) + 1]
                tg = tgb_ring.next()
                S.dma("sp", tg.t[:], tgbd[h, :, sl], reads=[self.dh("tgb_%d" % l, (h, i))], writes=[tg.h], owner=tg.h)
                hi = hl_ring.next()
                lo = hl_ring.next()
                S.op("dve", lambda e, sa=sa, sb_=sb_: e.tensor_tensor(sa.t[:], sa.t[:], sb_.t[:], ALU.add), reads=[sa.h, sb_.h], writes=[sa.h])
                S.op("pool", lambda e, hi=hi, sa=sa: e.tensor_copy(hi.t[:], sa.t[:]), reads=[sa.h], writes=[hi.h])
                S.op("pool", lambda e, hi=hi, lo=lo, sa=sa: e.tensor_tensor(lo.t[:], sa.t[:], hi.t[:], ALU.subtract), reads=[sa.h, hi.h], writes=[lo.h])
                self.mm_group(self.ps[:, 6, :], 6, [(twos.t[:, :], hi.t[:]), (twos.t[:, :], lo.t[:])], reads=[twos.h, hi.h, lo.h])
                rsb = rs_ring.next()
                yb = y_ring.next()
                S.op("dve", lambda e, rsb=rsb: e.reciprocal(rsb.t[:], self.ps[:, 6, :]), reads=[self.psh[6]], writes=[rsb.h])
                S.op("dve", lambda e, rsb=rsb, yb=yb, ab=ab: e.tensor_tensor(yb.t[:], self.ps[:, ab, :], rsb.t[:], ALU.mult),
                     reads=[self.psh[ab], rsb.h], writes=[yb.h])
                gb = gby_ring.next()
                S.op("dve", lambda e, gb=gb, tg=tg, yb=yb: e.scalar_tensor_tensor(gb.t[:], tg.t[:], 1.0, yb.t[:], ALU.add, ALU.mult),
                     reads=[tg.h, yb.h], writes=[gb.h])
                S.dma("sp", gbyd[h, :, sl], gb.t[:], reads=[gb.h], writes=[self.dh("gb_y_%d" % l, (h, i))], owner=gb.h, is_store=True)

            for t in range(min(LA, len(tiles))):
                emit_qk(t)
            for t in range(len(tiles)):
                i, ci, nck, kc, mk = tiles[t]
                if t + LA < len(tiles):
                    emit_qk(t + LA)
                b = sbanks[t % 4]
                ab = 4 + (i % 2)
                pT = pT_ring.next()
                S.op("act", lambda e, pT=pT, b=b: e.activation(pT.t[:], self.ps[:, b, :], AF.Exp), reads=[self.psh[b]], writes=[pT.h])
                S.op("pe", lambda e, pT=pT, kc=kc, ci=ci, nck=nck, ab=ab: e.matmul(self.ps[:, ab, :], Vt.t[:, kc, :], pT.t[:],
                                                                                   start=(ci == 0), stop=(ci == nck - 1)),
                     reads=[pT.h, Vt.h], writes=[self.psh[ab]])
                ss = ssum_r.tiles[2 * (i % 2) + (ci % 2)]
                if ci < 2:
                    S.op("dve", lambda e, ss=ss, pT=pT: e.tensor_copy(ss.t[:], pT.t[:]), reads=[pT.h], writes=[ss.h])
                else:
                    S.op("dve", lambda e, ss=ss, pT=pT: e.tensor_tensor(ss.t[:], ss.t[:], pT.t[:], ALU.add), reads=[ss.h, pT.h], writes=[ss.h])
                if ci == nck - 1:
                    epilogue(i)

    def phase_C1(self, l):
        S = self.S
        xname, xw = ("xT", "host") if l == 0 else ("x2_0", ("C2", 0))
        xd = self.D(xname, [NCH, 128, NT], F32, xw)
        gayd = self.D("ga_y_%d" % l, [NCH, 128, NT], BF16, ("A", l))
        gaAd = self.D("ga_A_%d" % l, [NCH, 128, NT], BF16, ("A", l))
        gbyd = self.D("gb_y_%d" % l, [NCH, 128, NT], BF16, ("B", l))
        gsd = self.D("G_sum_%d" % l, [256, 128], F32, ("X2", l))
        x1d = self.D("x1_%d" % l, [NCH, 128, NT], F32, ("C1", l))
        h2d = self.D("h2_%d" % l, [NCH, 128, NT], BF16, ("C1", l))
        woutd = self.D("w_out", [2, D, D], F32, "host")
        wo = self.sb("wo", [128, 8, D], BF16)
        src = woutd[l].rearrange("(k p) n -> p k n", p=128)
        for k in range(8):
            self.load_w(wo.t[:, k, :], src[:, k, :], wo)
        mv = self.load_modv(l)
        gs = self.sb("gs", [128, 2, 2, 8, 8], F32)
        for s in range(2):
            S.dma("sp", gs.t[:, s].rearrange("p a c i -> p (a c i)"), gsd[s * 128:(s + 1) * 128, :], reads=[self.dh("G_sum_%d" % l)],
                  writes=[gs.h], owner=gs.h)
        inits = self.sb("inits", [128, 17, 8], F32)
        tmpc = self.sb("tmpc", [128, 8], F32)
        S.op("dve", lambda e: e.memset(inits.t[:], 0.0), writes=[inits.h])
        for g in range(15):
            s, j = g % 2, g // 2
            S.op("dve", lambda e, s=s, j=j, g=g: e.tensor_tensor(tmpc.t[:], gs.t[:, s, 1, :, j], inits.t[:, g, :], ALU.mult),
                 reads=[gs.h, inits.h], writes=[tmpc.h])
            S.op("dve", lambda e, s=s, j=j, g=g: e.tensor_tensor(inits.t[:, g + 1, :], tmpc.t[:], gs.t[:, s, 0, :, j], ALU.add),
                 reads=[gs.h, tmpc.h, inits.h], writes=[inits.h])
        io = self.sb("io", [128, 8, 8], F32)
        f0 = self.ppc(PP_FLAG, 1)
        f1 = self.ppc(PP_FLAG + 1, 1)
        iv = inits.t[:, 0:16, :].rearrange("p (i two) c -> p i two c", two=2)
        S.op("dve", lambda e: e.tensor_scalar(io.t[:], iv[:, :, 0, :], f0, None, ALU.mult), reads=[inits.h, self.pp.h], writes=[io.h])
        S.op("dve", lambda e: e.scalar_tensor_tensor(io.t[:], iv[:, :, 1, :], f1, io.t[:], ALU.mult, ALU.add),
             reads=[inits.h, io.h, self.pp.h], writes=[io.h])
        xt_r = self.ring("xtC", 2, [128, 8, TB], F32)
        gay_r = self.ring("gay", 2, [128, 8, TB], BF16)
        gaA_r = self.ring("gaA", 2, [128, 8, TB], BF16)
        gby_r = self.ring("gbyC", 2, [128, 8, TB], BF16)
        yT = self.sb("yT", [128, 8, TB], BF16)
        h2_r = self.ring("h2", 2, [128, 8, TB], BF16)
        rs = self.sb("rsC", [128, TB], F32)
        sq_ring = self.ring("sqC", 2, [128, TB], BF16)
        tmp_ring = self.ring("tmpC", 3, [128, TB], F32)
        print("phase C1 sbuf used", self.sb_ptr - self.sb_base, "of", self.sb_top - self.sb_base, flush=True)

        def loads(i):
            sl = slice(i * TB, (i + 1) * TB)
            xt, gay, gaA, gby = xt_r.tiles[i % 2], gay_r.tiles[i % 2], gaA_r.tiles[i % 2], gby_r.tiles[i % 2]
            S.dma("sp", gay.t[:], gayd.ap().rearrange("c p t -> p c t")[:, :, sl], reads=[self.dh("ga_y_%d" % l, (c, i)) for c in range(8)],
                  writes=[gay.h], owner=gay.h)
            S.dma("sp", gaA.t[:], gaAd.ap().rearrange("c p t -> p c t")[:, :, sl], reads=[self.dh("ga_A_%d" % l, (c, i)) for c in range(8)],
                  writes=[gaA.h], owner=gaA.h)
            S.dma("sp", gby.t[:], gbyd.ap().rearrange("c p t -> p c t")[:, :, sl], reads=[self.dh("gb_y_%d" % l, (c, i)) for c in range(8)],
                  writes=[gby.h], owner=gby.h)
            S.dma("sp", xt.t[:], xd.ap().rearrange("c p t -> p c t")[:, :, sl], reads=[self.dh(xname, (i,))], writes=[xt.h], owner=xt.h)

        loads(0)
        for i in range(NB):
            sl = slice(i * TB, (i + 1) * TB)
            if i + 1 < NB:
                loads(i + 1)
            xt, gay, gaA, gby, h2 = xt_r.tiles[i % 2], gay_r.tiles[i % 2], gaA_r.tiles[i % 2], gby_r.tiles[i % 2], h2_r.tiles[i % 2]
            for c in range(NCH):
                tmp = tmp_ring.next()
                S.op("dve", lambda e, tmp=tmp, c=c, i=i, gaA=gaA, gay=gay: e.scalar_tensor_tensor(tmp.t[:], gaA.t[:, c, :], io.t[:, i, c:c + 1], gay.t[:, c, :], ALU.mult, ALU.add),
                     reads=[gaA.h, gay.h, io.h], writes=[tmp.h])
                S.op("pool", lambda e, tmp=tmp, c=c, gby=gby: e.tensor_tensor(yT.t[:, c, :], tmp.t[:], gby.t[:, c, :], ALU.add),
                     reads=[tmp.h, gby.h], writes=[yT.h])
            for m in range(NCH):
                b = self.ps_next()
                self.mm_group(self.ps[:, b, :], b, [(wo.t[:, k, m * 128:(m + 1) * 128], yT.t[:, k, :]) for k in range(8)], reads=[wo.h, yT.h])
                S.op("dve", lambda e, b=b, m=m, xt=xt: e.scalar_tensor_tensor(xt.t[:, m, :], self.ps[:, b, :], mv.t[:, l * 48 + 16 + m:l * 48 + 17 + m],
                                                                              xt.t[:, m, :], ALU.mult, ALU.add),
                     reads=[self.psh[b], xt.h, mv.h], writes=[xt.h])
            S.dma("sp", x1d.ap().rearrange("c p t -> p c t")[:, :, sl], xt.t[:], reads=[xt.h], writes=[self.dh("x1_%d" % l, (i,))],
                  owner=xt.h, is_store=True)
            self.norm_block(xt, TB, mv.t[:, l * 48 + 32:l * 48 + 40], mv.t[:, l * 48 + 24:l * 48 + 32], h2, rs, sq_ring, tmp_ring, mvh=mv.h)
            S.dma("sp", h2d.ap().rearrange("c p t -> p c t")[:, :, sl], h2.t[:], reads=[h2.h], writes=[self.dh("h2_%d" % l, (i,))],
                  owner=h2.h, is_store=True)

    def phase_C2(self, l):
        S = self.S
        last = (l == 1)
        x1d = self.D("x1_%d" % l, [NCH, 128, NT], F32, ("C1", l))
        h2d = self.D("h2_%d" % l, [NCH, 128, NT], BF16, ("C1", l))
        oname = "out" if last else "x2_0"
        x2d = self.D(oname, [NCH, 128, NT], F32, ("C2", l))
        wfid = self.D("w_ffn_in", [2, D, 2 * DFF], F32, "host")
        wfod = self.D("w_ffn_out", [2, DFF, D], F32, "host")
        wfi = self.sb("wfi", [128, 8, 2 * DFF], BF16)
        wfo = self.sb("wfo", [128, NFF, D], BF16)
        src = wfid[l].rearrange("(k p) n -> p k n", p=128)
        for k in range(8):
            for half in range(2):
                self.load_w(wfi.t[:, k, half * DFF:(half + 1) * DFF], src[:, k, half * DFF:(half + 1) * DFF], wfi)
        src = wfod[l].rearrange("(k p) n -> p k n", p=128)
        for k in range(NFF):
            self.load_w(wfo.t[:, k, :], src[:, k, :], wfo)
        mv = self.load_modv(l)
        xt = self.sb("xtF", [128, 8, TB], F32)
        h2_r = self.ring("h2F", 2, [128, 8, TB], BF16)
        act = self.sb("actT", [128, NFF, TB], BF16)
        sg_ring = self.ring("sg", 2, [128, TB], F32)
        if last:
            rs = self.sb("rsF", [128, TB], F32)
            sq_ring = self.ring("sqF", 2, [128, TB], BF16)
            tmp_ring = self.ring("tmpF", 2, [128, TB], F32)
        print("phase C2 sbuf used", self.sb_ptr - self.sb_base, "of", self.sb_top - self.sb_base, flush=True)
        def load_h2(i):
            h2 = h2_r.tiles[i % 2]
            S.dma("sp", h2.t[:], h2d.ap().rearrange("c p t -> p c t")[:, :, i * TB:(i + 1) * TB], reads=[self.dh("h2_%d" % l, (i,))],
                  writes=[h2.h], owner=h2.h)

        load_h2(0)
        for i in range(NB):
            sl = slice(i * TB, (i + 1) * TB)
            h2 = h2_r.tiles[i % 2]
            if i + 1 < NB:
                load_h2(i + 1)
            S.dma("sp", xt.t[:], x1d.ap().rearrange("c p t -> p c t")[:, :, sl], reads=[self.dh("x1_%d" % l, (i,))], writes=[xt.h], owner=xt.h)
            for f in range(NFF):
                bA = self.ps_next()
                self.mm_group(self.ps[:, bA, :], bA, [(wfi.t[:, k, f * 128:(f + 1) * 128], h2.t[:, k, :]) for k in range(8)], reads=[wfi.h, h2.h])
                bB = self.ps_next()
                self.mm_group(self.ps[:, bB, :], bB, [(wfi.t[:, k, DFF + f * 128:DFF + (f + 1) * 128], h2.t[:, k, :]) for k in range(8)],
                              reads=[wfi.h, h2.h])
                sg = sg_ring.next()
                S.op("act", lambda e, sg=sg, bA=bA: e.activation(sg.t[:], self.ps[:, bA, :], AF.Silu), reads=[self.psh[bA]], writes=[sg.h])
                S.op("dve", lambda e, sg=sg, bB=bB, f=f: e.tensor_tensor(act.t[:, f, :], sg.t[:], self.ps[:, bB, :], ALU.mult),
                     reads=[sg.h, self.psh[bB]], writes=[act.h])
            for m in range(NCH):
                b = self.ps_next()
                self.mm_group(self.ps[:, b, :], b, [(wfo.t[:, k, m * 128:(m + 1) * 128], act.t[:, k, :]) for k in range(NFF)], reads=[wfo.h, act.h])
                S.op("dve", lambda e, b=b, m=m: e.scalar_tensor_tensor(xt.t[:, m, :], self.ps[:, b, :], mv.t[:, l * 48 + 40 + m:l * 48 + 41 + m],
                                                                       xt.t[:, m, :], ALU.mult, ALU.add),
                     reads=[self.psh[b], xt.h, mv.h], writes=[xt.h])
            if last:
                bank = self.ps_next()
                for c in range(NCH):
                    sq = sq_ring.next()
                    S.op("act", lambda e, sq=sq, c=c: e.activation(sq.t[:], xt.t[:, c, :], AF.Square), reads=[xt.h], writes=[sq.h])
                    S.op("pe", lambda e, sq=sq, c=c, bank=bank: e.matmul(self.ps[:, bank, :], self.ones.t[:, :], sq.t[:], start=(c == 0), stop=(c == NCH - 1)),
                         reads=[sq.h, self.ones.h], writes=[self.psh[bank]])
                S.op("act", lambda e, bank=bank: e.activation(rs.t[:], self.ps[:, bank, :], AF.Sqrt, bias=EPS, scale=1.0 / D),
                     reads=[self.psh[bank]], writes=[rs.h])
                S.op("dve", lambda e: e.reciprocal(rs.t[:], rs.t[:]), reads=[rs.h], writes=[rs.h])
                for c in range(NCH):
                    S.op("dve", lambda e, c=c: e.scalar_tensor_tensor(xt.t[:, c, :], xt.t[:, c, :], self.ppc(PP_FG + c, 1), rs.t[:], ALU.mult, ALU.mult),
                         reads=[xt.h, rs.h, self.pp.h], writes=[xt.h])
            S.dma("sp", x2d.ap().rearrange("c p t -> p c t")[:, :, sl], xt.t[:], reads=[xt.h], writes=[self.dh(oname, (i,))],
                  owner=xt.h, is_store=True)


class _Ring:
    def __init__(self, tiles):
        self.tiles = tiles
        self.i = 0

    def next(self):
        t = self.tiles[self.i % len(self.tiles)]
        self.i += 1
        return t


def _host_inputs(inp):
    x = np.asarray(inp["x"], np.float32)
    cores = []
    shared = {
        "w_ada": np.ascontiguousarray(inp["w_ada"], np.float32),
        "w_in": np.ascontiguousarray(inp["w_in"], np.float32),
        "lru_wa": np.ascontiguousarray(inp["lru_wa"], np.float32),
        "lru_wx": np.ascontiguousarray(inp["lru_wx"], np.float32),
        "w_uq": np.ascontiguousarray(inp["w_uq"], np.float32),
        "w_ukv": np.ascontiguousarray(inp["w_ukv"], np.float32),
        "w_out": np.ascontiguousarray(inp["w_out"], np.float32),
        "w_ffn_in": np.ascontiguousarray(inp["w_ffn_in"], np.float32),
        "w_ffn_out": np.ascontiguousarray(inp["w_ffn_out"], np.float32),
    }
    half = 32
    invf = (10000.0 ** (-np.arange(0, 64, 2, dtype=np.float32) / 64)).astype(np.float32)
    invf64 = np.concatenate([invf, invf]).astype(np.float32)
    p = np.arange(128)[:, None]
    f = np.arange(TB)[None, :]
    diag = [((128 * j + p) <= f).astype(np.float32) for j in range(4)]
    onesm = np.ones((128, TB), np.float32)
    zerom = np.zeros((128, TB), np.float32)
    for core in range(8):
        b, r = core // 2, core % 2
        tok = np.concatenate([np.arange((2 * i + r) * TB, (2 * i + r + 1) * TB) for i in range(NB)])
        xs = x[b][tok]
        xT = np.ascontiguousarray(xs.T.reshape(NCH, 128, NT))
        pos = np.ascontiguousarray(np.asarray(inp["positions"])[b][tok].astype(np.int32)[None, :])
        pp = np.zeros((128, NPP), np.float32)
        for l in range(2):
            base = l * PPL
            pp[:, base:base + 48] = np.asarray(inp["b_ada"])[l].reshape(48, 128).T
            pp[:, base + 48:base + 80] = np.asarray(inp["conv_w"])[l].reshape(4, 8, 128).transpose(2, 0, 1).reshape(128, 32)
            pp[:, base + 80:base + 88] = np.asarray(inp["conv_b"])[l].reshape(8, 128).T
            pp[:, base + 88:base + 96] = np.asarray(inp["lru_ba"])[l].T
            pp[:, base + 96:base + 104] = np.asarray(inp["lru_bx"])[l].T
            pp[:, base + 104:base + 112] = np.asarray(inp["lru_a_param"])[l].reshape(8, 128).T
            pp[:, base + 112:base + 114] = np.asarray(inp["q_norm_g"])[l].reshape(2, 128).T
            pp[:, base + 114] = np.asarray(inp["kv_norm_g"])[l]
        pp[:, PP_FG:PP_FG + 8] = np.asarray(inp["final_norm_g"]).reshape(8, 128).T
        pp[:, PP_FLAG] = 1.0 - r
        pp[:, PP_FLAG + 1] = float(r)
        pp[0:64, PP_INVF] = invf64
        pp[0:64, PP_INVF + 1] = (invf64.astype(np.float64) / (2 * np.pi)).astype(np.float32)
        pp[:, PP_C:PP_C + 8] = np.asarray(inp["c"])[b].reshape(8, 128).T
        if r == 0:
            mk = diag + [zerom] * 4
        else:
            mk = [onesm] * 4 + diag
        masks = np.ascontiguousarray(np.concatenate(mk, axis=1))
        d = dict(shared)
        d.update({"xT": xT, "pos": pos, "pp": pp, "masks": masks})
        cores.append(d)
    return cores


def _assemble(outs):
    out = np.zeros((BATCH, SEQ, D), np.float32)
    for core in range(8):
        b, r = core // 2, core % 2
        oT = np.asarray(outs[core]).reshape(D, NT)
        for i in range(NB):
            g = 2 * i + r
            out[b, g * TB:(g + 1) * TB, :] = oT[:, i * TB:(i + 1) * TB].T
    return out


def _all_phases():
    ph = [("M",), ("R",)]
    for l in range(2):
        ph += [("P", l), ("X1", l), ("A", l), ("X2", l), ("B", l), ("C1", l), ("C2", l)]
    return ph


def kernel(**inputs):
    cores = _host_inputs(inputs)
    if MODE == "fused":
        bld = Builder(_all_phases(), fused=True)
        nc = bld.build()
        in_maps = [{k: c[k] for k in bld.ext_in} for c in cores]
        res = run_bass_kernel_spmd(nc, in_maps, core_ids=list(range(8)))
        return _assemble([res.results[i]["out"] for i in range(8)])
    store = [dict(c) for c in cores]
    for ph in _all_phases():
        if ph[0] == "X1":
            l = ph[1]
            for pair in range(4):
                g = np.concatenate([store[2 * pair]["send_halo_%d" % l], store[2 * pair + 1]["send_halo_%d" % l]], axis=0)
                store[2 * pair]["G_halo_%d" % l] = g
                store[2 * pair + 1]["G_halo_%d" % l] = g
            continue
        if ph[0] == "X2":
            l = ph[1]
            for pair in range(4):
                for nm in ("kv", "sum"):
                    g = np.concatenate([store[2 * pair]["send_%s_%d" % (nm, l)], store[2 * pair + 1]["send_%s_%d" % (nm, l)]], axis=0)
                    store[2 * pair]["G_%s_%d" % (nm, l)] = g
                    store[2 * pair + 1]["G_%s_%d" % (nm, l)] = g
            continue
        bld = Builder([ph], fused=False)
        nc = bld.build()
        in_maps = [{k: s[k] for k in bld.ext_in} for s in store]
        res = run_bass_kernel_spmd(nc, in_maps, core_ids=list(range(8)))
        for i in range(8):
            for k in bld.ext_out:
                store[i][k] = res.results[i][k]
    return _assemble([store[i]["out"] for i in range(8)])
```
